# Optimizing a Trainium2 kernel written in Bass

```python
import math
import jax, jax.numpy as jnp
from jax import lax
import numpy as np

D_MODEL = 1024
BATCH = 16
SEQ = 4096
DEPTH = 1

D_MIX = D_MODEL
D_SSM = D_MIX // 2
SSM_GROUP = 16
N_SSM_GROUPS = D_SSM // SSM_GROUP
SSM_STATE = 64
D_ATTN = D_MIX - D_SSM
N_HEADS = 8
QK_NOPE = 64
QK_ROPE = 32
V_HEAD = D_ATTN // N_HEADS
Q_LORA = 384
KV_LORA = 256
IN_COLS = D_SSM + Q_LORA + KV_LORA + QK_ROPE
D_FF = 4 * D_MODEL
ROPE_BASE = 10000.0
Q_BLOCK = 128
EPS = 1e-6
DT_MIN = 1e-3
DT_MAX = 1e-1
N_MOD = 6

kernel_name = "hymba_s5_mla_adaln_block"


def rmsnorm(x, g):
    xf = x.astype(jnp.float32)
    y = xf * lax.rsqrt(jnp.mean(xf * xf, axis=-1, keepdims=True) + EPS)
    return (y * g.astype(jnp.float32)).astype(x.dtype)


def rope_tables(positions):
    inv_freq = ROPE_BASE ** (-jnp.arange(0, QK_ROPE, 2, dtype=jnp.float32) / QK_ROPE)
    ang = positions.astype(jnp.float32)[..., None] * inv_freq
    return jnp.cos(ang), jnp.sin(ang)


def apply_rope(x, cos, sin):
    xf = x.astype(jnp.float32)
    x1, x2 = jnp.split(xf, 2, axis=-1)
    out = jnp.concatenate([x1 * cos - x2 * sin, x1 * sin + x2 * cos], axis=-1)
    return out.astype(x.dtype)


def s5_mixer(u, lam_re, lam_im, b_re, b_im, c_re, c_im, d, log_dt, w_glu):
    f32 = jnp.float32
    bsz, seq, _ = u.shape
    uf = u.astype(f32).reshape(bsz, seq, N_SSM_GROUPS, SSM_GROUP)
    lam = lax.complex(lam_re.astype(f32), lam_im.astype(f32))
    dt = jnp.exp(log_dt.astype(f32))[:, None]
    lam_bar = jnp.exp(lam * dt)
    b = lax.complex(b_re.astype(f32), b_im.astype(f32))
    b_bar = ((lam_bar - 1.0) / lam)[..., None] * b
    bu = jnp.einsum("bsgh,gph->bsgp", uf, b_bar)
    a = jnp.broadcast_to(lam_bar, (1, seq) + lam_bar.shape)

    def combine(left, right):
        a_l, b_l = left
        a_r, b_r = right
        return a_r * a_l, a_r * b_l + b_r

    _, states = lax.associative_scan(combine, (a, bu), axis=1)
    y = (jnp.einsum("bsgp,ghp->bsgh", jnp.real(states), c_re.astype(f32))
         - jnp.einsum("bsgp,ghp->bsgh", jnp.imag(states), c_im.astype(f32))
         + d.astype(f32) * uf)
    y = jax.nn.gelu(y).reshape(bsz, seq, D_SSM).astype(u.dtype)
    z = y @ w_glu
    return z[..., :D_SSM] * jax.nn.sigmoid(z[..., D_SSM:])


def causal_block_attention(q_nope, q_rope, k_nope, k_rope, v):
    bsz, seq = q_nope.shape[:2]
    n_blocks = seq // Q_BLOCK
    scale = (QK_NOPE + QK_ROPE) ** -0.5
    key_pos = jnp.arange(seq)

    def one_block(i):
        start = i * Q_BLOCK
        qn = lax.dynamic_slice_in_dim(q_nope, start, Q_BLOCK, axis=1)
        qr = lax.dynamic_slice_in_dim(q_rope, start, Q_BLOCK, axis=1)
        s = (jnp.einsum("bqhd,bkhd->bhqk", qn, k_nope)
             + jnp.einsum("bqhr,bkr->bhqk", qr, k_rope)).astype(jnp.float32) * scale
        q_pos = start + jnp.arange(Q_BLOCK)
        mask = key_pos[None, :] <= q_pos[:, None]
        s = jnp.where(mask, s, -jnp.inf)
        p = jax.nn.softmax(s, axis=-1).astype(v.dtype)
        return jnp.einsum("bhqk,bkhd->bqhd", p, v)

    out = lax.map(one_block, jnp.arange(n_blocks))
    return out.transpose(1, 0, 2, 3, 4).reshape(bsz, seq, N_HEADS, V_HEAD)


def mla_mixer(q_lat, kv_lat, k_rope, cos, sin, q_norm_g, w_uq, kv_norm_g, w_ukv):
    bsz, seq, _ = q_lat.shape
    q = (rmsnorm(q_lat, q_norm_g) @ w_uq).reshape(bsz, seq, N_HEADS, QK_NOPE + QK_ROPE)
    kv = (rmsnorm(kv_lat, kv_norm_g) @ w_ukv).reshape(bsz, seq, N_HEADS, QK_NOPE + V_HEAD)
    q_nope, q_rope = q[..., :QK_NOPE], q[..., QK_NOPE:]
    k_nope, v = kv[..., :QK_NOPE], kv[..., QK_NOPE:]
    q_rope = apply_rope(q_rope, cos[:, :, None, :], sin[:, :, None, :])
    k_rope = apply_rope(k_rope, cos, sin)
    out = causal_block_attention(q_nope, q_rope, k_nope, k_rope, v)
    return out.reshape(bsz, seq, D_ATTN)


def setup_inputs(seed: int = 0) -> dict:
    key = jax.random.key(seed)
    ks = jax.random.split(key, 32)
    f32 = jnp.float32
    L = DEPTH

    def nrm(k, shape, scale):
        return jax.random.normal(k, shape, f32) * scale

    def gain(k, shape):
        return 1.0 + 0.02 * jax.random.normal(k, shape, f32)

    x = jax.random.normal(ks[0], (BATCH, SEQ, D_MODEL), f32)
    c = jax.random.normal(ks[1], (BATCH, D_MODEL), f32)
    offset = jax.random.randint(ks[2], (BATCH, 1), 0, 2048, dtype=jnp.int32)
    positions = offset + jnp.arange(SEQ, dtype=jnp.int32)[None, :]

    n_idx = jnp.arange(SSM_STATE, dtype=f32)
    lam_re = -0.5 * jnp.exp(0.01 * jax.random.normal(ks[3], (L, N_SSM_GROUPS, SSM_STATE), f32))
    lam_im = math.pi * n_idx + 0.01 * jax.random.normal(ks[4], (L, N_SSM_GROUPS, SSM_STATE), f32)
    log_dt = jax.random.uniform(ks[5], (L, N_SSM_GROUPS), f32, math.log(DT_MIN), math.log(DT_MAX))

    return {
        "x": x,
        "c": c,
        "positions": positions,
        "ada_w": nrm(ks[6], (L, D_MODEL, N_MOD * D_MODEL), 0.5 * D_MODEL ** -0.5),
        "ada_b": nrm(ks[7], (L, N_MOD * D_MODEL), 0.02),
        "norm1_g": gain(ks[8], (L, D_MODEL)),
        "w_in": nrm(ks[9], (L, D_MODEL, IN_COLS), D_MODEL ** -0.5),
        "ssm_lambda_re": lam_re,
        "ssm_lambda_im": lam_im,
        "ssm_b_re": nrm(ks[10], (L, N_SSM_GROUPS, SSM_STATE, SSM_GROUP), (2 * SSM_GROUP) ** -0.5),
        "ssm_b_im": nrm(ks[11], (L, N_SSM_GROUPS, SSM_STATE, SSM_GROUP), (2 * SSM_GROUP) ** -0.5),
        "ssm_c_re": nrm(ks[12], (L, N_SSM_GROUPS, SSM_GROUP, SSM_STATE), (2 * SSM_STATE) ** -0.5),
        "ssm_c_im": nrm(ks[13], (L, N_SSM_GROUPS, SSM_GROUP, SSM_STATE), (2 * SSM_STATE) ** -0.5),
        "ssm_d": nrm(ks[14], (L, N_SSM_GROUPS, SSM_GROUP), 1.0),
        "ssm_log_dt": log_dt,
        "w_glu": nrm(ks[15], (L, D_SSM, 2 * D_SSM), D_SSM ** -0.5),
        "q_norm_g": gain(ks[16], (L, Q_LORA)),
        "w_uq": nrm(ks[17], (L, Q_LORA, N_HEADS * (QK_NOPE + QK_ROPE)), Q_LORA ** -0.5),
        "kv_norm_g": gain(ks[18], (L, KV_LORA)),
        "w_ukv": nrm(ks[19], (L, KV_LORA, N_HEADS * (QK_NOPE + V_HEAD)), KV_LORA ** -0.5),
        "ssm_out_g": gain(ks[20], (L, D_SSM)),
        "attn_out_g": gain(ks[21], (L, D_ATTN)),
        "w_out": nrm(ks[22], (L, D_MIX, D_MODEL), D_MIX ** -0.5),
        "norm2_g": gain(ks[23], (L, D_MODEL)),
        "w_ff1": nrm(ks[24], (L, D_MODEL, D_FF), D_MODEL ** -0.5),
        "w_ff2": nrm(ks[25], (L, D_FF, D_MODEL), D_FF ** -0.5),
        "final_ada_w": nrm(ks[26], (D_MODEL, 2 * D_MODEL), 0.5 * D_MODEL ** -0.5),
        "final_ada_b": nrm(ks[27], (2 * D_MODEL,), 0.02),
        "final_norm_g": gain(ks[28], (D_MODEL,)),
    }


def reference(x, c, positions, ada_w, ada_b, norm1_g, w_in, ssm_lambda_re, ssm_lambda_im,
              ssm_b_re, ssm_b_im, ssm_c_re, ssm_c_im, ssm_d, ssm_log_dt, w_glu,
              q_norm_g, w_uq, kv_norm_g, w_ukv, ssm_out_g, attn_out_g, w_out,
              norm2_g, w_ff1, w_ff2, final_ada_w, final_ada_b, final_norm_g):
    cond = jax.nn.silu(c)
    cos, sin = rope_tables(positions)
    s1 = D_SSM
    s2 = s1 + Q_LORA
    s3 = s2 + KV_LORA
    for l in range(DEPTH):
        mod = (cond @ ada_w[l] + ada_b[l])[:, None, :]
        shift1, scale1, gate1, shift2, scale2, gate2 = jnp.split(mod, N_MOD, axis=-1)

        h = rmsnorm(x, norm1_g[l]) * (1.0 + scale1) + shift1
        proj = h @ w_in[l]
        u = proj[..., :s1]
        q_lat = proj[..., s1:s2]
        kv_lat = proj[..., s2:s3]
        k_rope = proj[..., s3:]
        y_ssm = s5_mixer(u, ssm_lambda_re[l], ssm_lambda_im[l], ssm_b_re[l], ssm_b_im[l],
                         ssm_c_re[l], ssm_c_im[l], ssm_d[l], ssm_log_dt[l], w_glu[l])
        y_attn = mla_mixer(q_lat, kv_lat, k_rope, cos, sin, q_norm_g[l], w_uq[l],
                           kv_norm_g[l], w_ukv[l])
        y = jnp.concatenate([rmsnorm(y_ssm, ssm_out_g[l]), rmsnorm(y_attn, attn_out_g[l])], axis=-1)
        x = x + gate1 * (y @ w_out[l])

        h = rmsnorm(x, norm2_g[l]) * (1.0 + scale2) + shift2
        ff = jnp.square(jax.nn.relu(h @ w_ff1[l])) @ w_ff2[l]
        x = x + gate2 * ff

    fmod = (cond @ final_ada_w + final_ada_b)[:, None, :]
    fshift, fscale = jnp.split(fmod, 2, axis=-1)
    return rmsnorm(x, final_norm_g) * (1.0 + fscale) + fshift
```

```python
import contextlib
import math
import numpy as np
import ml_dtypes
import concourse.bass as bass
import concourse.mybir as mybir
from concourse.bass_utils import run_bass_kernel_spmd

F32 = mybir.dt.float32
BF16 = mybir.dt.bfloat16
I32 = mybir.dt.int32
AF = mybir.ActivationFunctionType
ALU = mybir.AluOpType
AX = mybir.AxisListType

D = 1024
SEQ = 4096
NSEQ = 2
DFF = 4096
EPS = 1e-6
D_SSM = 512
Q_LORA = 384
KV_LORA = 256
QK_ROPE = 32
IN_COLS = 1184
NH = 8


class Buf:
    __slots__ = ("w", "r", "sem", "semcnt", "name")

    def __init__(self, name=""):
        self.w = None
        self.r = []
        self.sem = None
        self.semcnt = 0
        self.name = name


class Eng:
    def __init__(self, nc, name, handle, needed=None):
        self.name = name
        self.h = handle
        self.sem = nc.semaphore("prog_" + name).__enter__()
        self.cnt = 0
        self.incs = 0
        self.waited = {}
        self.needed = needed
        self.used = set()
        self.val = {}


class KB:
    def __init__(self, nc, needed=None):
        self.nc = nc
        nd = needed or {}
        self.PE = Eng(nc, "pe", nc.tensor, nd.get("pe"))
        self.ACT = Eng(nc, "act", nc.scalar, nd.get("act"))
        self.DVE = Eng(nc, "dve", nc.vector, nd.get("dve"))
        self.POOL = Eng(nc, "pool", nc.gpsimd, nd.get("pool"))
        self.SP = Eng(nc, "sp", nc.sync, nd.get("sp"))
        self.nsb = 0
        self.nps = 0
        self.out_tokens = []
        self.stacks = [contextlib.ExitStack()]
        self.dma_bufs = []

    def sb(self, shape, dt, name=None):
        self.nsb += 1
        return self.stacks[-1].enter_context(self.nc.sbuf_tensor(f"{name or 'sb'}_{self.nsb}", list(shape), dt))

    def ps(self, shape, dt, name=None):
        self.nps += 1
        return self.stacks[-1].enter_context(self.nc.psum_tensor(f"{name or 'ps'}_{self.nps}", list(shape), dt))

    def push(self):
        self.stacks.append(contextlib.ExitStack())

    def pop(self):
        engs = [self.PE, self.ACT, self.DVE, self.POOL, self.SP]
        for e in engs:
            for o in engs:
                if o.cnt > 0 and (o is not e or e is not self.PE):
                    self._wait(e, (o.sem, o.cnt, o))
            for b in self.dma_bufs:
                self._wait(e, (b.sem, b.semcnt, None))
        self.stacks.pop().close()

    def _wait(self, eng, tok):
        sem, val, owner = tok
        key = id(sem)
        if eng.waited.get(key, 0) >= val:
            return
        eng.waited[key] = val
        if owner is not None:
            owner.used.add(val)
            eng.h.wait_ge(sem, owner.val[val])
        else:
            eng.h.wait_ge(sem, val)

    def _deps(self, eng, reads, writes):
        for b in reads:
            if b.w is not None:
                if b.w[2] is eng and eng is self.PE:
                    continue
                self._wait(eng, b.w)
        for b in writes:
            if b.w is not None and b.w[2] is not eng:
                self._wait(eng, b.w)
            for t in b.r:
                if t[2] is not eng:
                    self._wait(eng, t)

    def op(self, eng, fn, reads=(), writes=()):
        self._deps(eng, reads, writes)
        ins = fn(eng.h)
        eng.cnt += 1
        if eng.needed is None or eng.cnt in eng.needed:
            eng.incs += 1
            ins.then_inc(eng.sem, 1)
            eng.val[eng.cnt] = eng.incs
        tok = (eng.sem, eng.cnt, eng)
        for b in reads:
            b.r.append(tok)
        for b in writes:
            b.w = tok
            b.r = []
        return tok

    def dma(self, eng, out, in_, reads=(), writes=(), track=None, **kw):
        self._deps(eng, reads, writes)
        tb = track or (writes[0] if writes else reads[0])
        if tb.sem is None:
            tb.sem = self.nc.semaphore("dma_" + str(id(tb))).__enter__()
            self.dma_bufs.append(tb)
        ins = eng.h.dma_start(out=out, in_=in_, **kw)
        tb.semcnt += 16
        ins.then_inc(tb.sem, 16)
        tok = (tb.sem, tb.semcnt, None)
        for b in reads:
            b.r.append(tok)
        for b in writes:
            b.w = tok
            b.r = []
        return tok


def bcast_rows(ap, n):
    return ap.partition_broadcast(n)


def build_program(nseq=NSEQ, seq=SEQ, phases=("p0", "p1", "p2"), dbg=None):
    _, kb1 = _build_once(nseq, seq, phases, dbg, None)
    needed = {e.name: set(e.used) for e in (kb1.PE, kb1.ACT, kb1.DVE, kb1.POOL, kb1.SP)}
    nc, _ = _build_once(nseq, seq, phases, dbg, needed)
    return nc


def _build_once(nseq, seq, phases, dbg, needed):
    nc = bass.Bass("TRN2", target_bir_lowering=False)
    ntok = nseq * seq

    def dram_in(name, shape, dt=F32):
        return nc.dram_tensor(name, list(shape), dt, kind="ExternalInput").ap()

    x = dram_in("x", [ntok, D])
    c = dram_in("c", [nseq, D])
    positions = dram_in("positions", [nseq, seq], I32)
    ada_w = dram_in("ada_w", [D, 6 * D])
    ada_b = dram_in("ada_b", [1, 6 * D])
    norm1_g = dram_in("norm1_g", [1, D])
    w_in = dram_in("w_in", [D, IN_COLS])
    lam_re = dram_in("ssm_lambda_re", [32, 64])
    lam_im = dram_in("ssm_lambda_im", [32, 64])
    b_re = dram_in("ssm_b_re", [32, 64, 16])
    b_im = dram_in("ssm_b_im", [32, 64, 16])
    c_re = dram_in("ssm_c_re", [32, 16, 64])
    c_im = dram_in("ssm_c_im", [32, 16, 64])
    ssm_d = dram_in("ssm_d", [32, 16])
    log_dt = dram_in("ssm_log_dt", [1, 32])
    w_glu = dram_in("w_glu", [D_SSM, 2 * D_SSM])
    q_norm_g = dram_in("q_norm_g", [1, Q_LORA])
    w_uq = dram_in("w_uq", [Q_LORA, 768])
    kv_norm_g = dram_in("kv_norm_g", [1, KV_LORA])
    w_ukv = dram_in("w_ukv", [KV_LORA, 1024])
    ssm_out_g = dram_in("ssm_out_g", [1, 512])
    attn_out_g = dram_in("attn_out_g", [1, 512])
    w_out = dram_in("w_out", [D, D])
    norm2_g = dram_in("norm2_g", [1, D])
    w_ff1 = dram_in("w_ff1", [D, DFF])
    w_ff2 = dram_in("w_ff2", [DFF, D])
    final_ada_w = dram_in("final_ada_w", [D, 2 * D])
    final_ada_b = dram_in("final_ada_b", [1, 2 * D])
    final_norm_g = dram_in("final_norm_g", [1, D])
    ident_in = dram_in("ident", [128, 128])
    invf_in = dram_in("invf", [128, 1])
    tri_in = dram_in("tri", [128, 128])
    kr32_in = dram_in("kr32", [128, 32])
    cramp_in = dram_in("cramp", [128, 64])
    mask8_in = dram_in("mask8", [128, 128])
    sgn_in = dram_in("sgn", [128, 1])

    out = nc.dram_tensor("out", [ntok, D], F32, kind="ExternalOutput").ap()
    modsc = nc.dram_tensor("modsc", [nseq, 8 * D], F32, kind="Internal").ap()
    ropesc = nc.dram_tensor("ropesc", [2, 32, ntok], F32, kind="Internal").ap()
    gnsc = nc.dram_tensor("gnsc", [D_SSM, ntok], BF16, kind="Internal").ap()
    ROPESC, GNSC = Buf("ropesc"), Buf("gnsc")
    MODSC = Buf("modsc")
    OUTB = Buf("out_hbm")

    kb = KB(nc, needed)
    PE, ACT, DVE, POOL, SP = kb.PE, kb.ACT, kb.DVE, kb.POOL, kb.SP

    ident_f = kb.sb([128, 128], F32, "ident_f")
    ident_b = kb.sb([128, 128], BF16, "ident_b")
    IDF, IDB = Buf("idf"), Buf("idb")
    kb.dma(SP, ident_f[:], ident_in[:, :], writes=[IDF])
    kb.op(DVE, lambda e: e.tensor_copy(out=ident_b[:], in_=ident_f[:]), reads=[IDF], writes=[IDB])

    epsc = kb.sb([128, 1], F32, "epsc")
    EPSC = Buf("eps")
    kb.op(DVE, lambda e: e.memset(epsc[:], EPS), writes=[EPSC])

    kb.push()
    cT = kb.sb([128, 8, nseq], F32, "cT")
    CT = Buf("cT")
    for k in range(8):
        kb.dma(SP, cT[:, k, :], c[:, k * 128:(k + 1) * 128].rearrange("b p -> p b"), writes=[CT],
               allow_slow_non_contiguous=True)
    condT = kb.sb([128, 8, nseq], F32, "condT")
    COND = Buf("cond")
    kb.op(ACT, lambda e: e.activation(out=condT[:], in_=cT[:], func=AF.Silu), reads=[CT], writes=[COND])

    NAW = 8
    aw = [kb.sb([128, 8, 512], BF16, f"aw{i}") for i in range(NAW)]
    condTb = kb.sb([128, 8, nseq], BF16, "condTb")
    CONDB = Buf()
    kb.op(DVE, lambda e: e.tensor_copy(out=condTb[:], in_=condT[:]), reads=[COND], writes=[CONDB])
    AW = [Buf(f"aw{i}") for i in range(NAW)]
    brow = [kb.sb([nseq, 512], F32, f"brow{i}") for i in range(NAW)]
    BROW = [Buf() for _ in range(NAW)]
    mrow = [kb.sb([nseq, 512], F32, f"mrow{i}") for i in range(NAW)]
    MROW = [Buf() for _ in range(NAW)]
    ps_mod = [kb.ps([128, 512], F32, f"ps_mod{i}") for i in range(NAW)]
    PSM = [Buf() for _ in range(NAW)]
    pieces = [(ada_w, ada_b, i * 512, i * 512) for i in range(12)] + \
             [(final_ada_w, final_ada_b, i * 512, 6 * D + i * 512) for i in range(4)]
    def load_piece(pi):
        wsrc, bsrc, coff, doff = pieces[pi]
        i = pi % NAW
        for kh in range(2):
            kb.dma(POOL, aw[i][:, 4 * kh:4 * kh + 4, :],
                   wsrc[512 * kh:512 * (kh + 1), coff:coff + 512].rearrange("(k p) n -> p k n", p=128), writes=[AW[i]])

    for pi in range(NAW):
        load_piece(pi)
    if "p1" in phases:
        kb.push()
        RC = min(2048, seq)
        invf = kb.sb([96, 1], F32, "invf")
        INVF = Buf()
        kb.dma(SP, invf[64:96, :], invf_in[64:96, :], writes=[INVF])
        posi = kb.sb([96, RC], I32, "posi")
        posf = kb.sb([96, RC], F32, "posf")
        yy = kb.sb([96, RC], F32, "rope_y")
        yi = kb.sb([96, RC], I32, "rope_yi")
        yf = kb.sb([96, RC], F32, "rope_yf")
        tab = kb.sb([96, RC], F32, "rope_tab")
        POSI, POSF, YY, YI, YF, TAB = Buf(), Buf(), Buf(), Buf(), Buf(), Buf()
        R = slice(64, 96)
        for b in range(nseq):
            for c0 in range(0, seq, RC):
                kb.dma(SP, posi[R, :], positions[b:b + 1, c0:c0 + RC].partition_broadcast(32), writes=[POSI])
                kb.op(DVE, lambda e: e.tensor_copy(out=posf[R, :], in_=posi[R, :]), reads=[POSI], writes=[POSF])
                for which, off in ((1, 0.0), (0, 0.25)):
                    kb.op(DVE, lambda e, off=off: e.tensor_scalar(out=yy[R, :], in0=posf[R, :], scalar1=invf[R, 0:1],
                                                                  scalar2=off, op0=ALU.mult, op1=ALU.add),
                          reads=[POSF, INVF], writes=[YY])
                    kb.op(DVE, lambda e: e.tensor_copy(out=yi[R, :], in_=yy[R, :]), reads=[YY], writes=[YI])
                    kb.op(DVE, lambda e: e.tensor_copy(out=yf[R, :], in_=yi[R, :]), reads=[YI], writes=[YF])
                    kb.op(DVE, lambda e: e.tensor_tensor(out=yy[R, :], in0=yy[R, :], in1=yf[R, :], op=ALU.subtract),
                          reads=[YY, YF], writes=[YY])
                    kb.op(ACT, lambda e: e.activation(out=tab[R, :], in_=yy[R, :], func=AF.Sin, scale=2.0 * math.pi),
                          reads=[YY], writes=[TAB])
                    kb.dma(SP, ropesc[which, :, b * seq + c0:b * seq + c0 + RC], tab[R, :], reads=[TAB], writes=[ROPESC])
        kb.pop()


    for pi, (wsrc, bsrc, coff, doff) in enumerate(pieces):
        i = pi % NAW
        if pi >= NAW:
            load_piece(pi)
        kb.dma(SP, brow[i][:], bcast_rows(bsrc[0:1, coff:coff + 512], nseq), writes=[BROW[i]])
        for k in range(8):
            kb.op(PE, lambda e, k=k, i=i: e.matmul(ps_mod[i][0:nseq, :], lhsT=condTb[:, k, :], rhs=aw[i][:, k, :],
                                                    start=(k == 0), stop=(k == 7)),
                  reads=[CONDB, AW[i]], writes=[PSM[i]])
        kb.op(DVE, lambda e, i=i: e.tensor_tensor(out=mrow[i][:], in0=ps_mod[i][0:nseq, :], in1=brow[i][:], op=ALU.add),
              reads=[PSM[i], BROW[i]], writes=[MROW[i]])
        kb.dma(SP, modsc[:, doff:doff + 512], mrow[i][:], reads=[MROW[i]], writes=[MODSC])

    kb.pop()

    def rsqrt_cols(dst, src, inv_n, SRC, DST):
        kb.op(ACT, lambda e: e.activation(out=dst, in_=src, func=AF.Ln, bias=epsc[:, 0:1], scale=inv_n),
              reads=[SRC, EPSC], writes=[DST])
        kb.op(ACT, lambda e: e.activation(out=dst, in_=dst, func=AF.Exp, scale=-0.5), reads=[DST], writes=[DST])

    if dbg == "setup":
        kb.push()
        t_ = kb.sb([nseq, 8 * D], F32, "dbgt")
        T_ = Buf()
        kb.dma(SP, t_[:], modsc[:, :], reads=[MODSC], writes=[T_])
        for r in range(nseq):
            tok = kb.dma(SP, out[r:r + 1, :].rearrange("o (a n) -> (o a) n", a=1), t_[r:r + 1, 0:D], reads=[T_], writes=[OUTB])
            kb.out_tokens.append(tok)
            tok = kb.dma(SP, out[nseq + r:nseq + r + 1, :], t_[r:r + 1, 7 * D:8 * D], reads=[T_], writes=[OUTB])
            kb.out_tokens.append(tok)
        for tok in kb.out_tokens:
            kb._wait(SP, tok)
        kb.pop()
        return nc, kb

    def load_featmajor(dst_ap, src_row_ap, dstbuf, extra_reads=()):
        kb.dma(SP, dst_ap, src_row_ap.rearrange("o (k p) -> p (o k)", p=128), reads=list(extra_reads), writes=[dstbuf],
               allow_slow_non_contiguous=True)


    if "p0" in phases:
        kb.push()
        T0 = 512
        nt0 = seq // T0
        NC_ = T0 // 8
        TWO_PI = 2.0 * math.pi
        M1a = kb.sb([128, 32, 128], BF16, "M1a")
        M1b = kb.sb([128, 32, 128], BF16, "M1b")
        M2 = kb.sb([128, 32, 128], BF16, "M2")
        M3 = kb.sb([128, 32, 128], BF16, "M3")
        Tc = kb.sb([128, 32, NC_], F32, "Tc")
        Ts = kb.sb([128, 32, NC_], F32, "Ts")
        Rt = kb.sb([128, 32, NC_], F32, "Rt")
        Rho = kb.sb([128, 32], F32, "Rho")
        M1A, M1B, M2B, M3B, TCB, TSB, RTB, RHO = (Buf() for _ in range(8))
        Bk0 = [kb.ps([128, 512], F32, f"p0_bank{i}") for i in range(8)]
        BK0 = [Buf(f"p0bank{i}") for i in range(8)]

        kb.push()
        kr32 = kb.sb([128, 32], F32, "kr32")
        cramp = kb.sb([128, NC_], F32, "cramp")
        mask8 = kb.sb([128, 128], F32, "mask8")
        sgn = kb.sb([128, 1], F32, "sgn")
        KR32, CRAMP, MASK8, SGN = Buf(), Buf(), Buf(), Buf()
        kb.dma(SP, kr32[:], kr32_in[:, :], writes=[KR32])
        kb.dma(SP, cramp[:], cramp_in[:, 0:NC_], writes=[CRAMP])
        kb.dma(SP, mask8[:], mask8_in[:, :], writes=[MASK8])
        kb.dma(SP, sgn[:], sgn_in[:, :], writes=[SGN])

        def DV(fn, reads, writes):
            return kb.op(DVE, fn, reads=reads, writes=writes)

        def frac_(t_ap, shape, TB):
            kb.push()
            ti = kb.sb(shape, I32, "frac_i")
            tf = kb.sb(shape, F32, "frac_f")
            TI, TF = Buf(), Buf()
            DV(lambda e: e.tensor_copy(out=ti[:], in_=t_ap), [TB], [TI])
            DV(lambda e: e.tensor_copy(out=tf[:], in_=ti[:]), [TI], [TF])
            DV(lambda e: e.tensor_tensor(out=t_ap, in0=t_ap, in1=tf[:], op=ALU.subtract), [TB, TF], [TB])
            kb.pop()

        lam2 = kb.sb([32, 2, 128], F32, "lam2")
        LAM2 = Buf()
        for ri, src in enumerate((lam_re, lam_im)):
            for du in range(2):
                kb.dma(SP, lam2[:, ri, du * 64:(du + 1) * 64], src[:, :], writes=[LAM2])
        lamre2 = kb.sb([128, 32], F32, "lamre2")
        lamim2 = kb.sb([128, 32], F32, "lamim2")
        LRE, LIM = Buf(), Buf()
        for ri, (dst, DB) in enumerate(((lamre2, LRE), (lamim2, LIM))):
            kb.op(PE, lambda e, ri=ri: e.transpose(out=Bk0[ri][:, 0:32], in_=lam2[:, ri, :], identity=ident_f[0:32, 0:32]),
                  reads=[LAM2, IDF], writes=[BK0[ri]])
            DV(lambda e, ri=ri, dst=dst: e.tensor_copy(out=dst[:], in_=Bk0[ri][:, 0:32]), [BK0[ri]], [DB])
        dt2 = kb.sb([128, 32], F32, "dt2")
        DT2 = Buf()
        kb.dma(SP, dt2[:], log_dt[0:1, :].partition_broadcast(128), writes=[DT2])
        kb.op(ACT, lambda e: e.activation(out=dt2[:], in_=dt2[:], func=AF.Exp), reads=[DT2], writes=[DT2])
        th = kb.sb([128, 32], F32, "th")
        ld = kb.sb([128, 32], F32, "ld")
        TH, LD = Buf(), Buf()
        DV(lambda e: e.scalar_tensor_tensor(out=th[:], in0=lamim2[:], scalar=1.0 / TWO_PI, in1=dt2[:], op0=ALU.mult,
                                            op1=ALU.mult), [LIM, DT2], [TH])
        DV(lambda e: e.tensor_tensor(out=ld[:], in0=lamre2[:], in1=dt2[:], op=ALU.mult), [LRE, DT2], [LD])

        def powers(ramp_ap, nk, Wre, Wim, WRE, WIM, base_th, BTH, base_ld, BLD, RAMPB):
            shp = [128, 32, nk]
            kb.push()
            y = kb.sb(shp, F32, "pw_y")
            yc = kb.sb(shp, F32, "pw_yc")
            mg = kb.sb(shp, F32, "pw_mg")
            Y, YC, MG = Buf(), Buf(), Buf()
            thb = base_th.unsqueeze(2).to_broadcast(shp)
            ldb = base_ld.unsqueeze(2).to_broadcast(shp)
            rb = ramp_ap.unsqueeze(1).to_broadcast(shp)
            DV(lambda e: e.tensor_tensor(out=y[:], in0=thb, in1=rb, op=ALU.mult), [BTH, RAMPB], [Y])
            frac_(y[:], shp, Y)
            DV(lambda e: e.tensor_scalar(out=yc[:], in0=y[:], scalar1=0.25, scalar2=None, op0=ALU.add), [Y], [YC])
            frac_(yc[:], shp, YC)
            DV(lambda e: e.tensor_tensor(out=mg[:], in0=ldb, in1=rb, op=ALU.mult), [BLD, RAMPB], [MG])
            kb.op(ACT, lambda e: e.activation(out=mg[:], in_=mg[:], func=AF.Exp), reads=[MG], writes=[MG])
            kb.op(ACT, lambda e: e.activation(out=y[:], in_=y[:], func=AF.Sin, scale=TWO_PI), reads=[Y], writes=[Y])
            kb.op(ACT, lambda e: e.activation(out=yc[:], in_=yc[:], func=AF.Sin, scale=TWO_PI), reads=[YC], writes=[YC])
            DV(lambda e: e.tensor_tensor(out=Wre[:], in0=mg[:], in1=yc[:], op=ALU.mult), [MG, YC], [WRE])
            DV(lambda e: e.tensor_tensor(out=Wim[:], in0=mg[:], in1=y[:], op=ALU.mult), [MG, Y], [WIM])
            kb.pop()

        Wre = kb.sb([128, 32, 32], F32, "Wre")
        Wim = kb.sb([128, 32, 32], F32, "Wim")
        WRE, WIM = Buf(), Buf()
        powers(kr32[:], 32, Wre, Wim, WRE, WIM, th[:], TH, ld[:], LD, KR32)
        th8 = kb.sb([128, 32], F32, "th8")
        ld8 = kb.sb([128, 32], F32, "ld8")
        TH8, LD8 = Buf(), Buf()
        DV(lambda e: e.tensor_scalar(out=th8[:], in0=th[:], scalar1=8.0, scalar2=None, op0=ALU.mult), [TH], [TH8])
        frac_(th8[:], [128, 32], TH8)
        DV(lambda e: e.tensor_scalar(out=ld8[:], in0=ld[:], scalar1=8.0, scalar2=None, op0=ALU.mult), [LD], [LD8])
        kb.op(ACT, lambda e: e.activation(out=Rho[:], in_=ld8[:], func=AF.Exp), reads=[LD8], writes=[RHO])
        shpT = [128, 32, NC_]
        kb.push()
        ty = kb.sb(shpT, F32, "ty")
        TY = Buf()
        DV(lambda e: e.tensor_tensor(out=ty[:], in0=th8[:].unsqueeze(2).to_broadcast(shpT),
                                     in1=cramp[:].unsqueeze(1).to_broadcast(shpT), op=ALU.mult), [TH8, CRAMP], [TY])
        frac_(ty[:], shpT, TY)
        kb.op(ACT, lambda e: e.activation(out=Ts[:], in_=ty[:], func=AF.Sin, scale=TWO_PI), reads=[TY], writes=[TSB])
        DV(lambda e: e.tensor_scalar(out=Ts[:], in0=Ts[:], scalar1=sgn[:, 0:1], scalar2=None, op0=ALU.mult), [TSB, SGN], [TSB])
        DV(lambda e: e.tensor_scalar(out=ty[:], in0=ty[:], scalar1=0.25, scalar2=None, op0=ALU.add), [TY], [TY])
        frac_(ty[:], shpT, TY)
        kb.op(ACT, lambda e: e.activation(out=Tc[:], in_=ty[:], func=AF.Sin, scale=TWO_PI), reads=[TY], writes=[TCB])
        kb.pop()
        DV(lambda e: e.memset(Rt[:], 0.0), [], [RTB])
        DV(lambda e: e.tensor_copy(out=Rt[:, :, 1:NC_], in_=Rho[:].unsqueeze(2).to_broadcast([128, 32, NC_ - 1])),
           [RHO], [RTB])
        def small(name):
            return kb.sb([128, 32], F32, name), Buf()
        lr, LR = small("lr"); nre, NRE = small("nre"); nim, NIM = small("nim"); den, DEN = small("den")
        tq, TQ = small("tq"); kre, KRE = small("kre"); kim, KIM = small("kim")
        DV(lambda e: e.tensor_scalar(out=lr[:], in0=Wre[:, :, 17], scalar1=-1.0, scalar2=None, op0=ALU.add), [WRE], [LR])
        DV(lambda e: e.tensor_tensor(out=nre[:], in0=lr[:], in1=lamre2[:], op=ALU.mult), [LR, LRE], [NRE])
        DV(lambda e: e.tensor_tensor(out=tq[:], in0=Wim[:, :, 17], in1=lamim2[:], op=ALU.mult), [WIM, LIM], [TQ])
        DV(lambda e: e.tensor_tensor(out=nre[:], in0=nre[:], in1=tq[:], op=ALU.add), [NRE, TQ], [NRE])
        DV(lambda e: e.tensor_tensor(out=nim[:], in0=Wim[:, :, 17], in1=lamre2[:], op=ALU.mult), [WIM, LRE], [NIM])
        DV(lambda e: e.tensor_tensor(out=tq[:], in0=lr[:], in1=lamim2[:], op=ALU.mult), [LR, LIM], [TQ])
        DV(lambda e: e.tensor_tensor(out=nim[:], in0=nim[:], in1=tq[:], op=ALU.subtract), [NIM, TQ], [NIM])
        DV(lambda e: e.tensor_tensor(out=den[:], in0=lamre2[:], in1=lamre2[:], op=ALU.mult), [LRE], [DEN])
        DV(lambda e: e.tensor_tensor(out=tq[:], in0=lamim2[:], in1=lamim2[:], op=ALU.mult), [LIM], [TQ])
        DV(lambda e: e.tensor_tensor(out=den[:], in0=den[:], in1=tq[:], op=ALU.add), [DEN, TQ], [DEN])
        DV(lambda e: e.reciprocal(out=den[:], in_=den[:]), [DEN], [DEN])
        DV(lambda e: e.tensor_tensor(out=kre[:], in0=nre[:], in1=den[:], op=ALU.mult), [NRE, DEN], [KRE])
        DV(lambda e: e.tensor_tensor(out=kim[:], in0=nim[:], in1=den[:], op=ALU.mult), [NIM, DEN], [KIM])
        shB = [128, 32, 16]
        b2re = kb.sb(shB, F32, "b2re"); b2im = kb.sb(shB, F32, "b2im")
        B2 = Buf()
        for du in range(2):
            kb.dma(SP, b2re[du * 64:(du + 1) * 64, :, :], b_re.rearrange("g p h -> p g h"), writes=[B2])
            kb.dma(SP, b2im[du * 64:(du + 1) * 64, :, :], b_im.rearrange("g p h -> p g h"), writes=[B2])
        bbre = kb.sb(shB, F32, "bbre"); bbim = kb.sb(shB, F32, "bbim")
        tb1 = kb.sb(shB, F32, "tb1"); tb2 = kb.sb(shB, F32, "tb2")
        BBRE, BBIM, TB1, TB2 = Buf(), Buf(), Buf(), Buf()
        kreb = kre[:].unsqueeze(2).to_broadcast(shB)
        kimb = kim[:].unsqueeze(2).to_broadcast(shB)
        DV(lambda e: e.tensor_tensor(out=tb1[:], in0=b2re[:], in1=kreb, op=ALU.mult), [B2, KRE], [TB1])
        DV(lambda e: e.tensor_tensor(out=tb2[:], in0=b2im[:], in1=kimb, op=ALU.mult), [B2, KIM], [TB2])
        DV(lambda e: e.tensor_tensor(out=bbre[:], in0=tb1[:], in1=tb2[:], op=ALU.subtract), [TB1, TB2], [BBRE])
        DV(lambda e: e.tensor_tensor(out=tb1[:], in0=b2im[:], in1=kreb, op=ALU.mult), [B2, KRE], [TB1])
        DV(lambda e: e.tensor_tensor(out=tb2[:], in0=b2re[:], in1=kimb, op=ALU.mult), [B2, KIM], [TB2])
        DV(lambda e: e.tensor_tensor(out=bbim[:], in0=tb1[:], in1=tb2[:], op=ALU.add), [TB1, TB2], [BBIM])
        cdup = kb.sb([128, 4, 2, 128], F32, "cdup")
        CDUP = Buf()
        for j in range(4):
            for ri, src in enumerate((c_re, c_im)):
                for du in range(2):
                    kb.dma(SP, cdup[:, j, ri, du * 64:(du + 1) * 64],
                           src[j * 8:(j + 1) * 8, :, :].rearrange("g h p -> (g h) p"), writes=[CDUP])
        c2re = kb.sb(shB, F32, "c2re"); c2im = kb.sb(shB, F32, "c2im")
        C2RE, C2IM = Buf(), Buf()
        for j in range(4):
            for ri, (dst, DB) in enumerate(((c2re, C2RE), (c2im, C2IM))):
                bk = (j * 2 + ri) % 4
                kb.op(PE, lambda e, j=j, ri=ri, bk=bk: e.transpose(out=Bk0[bk][:, 0:128], in_=cdup[:, j, ri, :],
                                                                   identity=ident_f[:]),
                      reads=[CDUP, IDF], writes=[BK0[bk]])
                DV(lambda e, j=j, dst=dst, bk=bk: e.tensor_copy(
                    out=dst[:, j * 8:(j + 1) * 8, :], in_=Bk0[bk][:, 0:128].rearrange("p (g h) -> p g h", g=8)),
                   [BK0[bk]], [DB])
        drep = kb.sb([32, 8, 16], F32, "drep")
        DREP = Buf()
        for ta in range(8):
            kb.dma(SP, drep[:, ta, :], ssm_d[:, :], writes=[DREP])
        dcol = kb.sb([128, 32], F32, "dcol")
        DCOL = Buf()
        kb.op(PE, lambda e: e.transpose(out=Bk0[4][:, 0:32], in_=drep[:].rearrange("g t h -> g (t h)"),
                                        identity=ident_f[0:32, 0:32]), reads=[DREP, IDF], writes=[BK0[4]])
        DV(lambda e: e.tensor_copy(out=dcol[:], in_=Bk0[4][:, 0:32]), [BK0[4]], [DCOL])

        sh4 = [128, 32, 8, 16]
        pre = kb.sb(sh4, F32, "pre"); pim = kb.sb(sh4, F32, "pim")
        ta_ = kb.sb(sh4, F32, "cp_t1"); tb_ = kb.sb(sh4, F32, "cp_t2")
        arr = kb.sb(sh4, F32, "arr")
        PRE, PIM, TA_, TB_, ARR = Buf(), Buf(), Buf(), Buf(), Buf()

        def cprod(k0, vre, vim, VRE, VIM):
            wre_b = Wre[:, :, k0:k0 + 8].unsqueeze(3).to_broadcast(sh4)
            wim_b = Wim[:, :, k0:k0 + 8].unsqueeze(3).to_broadcast(sh4)
            vre_b = vre[:].unsqueeze(2).to_broadcast(sh4)
            vim_b = vim[:].unsqueeze(2).to_broadcast(sh4)
            DV(lambda e: e.tensor_tensor(out=ta_[:], in0=wre_b, in1=vre_b, op=ALU.mult), [WRE, VRE], [TA_])
            DV(lambda e: e.tensor_tensor(out=tb_[:], in0=wim_b, in1=vim_b, op=ALU.mult), [WIM, VIM], [TB_])
            DV(lambda e: e.tensor_tensor(out=pre[:], in0=ta_[:], in1=tb_[:], op=ALU.subtract), [TA_, TB_], [PRE])
            DV(lambda e: e.tensor_tensor(out=ta_[:], in0=wre_b, in1=vim_b, op=ALU.mult), [WRE, VIM], [TA_])
            DV(lambda e: e.tensor_tensor(out=tb_[:], in0=wim_b, in1=vre_b, op=ALU.mult), [WIM, VRE], [TB_])
            DV(lambda e: e.tensor_tensor(out=pim[:], in0=ta_[:], in1=tb_[:], op=ALU.add), [TA_, TB_], [PIM])

        def arrange(top, TOP, bot, BOT, bot_sign, dst_ap, DST):
            DV(lambda e: e.tensor_copy(out=dst_ap[0:64], in_=top[0:64]), [TOP], [DST])
            DV(lambda e: e.tensor_scalar(out=dst_ap[64:128], in0=bot[64:128], scalar1=bot_sign, scalar2=None, op0=ALU.mult),
               [BOT], [DST])

        def transposed_to(dstM, DSTM):
            for g4 in range(8):
                bk = g4 % 2
                for gl in range(4):
                    g = g4 * 4 + gl
                    kb.op(PE, lambda e, g=g, gl=gl, bk=bk: e.transpose(
                        out=Bk0[bk][:, gl * 128:(gl + 1) * 128], in_=arr[:, g, :, :].rearrange("p a b -> p (a b)"),
                        identity=ident_f[:]), reads=[ARR, IDF], writes=[BK0[bk]])
                DV(lambda e, g4=g4, bk=bk: e.tensor_copy(out=dstM[:, g4 * 4:(g4 + 1) * 4, :].rearrange("p a b -> p (a b)"),
                                                         in_=Bk0[bk][:]), [BK0[bk]], [DSTM])

        cprod(0, bbre, bbim, BBRE, BBIM)
        arrange(pre[:], PRE, pim[:], PIM, 1.0, arr[:], ARR)
        transposed_to(M1a, M1A)
        arrange(pim[:], PIM, pre[:], PRE, 1.0, arr[:], ARR)
        transposed_to(M1b, M1B)
        xarr = kb.sb(sh4, F32, "xarr")
        XARR = Buf()
        cprod(8, bbre, bbim, BBRE, BBIM)
        arrange(pre[:], PRE, pim[:], PIM, 1.0, xarr[:], XARR)
        cprod(16, c2re, c2im, C2RE, C2IM)
        arrange(pre[:], PRE, pim[:], PIM, -1.0, arr[:], ARR)
        m2t = kb.sb([128, 128], F32, "m2t")
        M2T = Buf()
        for g in range(32):
            bk = 2 + g % 2
            kb.op(PE, lambda e, g=g, bk=bk: e.matmul(Bk0[bk][:, 0:128], lhsT=xarr[:, g, :, :].rearrange("p a b -> p (a b)"),
                                                     rhs=arr[:, g, :, :].rearrange("p a b -> p (a b)"), start=True, stop=True),
                  reads=[XARR, ARR], writes=[BK0[bk]])
            DV(lambda e, bk=bk: e.tensor_tensor(out=m2t[:], in0=Bk0[bk][:, 0:128], in1=mask8[:], op=ALU.mult),
               [BK0[bk], MASK8], [M2T])
            DV(lambda e, g=g: e.scalar_tensor_tensor(out=M2[:, g, :], in0=ident_f[:], scalar=dcol[:, g:g + 1], in1=m2t[:],
                                                     op0=ALU.mult, op1=ALU.add), [IDF, DCOL, M2T], [M2B])
        cprod(24, c2re, c2im, C2RE, C2IM)
        arrange(pre[:], PRE, pim[:], PIM, -1.0, arr[:], ARR)
        DV(lambda e: e.tensor_copy(out=M3[:].rearrange("p g m -> p (g m)"), in_=arr[:].rearrange("p g a b -> p (g a b)")),
           [ARR], [M3B])
        kb.pop()

        w_in_u = kb.sb([128, 8, 512], BF16, "w_in_u")
        wglu = kb.sb([128, 4, 1024], BF16, "wglu")
        WINU, WGLU = Buf(), Buf()
        kb.dma(POOL, w_in_u[:], w_in[:, 0:512].rearrange("(k p) n -> p k n", p=128), writes=[WINU])
        kb.dma(POOL, wglu[:], w_glu[:, :].rearrange("(k p) n -> p k n", p=128), writes=[WGLU])
        n1g0 = kb.sb([128, 8], F32, "p0_n1g")
        N1G0 = Buf()
        load_featmajor(n1g0[:], norm1_g[0:1, :], N1G0)
        ones0 = kb.sb([128, 128], BF16, "p0_ones")
        ONES0 = Buf()
        DV(lambda e: e.memset(ones0[:], 1.0), [], [ONES0])
        xs0 = [kb.sb([128, D], F32, f"p0_xs{i}") for i in range(2)]
        XS0 = [Buf(), Buf()]
        xn0 = kb.sb([128, 4, D], BF16, "p0_xn")
        XN0 = Buf()
        hT0 = kb.sb([128, 8, T0], BF16, "p0_hT")
        HT0 = Buf()
        junk0 = kb.sb([128, D], BF16, "p0_junk")
        JUNK0 = Buf()
        st0 = kb.sb([128, 8], F32, "p0_stat")
        SS0, RS0 = Buf(), Buf()
        sh10 = kb.sb([128, 8], F32, "p0_sh1"); sc10 = kb.sb([128, 8], F32, "p0_sc1"); G10 = kb.sb([128, 8], F32, "p0_G1")
        SH10, SC10, G1B0 = Buf(), Buf(), Buf()
        u8g = kb.sb([64, 32, 8, 16], BF16, "u8g")
        U8G = Buf()
        U8 = [kb.sb([128, 32, NC_], BF16, f"U8_{i}") for i in range(2)]
        U8B = [[Buf() for _ in range(2)] for _ in range(2)]
        qa = kb.sb([128, 512], F32, "q_tA"); qb = kb.sb([128, 512], F32, "q_tB")
        wa = [kb.sb([128, 512], F32, f"q_wa{i}") for i in range(2)]
        wb = [kb.sb([128, 512], F32, f"q_wb{i}") for i in range(2)]
        za = [kb.sb([128, 512], F32, f"q_za{i}") for i in range(2)]
        zb = [kb.sb([128, 512], F32, f"q_zb{i}") for i in range(2)]
        sa = kb.sb([128, 512], F32, "q_sa")
        pe_ = kb.sb([128, 512], F32, "q_pe"); pf_ = kb.sb([128, 512], F32, "q_pf")
        QA, QB, SA, PEB, PFB = (Buf() for _ in range(5))
        WA = [Buf(), Buf()]; WB = [Buf(), Buf()]; ZA = [Buf(), Buf()]; ZB = [Buf(), Buf()]
        t8a = kb.sb([128, 8], F32, "t8a"); t8b = kb.sb([128, 8], F32, "t8b")
        t8c = kb.sb([128, 8], F32, "t8c"); t8d = kb.sb([128, 8], F32, "t8d")
        T8A, T8B, T8C, T8D = Buf(), Buf(), Buf(), Buf()
        Sa_prev = kb.sb([128, 32], F32, "Sa_prev"); Sb_prev = kb.sb([128, 32], F32, "Sb_prev")
        SAP, SBP = Buf(), Buf()
        Sbuf = kb.sb([128, 32, NC_], BF16, "Sbuf")
        SBUF = [Buf() for _ in range(4)]
        Y8g = kb.sb([128, 32, NC_], BF16, "Y8g")
        Y8G = [Buf() for _ in range(4)]
        y8tm = kb.sb([64, 8, 512], BF16, "y8tm")
        Y8TM = [Buf() for _ in range(4)]
        yT = kb.sb([128, 4, T0], BF16, "yT")
        YT = Buf()
        sg = [kb.sb([128, T0], F32, f"sg{i}") for i in range(2)]
        SG = [Buf(), Buf()]
        gT = kb.sb([128, 4, T0], BF16, "gT")
        GT = [Buf() for _ in range(4)]
        gsq = [kb.sb([128, T0], BF16, f"gsq{i}") for i in range(2)]
        GSQ = [Buf(), Buf()]
        rbc0 = kb.sb([128, T0], F32, "p0_rbc")
        RBC0 = Buf()
        Gn0 = kb.sb([128, 4, T0], BF16, "p0_Gn")
        GN0 = Buf()

        for b in range(nseq):
            load_featmajor(sh10[:], modsc[b:b + 1, 0:D], SH10, extra_reads=[MODSC])
            load_featmajor(sc10[:], modsc[b:b + 1, D:2 * D], SC10, extra_reads=[MODSC])
            DV(lambda e: e.scalar_tensor_tensor(out=G10[:], in0=sc10[:], scalar=1.0, in1=n1g0[:], op0=ALU.add, op1=ALU.mult),
               [SC10, N1G0], [G1B0])
            DV(lambda e: e.memset(Sa_prev[:], 0.0), [], [SAP])
            DV(lambda e: e.memset(Sb_prev[:], 0.0), [], [SBP])
            def front0_steps(i):
                tok0 = b * seq + i * T0
                U8_ = U8[i % 2]
                steps = []
                steps.append(lambda: DV(lambda e: e.memset(st0[:, 0:4], 0.0), [], [SS0]))
                for su in range(4):
                    xb_ = su % 2
                    def s_load(su=su, xb_=xb_):
                        kb.dma(SP, xs0[xb_][:], x[tok0 + su * 128:tok0 + (su + 1) * 128, :], writes=[XS0[xb_]])
                    def s_stat(su=su, xb_=xb_):
                        kb.op(ACT, lambda e: e.activation(out=junk0[:], in_=xs0[xb_][:], func=AF.Square,
                                                          accum_out=st0[:, su:su + 1]), reads=[XS0[xb_]], writes=[JUNK0, SS0])
                        rsqrt_cols(st0[:, 4 + su:5 + su], st0[:, su:su + 1], 1.0 / D, SS0, RS0)
                    def s_xn(su=su, xb_=xb_):
                        DV(lambda e: e.tensor_scalar(out=xn0[:, su, :], in0=xs0[xb_][:], scalar1=st0[:, 4 + su:5 + su],
                                                     scalar2=None, op0=ALU.mult), [XS0[xb_], RS0], [XN0])
                    steps += [s_load, s_stat, s_xn]
                for k in range(8):
                    def s_tr(k=k):
                        pi = k % 2
                        tpv = Bk0[pi][:].bitcast(BF16)
                        for su in range(4):
                            kb.op(PE, lambda e, su=su: e.transpose(out=tpv[:, su * 128:(su + 1) * 128],
                                                                   in_=xn0[:, su, k * 128:(k + 1) * 128], identity=ident_b[:]),
                                  reads=[XN0, IDB], writes=[BK0[pi]])
                        kb.op(ACT, lambda e: e.activation(out=hT0[:, k, :], in_=tpv[:, 0:T0], func=AF.Identity,
                                                          bias=sh10[:, k:k + 1], scale=G10[:, k:k + 1]),
                              reads=[BK0[pi], SH10, G1B0], writes=[HT0])
                    steps.append(s_tr)
                for ta in range(8):
                    def s_u8(ta=ta):
                        bk = 2
                        for k in range(8):
                            kb.op(PE, lambda e, k=k: e.matmul(Bk0[bk][0:NC_, :], lhsT=hT0[:, k, ta:T0:8], rhs=w_in_u[:, k, :],
                                                              start=(k == 0), stop=(k == 7)), reads=[HT0, WINU], writes=[BK0[bk]])
                        kb.op(ACT, lambda e: e.activation(out=u8g[:, :, ta, :],
                                                          in_=Bk0[bk][0:NC_, :].rearrange("p (g h) -> p g h", g=32),
                                                          func=AF.Copy), reads=[BK0[bk]], writes=[U8G])
                    steps.append(s_u8)
                for gh in range(2):
                    def s_U8(gh=gh):
                        tpv = Bk0[gh][:].bitcast(BF16)
                        for gl in range(16):
                            g = gh * 16 + gl
                            kb.op(PE, lambda e, g=g, gl=gl: e.transpose(
                                out=tpv[:, gl * NC_:(gl + 1) * NC_], in_=u8g[:, g, :, :].rearrange("p a b -> p (a b)"),
                                identity=ident_b[0:NC_, 0:NC_]), reads=[U8G, IDB], writes=[BK0[gh]])
                        kb.op(ACT, lambda e: e.activation(
                            out=U8_[:, gh * 16:(gh + 1) * 16, :].rearrange("p a b -> p (a b)"), in_=tpv[:, 0:16 * NC_],
                            func=AF.Copy), reads=[BK0[gh]], writes=[U8B[i % 2][gh]])
                    steps.append(s_U8)
                return steps

            bg0 = {"steps": [], "slots": 1}

            def pull0():
                n = len(bg0["steps"])
                if n:
                    k = -(-n // max(bg0["slots"], 1))
                    for _ in range(k):
                        bg0["steps"].pop(0)()
                bg0["slots"] -= 1

            for st_ in front0_steps(0):
                st_()
            for i in range(nt0):
                tok0 = b * seq + i * T0
                U8c = U8[i % 2]
                U8Bc = U8B[i % 2]
                bg0["steps"] = front0_steps(i + 1) if i + 1 < nt0 else []
                bg0["slots"] = 20
                def stage_A(qd):
                    la, lb = 4 + (qd % 2) * 2, 5 + (qd % 2) * 2
                    for gl in range(8):
                        g = qd * 8 + gl
                        kb.op(PE, lambda e, g=g, gl=gl: e.matmul(Bk0[la][:, gl * NC_:(gl + 1) * NC_], lhsT=M1a[:, g, :],
                                                                 rhs=U8c[:, g, :], start=True, stop=True),
                              reads=[M1A, U8Bc[g // 16]], writes=[BK0[la]])
                    for gl in range(8):
                        g = qd * 8 + gl
                        kb.op(PE, lambda e, g=g, gl=gl: e.matmul(Bk0[lb][:, gl * NC_:(gl + 1) * NC_], lhsT=M1b[:, g, :],
                                                                 rhs=U8c[:, g, :], start=True, stop=True),
                              reads=[M1B, U8Bc[g // 16]], writes=[BK0[lb]])

                def stage_B(qd):
                    la, lb = 4 + (qd % 2) * 2, 5 + (qd % 2) * 2
                    pq = qd % 2
                    gs = slice(qd * 8, (qd + 1) * 8)
                    TcQ = Tc[:, gs, :].rearrange("p a b -> p (a b)")
                    TsQ = Ts[:, gs, :].rearrange("p a b -> p (a b)")
                    RQ = Rt[:, gs, :].rearrange("p a b -> p (a b)")
                    La, Lb = Bk0[la][:], Bk0[lb][:]
                    wa_, wb_, za_, zb_ = wa[pq], wb[pq], za[pq], zb[pq]
                    WA_, WB_, ZA_, ZB_ = WA[pq], WB[pq], ZA[pq], ZB[pq]
                    DV(lambda e: e.tensor_tensor(out=qa[:], in0=La, in1=TcQ, op=ALU.mult), [BK0[la], TCB], [QA])
                    DV(lambda e: e.tensor_tensor(out=qb[:], in0=Lb, in1=TsQ, op=ALU.mult), [BK0[lb], TSB], [QB])
                    DV(lambda e: e.tensor_tensor(out=wa_[:], in0=qa[:], in1=qb[:], op=ALU.add), [QA, QB], [WA_])
                    DV(lambda e: e.tensor_tensor(out=qa[:], in0=Lb, in1=TcQ, op=ALU.mult), [BK0[lb], TCB], [QA])
                    DV(lambda e: e.tensor_tensor(out=qb[:], in0=La, in1=TsQ, op=ALU.mult), [BK0[la], TSB], [QB])
                    DV(lambda e: e.tensor_tensor(out=wb_[:], in0=qa[:], in1=qb[:], op=ALU.subtract), [QA, QB], [WB_])
                    wa3 = wa_[:].rearrange("p (g c) -> p g c", g=8)
                    wb3 = wb_[:].rearrange("p (g c) -> p g c", g=8)
                    za3 = za_[:].rearrange("p (g c) -> p g c", g=8)
                    zb3 = zb_[:].rearrange("p (g c) -> p g c", g=8)
                    sa3 = sa[:].rearrange("p (g c) -> p g c", g=8)
                    DV(lambda e: e.tensor_tensor(out=t8a[:], in0=Rho[:, gs], in1=Sa_prev[:, gs], op=ALU.mult), [RHO, SAP], [T8A])
                    DV(lambda e: e.tensor_tensor(out=wa3[:, :, 0], in0=wa3[:, :, 0], in1=t8a[:], op=ALU.add), [WA_, T8A], [WA_])
                    DV(lambda e: e.tensor_tensor(out=t8b[:], in0=Rho[:, gs], in1=Sb_prev[:, gs], op=ALU.mult), [RHO, SBP], [T8B])
                    DV(lambda e: e.tensor_tensor(out=wb3[:, :, 0], in0=wb3[:, :, 0], in1=t8b[:], op=ALU.add), [WB_, T8B], [WB_])
                    DV(lambda e: e.tensor_tensor_scan(out=za_[:], data0=RQ, data1=wa_[:], initial=0.0, op0=ALU.mult, op1=ALU.add),
                       [RTB, WA_], [ZA_])
                    DV(lambda e: e.tensor_tensor_scan(out=zb_[:], data0=RQ, data1=wb_[:], initial=0.0, op0=ALU.mult, op1=ALU.add),
                       [RTB, WB_], [ZB_])
                    PL = lambda fn, r, w: kb.op(POOL, fn, reads=r, writes=w)
                    PL(lambda e: e.tensor_tensor(out=pe_[:], in0=za_[:], in1=TcQ, op=ALU.mult), [ZA_, TCB], [PEB])
                    PL(lambda e: e.tensor_tensor(out=pf_[:], in0=zb_[:], in1=TsQ, op=ALU.mult), [ZB_, TSB], [PFB])
                    PL(lambda e: e.tensor_tensor(out=sa[:], in0=pe_[:], in1=pf_[:], op=ALU.subtract), [PEB, PFB], [SA])
                    PL(lambda e: e.tensor_copy(out=Sbuf[:, gs, 0], in_=Sa_prev[:, gs]), [SAP], [SBUF[qd]])
                    PL(lambda e: e.tensor_copy(out=Sbuf[:, gs, 1:NC_], in_=sa3[:, :, 0:NC_ - 1]), [SA], [SBUF[qd]])
                    PL(lambda e: e.tensor_tensor(out=t8c[:], in0=zb3[:, :, NC_ - 1], in1=Tc[:, gs, NC_ - 1], op=ALU.mult),
                       [ZB_, TCB], [T8C])
                    PL(lambda e: e.tensor_tensor(out=t8d[:], in0=za3[:, :, NC_ - 1], in1=Ts[:, gs, NC_ - 1], op=ALU.mult),
                       [ZA_, TSB], [T8D])
                    PL(lambda e: e.tensor_tensor(out=Sb_prev[:, gs], in0=t8c[:], in1=t8d[:], op=ALU.add), [T8C, T8D], [SBP])
                    PL(lambda e: e.tensor_copy(out=Sa_prev[:, gs], in_=sa3[:, :, NC_ - 1]), [SA], [SAP])

                def stage_C(qd):
                    gs = slice(qd * 8, (qd + 1) * 8)
                    yb = 2 + qd % 2
                    for gl in range(8):
                        g = qd * 8 + gl
                        kb.op(PE, lambda e, g=g, gl=gl, yb=yb: e.matmul(Bk0[yb][:, gl * NC_:(gl + 1) * NC_], lhsT=M2[:, g, :],
                                                                        rhs=U8c[:, g, :], start=True, stop=False),
                              reads=[M2B, U8Bc[g // 16]], writes=[BK0[yb]])
                        kb.op(PE, lambda e, g=g, gl=gl, yb=yb: e.matmul(Bk0[yb][:, gl * NC_:(gl + 1) * NC_], lhsT=M3[:, g, :],
                                                                        rhs=Sbuf[:, g, :], start=False, stop=True),
                              reads=[M3B, SBUF[qd]], writes=[BK0[yb]])
                    kb.op(ACT, lambda e, yb=yb: e.activation(out=Y8g[:, gs, :].rearrange("p a b -> p (a b)"), in_=Bk0[yb][:],
                                                             func=AF.Gelu_apprx_tanh), reads=[BK0[yb]], writes=[Y8G[qd]])
                    tb = qd % 2
                    tpv = Bk0[tb][:].bitcast(BF16)
                    for gl in range(8):
                        g = qd * 8 + gl
                        kb.op(PE, lambda e, g=g, gl=gl, tpv=tpv: e.transpose(out=tpv[0:NC_, gl * 128:(gl + 1) * 128],
                                                                             in_=Y8g[:, g, :], identity=ident_b[:]),
                              reads=[Y8G[qd], IDB], writes=[BK0[tb]])
                    kb.op(ACT, lambda e, qd=qd, tpv=tpv: e.activation(
                        out=y8tm[:, :, qd * 128:(qd + 1) * 128].rearrange("p t (g h) -> p g t h", g=8),
                        in_=tpv[0:NC_, 0:1024].rearrange("p (g t h) -> p g t h", g=8, t=8), func=AF.Copy),
                          reads=[BK0[tb]], writes=[Y8TM[qd]])

                stage_A(0)
                stage_A(1)
                pull0()
                for qd in range(4):
                    stage_B(qd)
                    pull0()
                    stage_C(qd)
                    pull0()
                    if qd + 2 < 4:
                        stage_A(qd + 2)
                        pull0()
                for j in range(4):
                    tb = j % 2
                    tpv = Bk0[tb][:].bitcast(BF16)
                    for t8 in range(8):
                        kb.op(PE, lambda e, j=j, t8=t8, tpv=tpv: e.transpose(out=tpv[:, t8 * NC_:(t8 + 1) * NC_],
                                                                             in_=y8tm[:, t8, j * 128:(j + 1) * 128],
                                                                             identity=ident_b[0:NC_, 0:NC_]),
                              reads=[Y8TM[j], IDB], writes=[BK0[tb]])
                    kb.op(ACT, lambda e, j=j, tpv=tpv: e.activation(out=yT[:, j, :].rearrange("p (c t) -> p t c", t=8),
                                                                    in_=tpv[:, 0:T0].rearrange("p (t c) -> p t c", t=8),
                                                                    func=AF.Copy), reads=[BK0[tb]], writes=[YT])
                    pull0()
                def glu_mm(n):
                    za_, zb_ = 4 + n % 2, 6 + n % 2
                    for cc in range(4):
                        kb.op(PE, lambda e, cc=cc: e.matmul(Bk0[za_][:], lhsT=wglu[:, cc, n * 128:(n + 1) * 128],
                                                            rhs=yT[:, cc, :], start=(cc == 0), stop=(cc == 3)),
                              reads=[WGLU, YT], writes=[BK0[za_]])
                    for cc in range(4):
                        kb.op(PE, lambda e, cc=cc: e.matmul(Bk0[zb_][:], lhsT=wglu[:, cc, 512 + n * 128:512 + (n + 1) * 128],
                                                            rhs=yT[:, cc, :], start=(cc == 0), stop=(cc == 3)),
                              reads=[WGLU, YT], writes=[BK0[zb_]])

                glu_mm(0)
                glu_mm(1)
                for n in range(4):
                    za_, zb_ = 4 + n % 2, 6 + n % 2
                    sg_, SG_ = sg[n % 2], SG[n % 2]
                    gq_, GQ_ = gsq[n % 2], GSQ[n % 2]
                    kb.op(ACT, lambda e: e.activation(out=sg_[:], in_=Bk0[zb_][:], func=AF.Sigmoid), reads=[BK0[zb_]], writes=[SG_])
                    DV(lambda e, n=n: e.tensor_tensor(out=gT[:, n, :], in0=Bk0[za_][:], in1=sg_[:], op=ALU.mult),
                       [BK0[za_], SG_], [GT[n]])
                    kb.op(POOL, lambda e, n=n: e.tensor_tensor(out=gq_[:], in0=gT[:, n, :], in1=gT[:, n, :], op=ALU.mult),
                          reads=[GT[n]], writes=[GQ_])
                    if n + 2 < 4:
                        glu_mm(n + 2)
                    kb.op(PE, lambda e, n=n: e.matmul(Bk0[3][:], lhsT=ones0[:], rhs=gq_[:], start=(n == 0), stop=(n == 3)),
                          reads=[ONES0, GQ_], writes=[BK0[3]])
                    pull0()
                kb.op(ACT, lambda e: e.activation(out=rbc0[:], in_=Bk0[3][:], func=AF.Ln, bias=epsc[:, 0:1], scale=1.0 / 512),
                      reads=[BK0[3], EPSC], writes=[RBC0])
                kb.op(ACT, lambda e: e.activation(out=rbc0[:], in_=rbc0[:], func=AF.Exp, scale=-0.5), reads=[RBC0], writes=[RBC0])
                for n in range(4):
                    DV(lambda e, n=n: e.tensor_tensor(out=Gn0[:, n, :], in0=gT[:, n, :], in1=rbc0[:], op=ALU.mult),
                       [GT[n], RBC0], [GN0])
                kb.dma(SP, gnsc[:, tok0:tok0 + T0].rearrange("(c p) t -> p c t", p=128), Gn0[:], reads=[GN0], writes=[GNSC])
                while bg0["steps"]:
                    bg0["steps"].pop(0)()
        kb.pop()

    if dbg == "p0_dump":
        kb.push()
        gd = kb.sb([128, 4, 1024], BF16, "gd")
        gf = kb.sb([128, 4, 1024], F32, "gf")
        GD, GF = Buf(), Buf()
        for blk in range(ntok // 1024):
            kb.dma(SP, gd[:], gnsc[:, blk * 1024:(blk + 1) * 1024].rearrange("(c p) t -> p c t", p=128), reads=[GNSC], writes=[GD])
            kb.op(DVE, lambda e: e.tensor_copy(out=gf[:], in_=gd[:]), reads=[GD], writes=[GF])
            tok = kb.dma(SP, out[blk * 512:(blk + 1) * 512, :].rearrange("(c p) t -> p c t", p=128), gf[:], reads=[GF], writes=[OUTB])
            kb.out_tokens.append(tok)
        for tok in kb.out_tokens:
            kb._wait(SP, tok)
        kb.pop()
        return nc, kb

    if "p1" in phases:
        kb.push()
        T1 = 512
        nt1 = seq // T1
        NSUB = T1 // 128
        s1, s2, s3 = D_SSM, D_SSM + Q_LORA, D_SSM + Q_LORA + KV_LORA
        w_in_a = kb.sb([128, 8, 672], BF16, "w_in_a")
        w_rot = kb.sb([128, 8, 128], BF16, "w_kr")
        wuq = kb.sb([128, 3, NH * 128], BF16, "wuq_c")
        Kw = kb.sb([128, 2, 512], BF16, "Kw")
        Vw = kb.sb([128, 2, 512], BF16, "Vw")
        wout = kb.sb([128, 8, D], BF16, "wout")
        WINA, WROT, WUQ, KW, VW, WOUT = Buf(), Buf(), Buf(), Buf(), Buf(), Buf()
        kb.dma(POOL, w_in_a[:], w_in[:, s1:IN_COLS].rearrange("(k p) n -> p k n", p=128), writes=[WINA])
        kb.op(DVE, lambda e: e.memset(w_rot[:], 0.0), writes=[WROT])
        kb.dma(POOL, w_rot[:, :, 64:96], w_in[:, s3:s3 + 32].rearrange("(k p) n -> p k n", p=128), writes=[WROT])
        kb.dma(POOL, w_rot[:, :, 96:112], w_in[:, s3 + 16:s3 + 32].rearrange("(k p) n -> p k n", p=128), writes=[WROT])
        kb.dma(POOL, w_rot[:, :, 112:128], w_in[:, s3:s3 + 16].rearrange("(k p) n -> p k n", p=128), writes=[WROT])
        kb.op(DVE, lambda e: e.tensor_scalar(out=w_rot[:, :, 96:112], in0=w_rot[:, :, 96:112], scalar1=-1.0, scalar2=None,
                                             op0=ALU.mult), reads=[WROT], writes=[WROT])
        qg = kb.sb([128, 3], F32, "qg")
        kvg = kb.sb([128, 2], F32, "kvg")
        og = kb.sb([128, 8], F32, "og")
        n1g = kb.sb([128, 8], F32, "n1g")
        QG, KVG, OG, N1G = Buf(), Buf(), Buf(), Buf()
        load_featmajor(qg[:], q_norm_g[0:1, :], QG)
        load_featmajor(kvg[:], kv_norm_g[0:1, :], KVG)
        load_featmajor(og[:, 0:4], ssm_out_g[0:1, :], OG)
        load_featmajor(og[:, 4:8], attn_out_g[0:1, :], OG)
        load_featmajor(n1g[:], norm1_g[0:1, :], N1G)
        kb.push()
        stg = kb.sb([128, 3, 1024], F32, "stg")
        STG = Buf()
        QSCALE = (64 + 32) ** -0.5
        kb.dma(SP, stg[:, :, 0:768], w_uq[:, :].rearrange("(k p) n -> p k n", p=128), writes=[STG])
        wq4 = wuq[:].rearrange("p c (h d) -> p c h d", h=NH)
        st4 = stg[:, :, 0:768].rearrange("p c (h d) -> p c h d", h=NH)
        for cc in range(3):
            kb.op(DVE, lambda e, cc=cc: e.tensor_scalar(out=wq4[:, cc, :, 0:96], in0=st4[:, cc, :, :], scalar1=qg[:, cc:cc + 1],
                                                        scalar2=QSCALE, op0=ALU.mult, op1=ALU.mult),
                  reads=[STG, QG], writes=[WUQ])
            kb.op(DVE, lambda e, cc=cc: e.tensor_scalar(out=wq4[:, cc, :, 96:112], in0=st4[:, cc, :, 80:96],
                                                        scalar1=qg[:, cc:cc + 1], scalar2=-QSCALE, op0=ALU.mult, op1=ALU.mult),
                  reads=[STG, QG], writes=[WUQ])
            kb.op(DVE, lambda e, cc=cc: e.tensor_scalar(out=wq4[:, cc, :, 112:128], in0=st4[:, cc, :, 64:80],
                                                        scalar1=qg[:, cc:cc + 1], scalar2=QSCALE, op0=ALU.mult, op1=ALU.mult),
                  reads=[STG, QG], writes=[WUQ])
        kb.dma(SP, stg[:, 0:2, :], w_ukv[:, :].rearrange("(k p) n -> p k n", p=128), reads=[STG], writes=[STG])
        st5 = stg[:, 0:2, :].rearrange("p c (h t d) -> p c h t d", h=NH, t=2)
        for cc in range(2):
            kb.op(DVE, lambda e, cc=cc: e.tensor_scalar(out=Kw[:, cc, :].rearrange("p (h d) -> p h d", h=NH),
                                                        in0=st5[:, cc, :, 0, :], scalar1=kvg[:, cc:cc + 1], scalar2=None,
                                                        op0=ALU.mult), reads=[STG, KVG], writes=[KW])
            kb.op(DVE, lambda e, cc=cc: e.tensor_scalar(out=Vw[:, cc, :].rearrange("p (h d) -> p h d", h=NH),
                                                        in0=st5[:, cc, :, 1, :], scalar1=kvg[:, cc:cc + 1], scalar2=None,
                                                        op0=ALU.mult), reads=[STG, KVG], writes=[VW])
        kb.pop()
        Kc = kb.sb([96, NH, seq], BF16, "Kc")
        Vc = kb.sb([128, seq // 128, NH, 65], BF16, "Vc")
        KC = [Buf(f"kc{h}") for h in range(NH)]
        VC = [Buf(f"vc{j}") for j in range(seq // 128)]
        kb.op(POOL, lambda e: e.memset(Vc[:], 1.0), writes=VC)
        ones_b = kb.sb([128, 128], BF16, "ones_b")
        ones_f = kb.sb([128, 64], F32, "ones_f")
        tri = kb.sb([128, 128], BF16, "tri")
        ONES, TRI = Buf(), Buf()
        kb.op(DVE, lambda e: e.memset(ones_b[:], 1.0), writes=[ONES])
        kb.op(DVE, lambda e: e.memset(ones_f[:], 1.0), writes=[ONES])
        kb.dma(POOL, tri[:], tri_in[:, :], writes=[TRI])
        KCT = [[Buf(f"kc{h}_{i}") for i in range(nt1)] for h in range(NH)]
        xs_ = [kb.sb([128, D], F32, f"p1_xs{i}") for i in range(2)]
        XS = [Buf(), Buf()]
        scr = [kb.sb([128, 4, D], BF16, f"p1_scr{i}") for i in range(2)]
        SCR = [Buf(), Buf()]

        def qt_view(pp):
            return scr[pp][0:96, :, :].rearrange("p a b -> p (a b)")[:, 0:NH * T1].rearrange("p (h t) -> p h t", h=NH)
        hT = kb.sb([128, 8, T1], BF16, "p1_hT")
        HT = Buf()
        st1 = kb.sb([128, 8], F32, "p1_stat")
        SS1, RS1 = Buf(), Buf()
        sh1 = kb.sb([128, 8], F32, "p1_sh1")
        sc1 = kb.sb([128, 8], F32, "p1_sc1")
        G1 = kb.sb([128, 8], F32, "p1_G1")
        SH1, SC1, G1B = Buf(), Buf(), Buf()
        qnT = kb.sb([128, 3, T1], BF16, "p1_qnT")
        kvnT = kb.sb([128, 2, T1], BF16, "p1_kvnT")
        QNT, KVNT = Buf(), Buf()
        sq = kb.sb([128, T1], BF16, "p1_sq")
        SQ = Buf()
        rbc = kb.sb([128, T1], F32, "p1_rbc")
        RBC = Buf()
        junk1 = rbc[:].bitcast(BF16)
        JUNK1 = RBC
        cosT = kb.sb([96, T1], F32, "p1_cos")
        sinT = kb.sb([128, T1], F32, "p1_sin")
        COS, SIN = Buf(), Buf()
        t1 = kb.sb([96, T1], F32, "p1_t1")
        t2 = kb.sb([96, T1], F32, "p1_t2")
        T1B, T2B = Buf(), Buf()
        kr = kb.sb([96, T1], BF16, "p1_kr")
        KR = Buf()
        pt = [kb.sb([128, T1], BF16, f"p1_pt{i}") for i in range(4)]
        PT = [Buf() for _ in range(4)]
        o_sb = [kb.sb([65, T1], F32, f"p1_osb{i}") for i in range(2)]
        OSB = [Buf(), Buf()]
        Ya = kb.sb([128, 4, T1], BF16, "p1_Ya")
        YA = [Buf() for _ in range(4)]
        Gn = kb.sb([128, 4, T1], BF16, "p1_Gn")
        GN = Buf()
        gate1 = Gn[:].rearrange("p a b -> p (a b)").bitcast(F32)
        wstg = hT[:].rearrange("p a b -> p (a b)").bitcast(F32)[:, 0:D]
        GATE1, WSTG = GN, HT
        B = [kb.ps([128, 512], F32, f"p1_bank{i}") for i in range(8)]
        BK = [Buf(f"bank{i}") for i in range(8)]
        FB = (6, 7)
        R = slice(64, 96)
        use_ssm = "p0" in phases

        def rope_combine(bk, dst_ap, DSTS):
            kb.op(DVE, lambda e: e.tensor_tensor(out=t1[R, :], in0=B[bk][R, :], in1=cosT[R, :], op=ALU.mult),
                  reads=[BK[bk], COS], writes=[T1B])
            kb.op(DVE, lambda e: e.tensor_tensor(out=t2[R, :], in0=B[bk][96:128, :], in1=sinT[96:128, :], op=ALU.mult),
                  reads=[BK[bk], SIN], writes=[T2B])
            kb.op(DVE, lambda e: e.tensor_tensor(out=dst_ap, in0=t1[R, :], in1=t2[R, :], op=ALU.add),
                  reads=[T1B, T2B], writes=DSTS)

        def latent_steps(col0, nchunk, dstT, DST, nlat):
            cb, sb_ = FB
            steps = []
            for cc in range(nchunk):
                def s_mm(cc=cc):
                    for k in range(8):
                        kb.op(PE, lambda e, k=k: e.matmul(
                            B[cb][:], lhsT=w_in_a[:, k, col0 + cc * 128:col0 + (cc + 1) * 128], rhs=hT[:, k, :],
                            start=(k == 0), stop=(k == 7)), reads=[WINA, HT], writes=[BK[cb]])
                def s_sq():
                    kb.op(ACT, lambda e: e.activation(out=sq[:], in_=B[cb][:], func=AF.Square), reads=[BK[cb]], writes=[SQ])
                def s_ones(cc=cc):
                    kb.op(PE, lambda e: e.matmul(B[sb_][:], lhsT=ones_b[:], rhs=sq[:], start=(cc == 0),
                                                 stop=(cc == nchunk - 1)), reads=[ONES, SQ], writes=[BK[sb_]])
                steps += [s_mm, s_sq, s_ones]
            def s_ln():
                kb.op(ACT, lambda e: e.activation(out=rbc[:], in_=B[sb_][:], func=AF.Ln, bias=epsc[:, 0:1], scale=1.0 / nlat),
                      reads=[BK[sb_], EPSC], writes=[RBC])
            def s_exp():
                kb.op(ACT, lambda e: e.activation(out=rbc[:], in_=rbc[:], func=AF.Exp, scale=-0.5), reads=[RBC], writes=[RBC])
            steps += [s_ln, s_exp]
            for cc in range(nchunk):
                bk = FB[cc % 2]
                def s_mm2(cc=cc, bk=bk):
                    for k in range(8):
                        kb.op(PE, lambda e, k=k: e.matmul(
                            B[bk][:], lhsT=w_in_a[:, k, col0 + cc * 128:col0 + (cc + 1) * 128], rhs=hT[:, k, :],
                            start=(k == 0), stop=(k == 7)), reads=[WINA, HT], writes=[BK[bk]])
                def s_scale(cc=cc, bk=bk):
                    kb.op(DVE, lambda e: e.tensor_tensor(out=dstT[:, cc, :], in0=B[bk][:], in1=rbc[:], op=ALU.mult),
                          reads=[BK[bk], RBC], writes=[DST])
                steps += [s_mm2, s_scale]
            return steps

        def front_steps(b, i):
            tok0 = b * seq + i * T1
            pp = i % 2
            xn1 = scr[pp]
            Qt = qt_view(pp)
            cols = slice(i * T1, (i + 1) * T1)
            steps = []

            def s_tables():
                kb.dma(SP, cosT[R, :], ropesc[0, :, tok0:tok0 + T1], reads=[ROPESC], writes=[COS])
                kb.dma(SP, sinT[96:128, :], ropesc[1, :, tok0:tok0 + T1], reads=[ROPESC], writes=[SIN])
                kb.op(DVE, lambda e: e.memset(st1[:, 0:4], 0.0), writes=[SS1])
            steps.append(s_tables)
            for su in range(NSUB):
                xb_ = su % 2
                def s_load(su=su, xb_=xb_):
                    kb.dma(SP, xs_[xb_][:], x[tok0 + su * 128:tok0 + (su + 1) * 128, :], writes=[XS[xb_]])
                def s_stat(su=su, xb_=xb_):
                    kb.op(ACT, lambda e: e.activation(out=junk1, in_=xs_[xb_][:], func=AF.Square,
                                                      accum_out=st1[:, su:su + 1]), reads=[XS[xb_]], writes=[JUNK1, SS1])
                    rsqrt_cols(st1[:, 4 + su:5 + su], st1[:, su:su + 1], 1.0 / D, SS1, RS1)
                def s_xn(su=su, xb_=xb_):
                    kb.op(DVE, lambda e: e.tensor_scalar(out=xn1[:, su, :], in0=xs_[xb_][:], scalar1=st1[:, 4 + su:5 + su],
                                                         scalar2=None, op0=ALU.mult), reads=[XS[xb_], RS1], writes=[SCR[pp]])
                steps += [s_load, s_stat, s_xn]
            for k in range(8):
                bk = FB[k % 2]
                def s_tr(k=k, bk=bk):
                    tpv = B[bk][:].bitcast(BF16)
                    for su in range(NSUB):
                        kb.op(PE, lambda e, su=su: e.transpose(out=tpv[:, su * 128:(su + 1) * 128],
                                                               in_=xn1[:, su, k * 128:(k + 1) * 128], identity=ident_b[:]),
                              reads=[SCR[pp], IDB], writes=[BK[bk]])
                def s_ev(k=k, bk=bk):
                    tpv = B[bk][:].bitcast(BF16)
                    kb.op(DVE, lambda e: e.tensor_scalar(out=hT[:, k, :], in0=tpv[:, 0:T1], scalar1=G1[:, k:k + 1],
                                                         scalar2=sh1[:, k:k + 1], op0=ALU.mult, op1=ALU.add),
                          reads=[BK[bk], SH1, G1B], writes=[HT])
                steps += [s_tr, s_ev]
            steps += latent_steps(0, 3, qnT, QNT, Q_LORA)
            steps += latent_steps(Q_LORA, 2, kvnT, KVNT, KV_LORA)

            def s_kr1():
                for k in range(8):
                    kb.op(PE, lambda e, k=k: e.matmul(B[FB[0]][:], lhsT=w_rot[:, k, :], rhs=hT[:, k, :],
                                                      start=(k == 0), stop=(k == 7)), reads=[WROT, HT], writes=[BK[FB[0]]])
            def s_kr3():
                rope_combine(FB[0], kr[R, :], [KR])
                kb.op(DVE, lambda e: e.tensor_copy(out=Kc[R, :, cols], in_=kr[R, :].unsqueeze(1).to_broadcast([32, NH, T1])),
                      reads=[KR], writes=[KCT[h][i] for h in range(NH)])
            steps += [s_kr1, s_kr3]
            for hp in range(4):
                bk = FB[hp % 2]
                def s_kmm(hp=hp, bk=bk):
                    for cc in range(2):
                        kb.op(PE, lambda e, cc=cc: e.matmul(B[bk][:], lhsT=Kw[:, cc, hp * 128:(hp + 1) * 128], rhs=kvnT[:, cc, :],
                                                            start=(cc == 0), stop=(cc == 1)), reads=[KW, KVNT], writes=[BK[bk]])
                def s_kev(hp=hp, bk=bk):
                    kb.op(DVE, lambda e: e.tensor_copy(out=Kc[0:64, 2 * hp, cols], in_=B[bk][0:64, :]),
                          reads=[BK[bk]], writes=[KCT[2 * hp][i]])
                    kb.op(DVE, lambda e: e.tensor_copy(out=Kc[0:64, 2 * hp + 1, cols], in_=B[bk][64:128, :]),
                          reads=[BK[bk]], writes=[KCT[2 * hp + 1][i]])
                steps += [s_kmm, s_kev]
            for su in range(NSUB):
                bk = FB[su % 2]
                blk = i * NSUB + su
                def s_vmm(su=su, bk=bk):
                    for cc in range(2):
                        kb.op(PE, lambda e, cc=cc: e.matmul(B[bk][:], lhsT=kvnT[:, cc, su * 128:(su + 1) * 128], rhs=Vw[:, cc, :],
                                                            start=(cc == 0), stop=(cc == 1)), reads=[KVNT, VW], writes=[BK[bk]])
                def s_vev(bk=bk, blk=blk):
                    kb.op(DVE, lambda e: e.tensor_copy(out=Vc[:, blk, :, 0:64], in_=B[bk][:].rearrange("p (h d) -> p h d", h=NH)),
                          reads=[BK[bk]], writes=[VC[blk]])
                steps += [s_vmm, s_vev]
            for h in range(NH):
                bk = FB[h % 2]
                def s_qmm(h=h, bk=bk):
                    for cc in range(3):
                        kb.op(PE, lambda e, cc=cc: e.matmul(B[bk][:], lhsT=wuq[:, cc, h * 128:(h + 1) * 128],
                                                            rhs=qnT[:, cc, :], start=(cc == 0), stop=(cc == 2)),
                              reads=[WUQ, QNT], writes=[BK[bk]])
                def s_qev(h=h, bk=bk):
                    kb.op(DVE, lambda e: e.tensor_copy(out=Qt[0:64, h, :], in_=B[bk][0:64, :]),
                          reads=[BK[bk]], writes=[SCR[pp]])
                    rope_combine(bk, Qt[R, h, :], [SCR[pp]])
                steps += [s_qmm, s_qev]
            return steps

        pending = []
        bg = {"urgent": [], "ublocks": 1, "steps": [], "blocks_left": 1}

        def pull_background():
            if bg["urgent"]:
                k = -(-len(bg["urgent"]) // max(bg["ublocks"], 1))
                for _ in range(k):
                    bg["urgent"].pop(0)()
                bg["ublocks"] -= 1
            else:
                n = len(bg["steps"])
                if n:
                    k = -(-n // max(bg["blocks_left"], 1))
                    for _ in range(k):
                        bg["steps"].pop(0)()
            bg["blocks_left"] -= 1

        def attention_head(i, h):
            pp = i % 2
            Qt = qt_view(pp)
            nblk = (i + 1) * NSUB
            ob = 3 + (h % 2)

            def emit_S(j):
                q0 = max(j - i * NSUB, 0) * 128
                sb_ = j % 3
                kb.op(PE, lambda e, j=j, q0=q0, sb_=sb_: e.matmul(
                    B[sb_][:, q0:T1], lhsT=Kc[0:96, h, j * 128:(j + 1) * 128], rhs=Qt[0:96, h, q0:T1],
                    start=True, stop=True), reads=[KCT[h][j // NSUB], SCR[pp]], writes=[BK[sb_]])

            for j in range(min(2, nblk)):
                emit_S(j)
            for j in range(nblk):
                jj = j - i * NSUB
                q0 = max(jj, 0) * 128
                sb_ = j % 3
                pb = j % 4
                kb.op(ACT, lambda e, q0=q0, sb_=sb_, pb=pb: e.activation(out=pt[pb][:, q0:T1], in_=B[sb_][:, q0:T1],
                                                                         func=AF.Exp),
                      reads=[BK[sb_]], writes=[PT[pb]])
                if jj >= 0:
                    kb.op(POOL, lambda e, q0=q0, pb=pb: e.tensor_tensor(out=pt[pb][:, q0:q0 + 128],
                                                                        in0=pt[pb][:, q0:q0 + 128], in1=tri[:],
                                                                        op=ALU.mult),
                          reads=[PT[pb], TRI], writes=[PT[pb]])
                if j + 2 < nblk:
                    emit_S(j + 2)
                kb.op(PE, lambda e, j=j, q0=q0, pb=pb: e.matmul(
                    B[ob][0:65, q0:T1], lhsT=Vc[:, j, h, :], rhs=pt[pb][:, q0:T1],
                    start=(j == 0), stop=(j == nblk - 1)), reads=[VC[j], PT[pb]], writes=[BK[ob]])
                if j == 1 and pending:
                    pending.pop()()
                pull_background()

            def epilogue():
                oi = h % 2
                kb.op(ACT, lambda e: e.activation(out=o_sb[oi][:], in_=B[ob][0:65, :], func=AF.Copy),
                      reads=[BK[ob]], writes=[OSB[oi]])
                kb.op(ACT, lambda e: e.activation(out=o_sb[oi][64:65, :], in_=o_sb[oi][64:65, :], func=AF.Ln),
                      reads=[OSB[oi]], writes=[OSB[oi]])
                kb.op(ACT, lambda e: e.activation(out=o_sb[oi][64:65, :], in_=o_sb[oi][64:65, :], func=AF.Exp, scale=-1.0),
                      reads=[OSB[oi]], writes=[OSB[oi]])
                kb.op(PE, lambda e: e.matmul(B[5][0:64, :], lhsT=ones_f[64:65, 0:64], rhs=o_sb[oi][64:65, :],
                                             start=True, stop=True), reads=[ONES, OSB[oi]], writes=[BK[5]])
                ro = (h % 2) * 64
                kb.op(DVE, lambda e: e.tensor_tensor(out=Ya[ro:ro + 64, h // 2, :], in0=o_sb[oi][0:64, :], in1=B[5][0:64, :],
                                                     op=ALU.mult), reads=[OSB[oi], BK[5]], writes=[YA[h // 2]])

            if pending:
                pending.pop()()
            pending.append(epilogue)

        def tail_steps(b, i):
            tok0 = b * seq + i * T1
            TB_ = FB[1]
            steps = []
            for cc in range(4):
                def s_sq(cc=cc):
                    kb.op(POOL, lambda e: e.tensor_tensor(out=sq[:], in0=Ya[:, cc, :], in1=Ya[:, cc, :], op=ALU.mult),
                          reads=[YA[cc]], writes=[SQ])
                def s_on(cc=cc):
                    kb.op(PE, lambda e: e.matmul(B[TB_][:], lhsT=ones_b[:], rhs=sq[:], start=(cc == 0), stop=(cc == 3)),
                          reads=[ONES, SQ], writes=[BK[TB_]])
                steps += [s_sq, s_on]
            def s_ln():
                kb.op(ACT, lambda e: e.activation(out=rbc[:], in_=B[TB_][:], func=AF.Ln, bias=epsc[:, 0:1], scale=1.0 / 512),
                      reads=[BK[TB_], EPSC], writes=[RBC])
            def s_exp():
                kb.op(ACT, lambda e: e.activation(out=rbc[:], in_=rbc[:], func=AF.Exp, scale=-0.5), reads=[RBC], writes=[RBC])
            def s_norm():
                for cc in range(4):
                    kb.op(DVE, lambda e, cc=cc: e.tensor_tensor(out=Ya[:, cc, :], in0=Ya[:, cc, :], in1=rbc[:], op=ALU.mult),
                          reads=[YA[cc], RBC], writes=[YA[cc]])
                if use_ssm:
                    kb.dma(SP, Gn[:], gnsc[:, tok0:tok0 + T1].rearrange("(c p) t -> p c t", p=128), reads=[GNSC], writes=[GN])
            steps += [s_ln, s_exp, s_norm]
            for su in range(NSUB):
                xb_ = su % 2
                def s_xl(su=su, xb_=xb_):
                    kb.dma(SP, xs_[xb_][:], x[tok0 + su * 128:tok0 + (su + 1) * 128, :], writes=[XS[xb_]])
                steps.append(s_xl)
                for hf in range(2):
                    ob_ = FB[hf]
                    def s_mm(su=su, hf=hf, xb_=xb_, ob_=ob_):
                        nmm = 8 if use_ssm else 4
                        n = 0
                        if use_ssm:
                            for cc in range(4):
                                kb.op(PE, lambda e, cc=cc, n=n: e.matmul(
                                    B[ob_][:], lhsT=Gn[:, cc, su * 128:(su + 1) * 128], rhs=wout[:, cc, hf * 512:(hf + 1) * 512],
                                    start=(n == 0), stop=False), reads=[GN, WOUT], writes=[BK[ob_]])
                                n += 1
                        for cc in range(4):
                            kb.op(PE, lambda e, cc=cc, n=n: e.matmul(
                                B[ob_][:], lhsT=Ya[:, cc, su * 128:(su + 1) * 128], rhs=wout[:, 4 + cc, hf * 512:(hf + 1) * 512],
                                start=(n == 0), stop=(n == nmm - 1)), reads=[YA[cc], WOUT], writes=[BK[ob_]])
                            n += 1
                        kb.op(DVE, lambda e: e.tensor_tensor(out=xs_[xb_][:, hf * 512:(hf + 1) * 512],
                                                             in0=xs_[xb_][:, hf * 512:(hf + 1) * 512], in1=B[ob_][:], op=ALU.add),
                              reads=[XS[xb_], BK[ob_]], writes=[XS[xb_]])
                    steps += [s_mm]
                def s_st(su=su, xb_=xb_):
                    tok = kb.dma(SP, out[tok0 + su * 128:tok0 + (su + 1) * 128, :], xs_[xb_][:], reads=[XS[xb_]], writes=[OUTB])
                    if "p2" not in phases:
                        kb.out_tokens.append(tok)
                steps.append(s_st)
            return steps

        for b in range(nseq):
            load_featmajor(sh1[:], modsc[b:b + 1, 0:D], SH1, extra_reads=[MODSC])
            load_featmajor(sc1[:], modsc[b:b + 1, D:2 * D], SC1, extra_reads=[MODSC])
            kb.op(DVE, lambda e: e.scalar_tensor_tensor(out=G1[:], in0=sc1[:], scalar=1.0, in1=n1g[:],
                                                        op0=ALU.add, op1=ALU.mult), reads=[SC1, N1G], writes=[G1B])
            kb.dma(SP, gate1, bcast_rows(modsc[b:b + 1, 2 * D:3 * D], 128), reads=[MODSC], writes=[GATE1])
            for kc in range(8):
                kb.dma(SP, wstg, w_out[kc * 128:(kc + 1) * 128, :], writes=[WSTG])
                kb.op(DVE, lambda e, kc=kc: e.scalar_tensor_tensor(out=wout[:, kc, :], in0=wstg, scalar=og[:, kc:kc + 1],
                                                                   in1=gate1, op0=ALU.mult, op1=ALU.mult),
                      reads=[WSTG, OG, GATE1], writes=[WOUT])
            for st_ in front_steps(b, 0):
                st_()
            for i in range(nt1):
                bg["urgent"] = tail_steps(b, i - 1) if i > 0 else []
                bg["ublocks"] = (i + 1) * NSUB
                bg["steps"] = front_steps(b, i + 1) if i + 1 < nt1 else []
                bg["blocks_left"] = NH * (i + 1) * NSUB
                for h in range(NH):
                    attention_head(i, h)
                    assert not bg["urgent"]
                while bg["steps"]:
                    bg["steps"].pop(0)()
                if pending:
                    pending.pop()()
            for st_ in tail_steps(b, nt1 - 1):
                st_()
        kb.pop()

    if "p2" in phases:
        kb.push()
        TT = 256
        nt2 = ntok // TT
        src_x1 = out if ("p1" in phases) else x
        wff1 = kb.sb([128, 8, DFF], BF16, "wff1")
        wff2 = kb.sb([128, 32, D], BF16, "wff2")
        WFF1 = [Buf(f"wff1_{k}") for k in range(8)]
        WFF2 = [Buf(f"wff2_{k}") for k in range(8)]
        for cb_ in range(8):
            kb.dma(POOL, wff1[:, :, cb_ * 512:(cb_ + 1) * 512],
                   w_ff1[:, cb_ * 512:(cb_ + 1) * 512].rearrange("(k p) n -> p k n", p=128), writes=[WFF1[cb_]])
        for k in range(8):
            kb.dma(POOL, wff2[:, 4 * k:4 * k + 4, :],
                   w_ff2[k * 512:(k + 1) * 512, :].rearrange("(k p) n -> p k n", p=128), writes=[WFF2[k]])
        xt = [kb.sb([128, 2, D], F32, f"p2_xt{i}") for i in range(2)]
        XT = [Buf("xt0"), Buf("xt1")]
        xn = kb.sb([128, 2, D], BF16, "p2_xn")
        XN = Buf("xn")
        junk = kb.sb([128, D], BF16, "p2_junk")
        JUNK = Buf("junk")
        h2T = kb.sb([128, 8, TT], BF16, "p2_h2T")
        H2T = Buf("h2T")
        hid = kb.sb([128, 32, TT], BF16, "p2_hid")
        HID = [Buf(f"hid{j}") for j in range(32)]
        tmp = kb.sb([128, 512], F32, "p2_tmp")
        TMP = Buf("tmp")
        stat = [kb.sb([128, 8], F32, f"p2_stat{i}") for i in range(2)]
        SS = [Buf(), Buf()]; RS = [Buf(), Buf()]; SS2 = [Buf(), Buf()]; RS2 = [Buf(), Buf()]
        g2n = kb.sb([128, 8], F32, "p2_g2n")
        G2N = Buf()
        load_featmajor(g2n[:], norm2_g[0:1, :], G2N)
        fng = kb.sb([128, D], F32, "p2_fng")
        FNG = Buf()
        kb.dma(SP, fng[:], bcast_rows(final_norm_g[0:1, :], 128), writes=[FNG])
        sc2 = kb.sb([128, 8], F32, "p2_sc2")
        SC2 = Buf()
        sh2 = [kb.sb([128, 8], F32, f"p2_sh2_{i}") for i in range(2)]
        G2 = [kb.sb([128, 8], F32, f"p2_G2_{i}") for i in range(2)]
        gate2 = [kb.sb([128, D], F32, f"p2_gate2_{i}") for i in range(2)]
        FG = [kb.sb([128, D], F32, f"p2_FG_{i}") for i in range(2)]
        fsh = [kb.sb([128, D], F32, f"p2_fsh_{i}") for i in range(2)]
        SH2 = [Buf(), Buf()]; G2B = [Buf(), Buf()]; GATE2 = [Buf(), Buf()]; FGB = [Buf(), Buf()]; FSH = [Buf(), Buf()]
        tp = [kb.ps([128, 1024], BF16, f"p2_tp{i}") for i in range(2)]
        TP = [Buf(), Buf()]
        ps_h = [kb.ps([128, 512], F32, f"p2_psh{i}") for i in range(3)]
        PSH = [Buf() for _ in range(3)]
        ps_o = [kb.ps([128, 512], F32, f"p2_pso{i}") for i in range(2)]
        PSO = [Buf(), Buf()]

        loaded_seq = set()

        def seq_consts(b):
            if b in loaded_seq:
                return
            loaded_seq.add(b)
            bi = b % 2
            load_featmajor(sh2[bi][:], modsc[b:b + 1, 3 * D:4 * D], SH2[bi], extra_reads=[MODSC])
            load_featmajor(sc2[:], modsc[b:b + 1, 4 * D:5 * D], SC2, extra_reads=[MODSC])
            kb.op(DVE, lambda e: e.scalar_tensor_tensor(out=G2[bi][:], in0=sc2[:], scalar=1.0, in1=g2n[:],
                                                        op0=ALU.add, op1=ALU.mult), reads=[SC2, G2N], writes=[G2B[bi]])
            kb.dma(SP, gate2[bi][:], bcast_rows(modsc[b:b + 1, 5 * D:6 * D], 128), reads=[MODSC], writes=[GATE2[bi]])
            kb.dma(SP, FG[bi][:], bcast_rows(modsc[b:b + 1, 7 * D:8 * D], 128), reads=[MODSC], writes=[FGB[bi]])
            kb.dma(SP, fsh[bi][:], bcast_rows(modsc[b:b + 1, 6 * D:7 * D], 128), reads=[MODSC], writes=[FSH[bi]])
            kb.op(DVE, lambda e: e.scalar_tensor_tensor(out=FG[bi][:], in0=FG[bi][:], scalar=1.0, in1=fng[:],
                                                        op0=ALU.add, op1=ALU.mult), reads=[FGB[bi], FNG], writes=[FGB[bi]])

        def prep_a(t):
            b = (t * TT) // seq
            xi = t % 2
            st = stat[xi]
            seq_consts(b)
            kb.dma(SP, xt[xi][:], src_x1[t * TT:(t + 1) * TT, :].rearrange("(s p) n -> p s n", p=128),
                   reads=[OUTB], writes=[XT[xi]])
            kb.op(DVE, lambda e: e.memset(st[:, 0:2], 0.0), writes=[SS[xi]])
            for s_ in range(2):
                kb.op(ACT, lambda e, s_=s_: e.activation(out=junk[:], in_=xt[xi][:, s_, :], func=AF.Square,
                                                         accum_out=st[:, s_:s_ + 1]),
                      reads=[XT[xi]], writes=[JUNK, SS[xi]])
            rsqrt_cols(st[:, 2:4], st[:, 0:2], 1.0 / D, SS[xi], RS[xi])
            for s_ in range(2):
                kb.op(DVE, lambda e, s_=s_: e.tensor_scalar(out=xn[:, s_, :], in0=xt[xi][:, s_, :],
                                                            scalar1=st[:, 2 + s_:3 + s_], scalar2=None, op0=ALU.mult),
                      reads=[XT[xi], RS[xi]], writes=[XN])

        def prep_b(t):
            bi = ((t * TT) // seq) % 2
            for k in range(8):
                pi = k % 2
                for s_ in range(2):
                    kb.op(PE, lambda e, s_=s_, k=k, pi=pi: e.transpose(out=tp[pi][:, s_ * 128:(s_ + 1) * 128],
                                                                       in_=xn[:, s_, k * 128:(k + 1) * 128],
                                                                       identity=ident_b[:]),
                          reads=[XN, IDB], writes=[TP[pi]])
                kb.op(ACT, lambda e, k=k, pi=pi: e.activation(out=h2T[:, k, :], in_=tp[pi][:, 0:TT], func=AF.Identity,
                                                              bias=sh2[bi][:, k:k + 1], scale=G2[bi][:, k:k + 1]),
                      reads=[TP[pi], SH2[bi], G2B[bi]], writes=[H2T])

        prep_a(0)
        prep_b(0)
        for t in range(nt2):
            b = (t * TT) // seq
            bi = b % 2
            xi = t % 2
            st = stat[xi]
            for jj in range(16):
                pj = jj % 3
                for j2 in range(2):
                    j = 2 * jj + j2
                    for k in range(8):
                        kb.op(PE, lambda e, j=j, j2=j2, k=k, pj=pj: e.matmul(
                            ps_h[pj][:, j2 * TT:(j2 + 1) * TT], lhsT=wff1[:, k, j * 128:(j + 1) * 128],
                            rhs=h2T[:, k, :], start=(k == 0), stop=(k == 7)),
                              reads=[H2T, WFF1[j // 4]], writes=[PSH[pj]])
                kb.op(ACT, lambda e, jj=jj, pj=pj: e.activation(out=hid[:, 2 * jj:2 * jj + 2, :], in_=ps_h[pj][:],
                                                                func=AF.Relu),
                      reads=[PSH[pj]], writes=[HID[2 * jj], HID[2 * jj + 1]])
                kb.op(DVE, lambda e, jj=jj: e.tensor_tensor(out=hid[:, 2 * jj:2 * jj + 2, :], in0=hid[:, 2 * jj:2 * jj + 2, :],
                                                            in1=hid[:, 2 * jj:2 * jj + 2, :], op=ALU.mult),
                      reads=[HID[2 * jj], HID[2 * jj + 1]], writes=[HID[2 * jj], HID[2 * jj + 1]])
                if jj == 3 and t + 1 < nt2:
                    prep_a(t + 1)
            if t + 1 < nt2:
                prep_b(t + 1)
            for s_ in range(2):
                for hf in range(2):
                    po = (s_ * 2 + hf) % 2
                    for k in range(32):
                        kb.op(PE, lambda e, s_=s_, hf=hf, k=k, po=po: e.matmul(
                            ps_o[po][:], lhsT=hid[:, k, s_ * 128:(s_ + 1) * 128], rhs=wff2[:, k, hf * 512:(hf + 1) * 512],
                            start=(k == 0), stop=(k == 31)),
                              reads=[HID[k], WFF2[k // 4]], writes=[PSO[po]])
                    kb.op(DVE, lambda e, hf=hf, po=po: e.tensor_tensor(out=tmp[:], in0=ps_o[po][:],
                                                                       in1=gate2[bi][:, hf * 512:(hf + 1) * 512], op=ALU.mult),
                          reads=[PSO[po], GATE2[bi]], writes=[TMP])
                    kb.op(DVE, lambda e, s_=s_, hf=hf: e.tensor_tensor(out=xt[xi][:, s_, hf * 512:(hf + 1) * 512],
                                                                       in0=xt[xi][:, s_, hf * 512:(hf + 1) * 512],
                                                                       in1=tmp[:], op=ALU.add),
                          reads=[TMP, XT[xi]], writes=[XT[xi]])
            kb.op(DVE, lambda e: e.memset(st[:, 4:6], 0.0), writes=[SS2[xi]])
            for s_ in range(2):
                kb.op(ACT, lambda e, s_=s_: e.activation(out=junk[:], in_=xt[xi][:, s_, :], func=AF.Square,
                                                         accum_out=st[:, 4 + s_:5 + s_]),
                      reads=[XT[xi]], writes=[JUNK, SS2[xi]])
            rsqrt_cols(st[:, 6:8], st[:, 4:6], 1.0 / D, SS2[xi], RS2[xi])
            for s_ in range(2):
                kb.op(DVE, lambda e, s_=s_: e.scalar_tensor_tensor(out=xt[xi][:, s_, :], in0=xt[xi][:, s_, :],
                                                                   scalar=st[:, 6 + s_:7 + s_], in1=FG[bi][:],
                                                                   op0=ALU.mult, op1=ALU.mult),
                      reads=[XT[xi], RS2[xi], FGB[bi]], writes=[XT[xi]])
                kb.op(DVE, lambda e, s_=s_: e.tensor_tensor(out=xt[xi][:, s_, :], in0=xt[xi][:, s_, :], in1=fsh[bi][:],
                                                            op=ALU.add),
                      reads=[XT[xi], FSH[bi]], writes=[XT[xi]])
            tok = kb.dma(SP, out[t * TT:(t + 1) * TT, :].rearrange("(s p) n -> p s n", p=128), xt[xi][:],
                         reads=[XT[xi]], writes=[OUTB])
            kb.out_tokens.append(tok)
        kb.pop()

    for tok in kb.out_tokens:
        kb._wait(SP, tok)
    return nc, kb


_NC_CACHE = {}


def _consts():
    inv_freq = 10000.0 ** (-np.arange(0, QK_ROPE, 2, dtype=np.float64) / QK_ROPE)
    invf = np.array([inv_freq[(r % 32) % 16] / (2.0 * np.pi) for r in range(128)], dtype=np.float32).reshape(128, 1)
    tri = np.triu(np.ones((128, 128), dtype=np.float32))
    kr = np.array(list(range(7, -1, -1)) + list(range(0, -8, -1)) + list(range(0, 8)) + list(range(1, 9)), dtype=np.float32)
    kr32 = np.tile(kr[None, :], (128, 1))
    cramp = np.tile(np.arange(1, 65, dtype=np.float32)[None, :], (128, 1))
    tau = np.arange(128) // 16
    mask8 = (tau[None, :] >= tau[:, None]).astype(np.float32)
    sgn = np.concatenate([np.ones(64), -np.ones(64)]).astype(np.float32).reshape(128, 1)
    return {"ident": np.eye(128, dtype=np.float32), "invf": invf, "tri": tri, "kr32": kr32, "cramp": cramp,
            "mask8": mask8, "sgn": sgn}


def kernel(**inputs):
    n = 8
    if "full" not in _NC_CACHE:
        _NC_CACHE["full"] = build_program()
    nc = _NC_CACHE["full"]
    in_maps = []
    for i in range(n):
        m = _core_inputs(inputs, i, NSEQ, SEQ)
        in_maps.append(m)
    res = run_bass_kernel_spmd(nc, in_maps, core_ids=list(range(n)))
    outs = [np.asarray(r["out"]).reshape(NSEQ, SEQ, D) for r in res.results]
    return np.concatenate(outs, axis=0).astype(np.float32)


def _core_inputs(inputs, i, nseq, seq):
    g = lambda k: np.ascontiguousarray(np.asarray(inputs[k]))
    sl = slice(i * nseq, (i + 1) * nseq)
    m = {
        "x": np.ascontiguousarray(g("x")[sl, :seq].reshape(nseq * seq, D)),
        "c": g("c")[sl],
        "positions": np.ascontiguousarray(g("positions")[sl, :seq]).astype(np.int32),
        "ada_w": g("ada_w")[0], "ada_b": g("ada_b").reshape(1, -1), "norm1_g": g("norm1_g").reshape(1, -1),
        "w_in": g("w_in")[0],
        "ssm_lambda_re": g("ssm_lambda_re")[0], "ssm_lambda_im": g("ssm_lambda_im")[0],
        "ssm_b_re": g("ssm_b_re")[0], "ssm_b_im": g("ssm_b_im")[0],
        "ssm_c_re": g("ssm_c_re")[0], "ssm_c_im": g("ssm_c_im")[0],
        "ssm_d": g("ssm_d")[0], "ssm_log_dt": g("ssm_log_dt").reshape(1, -1),
        "w_glu": g("w_glu")[0], "q_norm_g": g("q_norm_g").reshape(1, -1), "w_uq": g("w_uq")[0],
        "kv_norm_g": g("kv_norm_g").reshape(1, -1), "w_ukv": g("w_ukv")[0],
        "ssm_out_g": g("ssm_out_g").reshape(1, -1), "attn_out_g": g("attn_out_g").reshape(1, -1),
        "w_out": g("w_out")[0], "norm2_g": g("norm2_g").reshape(1, -1),
        "w_ff1": g("w_ff1")[0], "w_ff2": g("w_ff2")[0],
        "final_ada_w": g("final_ada_w"), "final_ada_b": g("final_ada_b").reshape(1, -1),
        "final_norm_g": g("final_norm_g").reshape(1, -1),
    }
    m.update(_consts())
    return m
```

```python
import contextlib
import math
import numpy as np
import ml_dtypes
import concourse.bass as bass
import concourse.mybir as mybir
from concourse.bass_utils import run_bass_kernel_spmd

F32 = mybir.dt.float32
BF16 = mybir.dt.bfloat16
I32 = mybir.dt.int32
AF = mybir.ActivationFunctionType
ALU = mybir.AluOpType
AX = mybir.AxisListType

D = 1024
SEQ = 4096
NSEQ = 2
DFF = 4096
EPS = 1e-6
D_SSM = 512
Q_LORA = 384
KV_LORA = 256
QK_ROPE = 32
IN_COLS = 1184
NH = 8


class Buf:
    __slots__ = ("w", "r", "sem", "semcnt", "name")

    def __init__(self, name=""):
        self.w = None
        self.r = []
        self.sem = None
        self.semcnt = 0
        self.name = name


class Eng:
    def __init__(self, nc, name, handle, needed=None):
        self.name = name
        self.h = handle
        self.sem = nc.semaphore("prog_" + name).__enter__()
        self.cnt = 0
        self.incs = 0
        self.waited = {}
        self.needed = needed
        self.used = set()
        self.val = {}


class KB:
    def __init__(self, nc, needed=None):
        self.nc = nc
        nd = needed or {}
        self.PE = Eng(nc, "pe", nc.tensor, nd.get("pe"))
        self.ACT = Eng(nc, "act", nc.scalar, nd.get("act"))
        self.DVE = Eng(nc, "dve", nc.vector, nd.get("dve"))
        self.POOL = Eng(nc, "pool", nc.gpsimd, nd.get("pool"))
        self.SP = Eng(nc, "sp", nc.sync, nd.get("sp"))
        self.nsb = 0
        self.nps = 0
        self.out_tokens = []
        self.stacks = [contextlib.ExitStack()]
        self.dma_bufs = []

    def sb(self, shape, dt, name=None):
        self.nsb += 1
        return self.stacks[-1].enter_context(self.nc.sbuf_tensor(f"{name or 'sb'}_{self.nsb}", list(shape), dt))

    def ps(self, shape, dt, name=None):
        self.nps += 1
        return self.stacks[-1].enter_context(self.nc.psum_tensor(f"{name or 'ps'}_{self.nps}", list(shape), dt))

    def push(self):
        self.stacks.append(contextlib.ExitStack())

    def pop(self):
        engs = [self.PE, self.ACT, self.DVE, self.POOL, self.SP]
        for e in engs:
            for o in engs:
                if o.cnt > 0 and (o is not e or e is not self.PE):
                    self._wait(e, (o.sem, o.cnt, o))
            for b in self.dma_bufs:
                self._wait(e, (b.sem, b.semcnt, None))
        self.stacks.pop().close()

    def _wait(self, eng, tok):
        sem, val, owner = tok
        key = id(sem)
        if eng.waited.get(key, 0) >= val:
            return
        eng.waited[key] = val
        if owner is not None:
            owner.used.add(val)
            eng.h.wait_ge(sem, owner.val[val])
        else:
            eng.h.wait_ge(sem, val)

    def _deps(self, eng, reads, writes):
        for b in reads:
            if b.w is not None:
                if b.w[2] is eng and eng is self.PE:
                    continue
                self._wait(eng, b.w)
        for b in writes:
            if b.w is not None and b.w[2] is not eng:
                self._wait(eng, b.w)
            for t in b.r:
                if t[2] is not eng:
                    self._wait(eng, t)

    def op(self, eng, fn, reads=(), writes=()):
        self._deps(eng, reads, writes)
        ins = fn(eng.h)
        eng.cnt += 1
        if eng.needed is None or eng.cnt in eng.needed:
            eng.incs += 1
            ins.then_inc(eng.sem, 1)
            eng.val[eng.cnt] = eng.incs
        tok = (eng.sem, eng.cnt, eng)
        for b in reads:
            b.r.append(tok)
        for b in writes:
            b.w = tok
            b.r = []
        return tok

    def dma(self, eng, out, in_, reads=(), writes=(), track=None, **kw):
        self._deps(eng, reads, writes)
        tb = track or (writes[0] if writes else reads[0])
        if tb.sem is None:
            tb.sem = self.nc.semaphore("dma_" + str(id(tb))).__enter__()
            self.dma_bufs.append(tb)
        ins = eng.h.dma_start(out=out, in_=in_, **kw)
        tb.semcnt += 16
        ins.then_inc(tb.sem, 16)
        tok = (tb.sem, tb.semcnt, None)
        for b in reads:
            b.r.append(tok)
        for b in writes:
            b.w = tok
            b.r = []
        return tok


def bcast_rows(ap, n):
    return ap.partition_broadcast(n)


def build_program(nseq=NSEQ, seq=SEQ, phases=("p0", "p1", "p2"), dbg=None):
    _, kb1 = _build_once(nseq, seq, phases, dbg, None)
    needed = {e.name: set(e.used) for e in (kb1.PE, kb1.ACT, kb1.DVE, kb1.POOL, kb1.SP)}
    nc, _ = _build_once(nseq, seq, phases, dbg, needed)
    return nc


def _build_once(nseq, seq, phases, dbg, needed):
    nc = bass.Bass("TRN2", target_bir_lowering=False)
    ntok = nseq * seq

    def dram_in(name, shape, dt=F32):
        return nc.dram_tensor(name, list(shape), dt, kind="ExternalInput").ap()

    x = dram_in("x", [ntok, D])
    c = dram_in("c", [nseq, D])
    positions = dram_in("positions", [nseq, seq], I32)
    ada_w = dram_in("ada_w", [D, 6 * D])
    ada_b = dram_in("ada_b", [1, 6 * D])
    norm1_g = dram_in("norm1_g", [1, D])
    w_in = dram_in("w_in", [D, IN_COLS])
    lam_re = dram_in("ssm_lambda_re", [32, 64])
    lam_im = dram_in("ssm_lambda_im", [32, 64])
    b_re = dram_in("ssm_b_re", [32, 64, 16])
    b_im = dram_in("ssm_b_im", [32, 64, 16])
    c_re = dram_in("ssm_c_re", [32, 16, 64])
    c_im = dram_in("ssm_c_im", [32, 16, 64])
    ssm_d = dram_in("ssm_d", [32, 16])
    log_dt = dram_in("ssm_log_dt", [1, 32])
    w_glu = dram_in("w_glu", [D_SSM, 2 * D_SSM])
    q_norm_g = dram_in("q_norm_g", [1, Q_LORA])
    w_uq = dram_in("w_uq", [Q_LORA, 768])
    kv_norm_g = dram_in("kv_norm_g", [1, KV_LORA])
    w_ukv = dram_in("w_ukv", [KV_LORA, 1024])
    ssm_out_g = dram_in("ssm_out_g", [1, 512])
    attn_out_g = dram_in("attn_out_g", [1, 512])
    w_out = dram_in("w_out", [D, D])
    norm2_g = dram_in("norm2_g", [1, D])
    w_ff1 = dram_in("w_ff1", [D, DFF])
    w_ff2 = dram_in("w_ff2", [DFF, D])
    final_ada_w = dram_in("final_ada_w", [D, 2 * D])
    final_ada_b = dram_in("final_ada_b", [1, 2 * D])
    final_norm_g = dram_in("final_norm_g", [1, D])
    ident_in = dram_in("ident", [128, 128])
    invf_in = dram_in("invf", [128, 1])
    tri_in = dram_in("tri", [128, 128])
    kr32_in = dram_in("kr32", [128, 32])
    cramp_in = dram_in("cramp", [128, 64])
    mask8_in = dram_in("mask8", [128, 128])
    sgn_in = dram_in("sgn", [128, 1])

    out = nc.dram_tensor("out", [ntok, D], F32, kind="ExternalOutput").ap()
    modsc = nc.dram_tensor("modsc", [nseq, 8 * D], F32, kind="Internal").ap()
    ropesc = nc.dram_tensor("ropesc", [2, 32, ntok], F32, kind="Internal").ap()
    gnsc = nc.dram_tensor("gnsc", [D_SSM, ntok], BF16, kind="Internal").ap()
    ROPESC, GNSC = Buf("ropesc"), Buf("gnsc")
    MODSC = Buf("modsc")
    OUTB = Buf("out_hbm")

    kb = KB(nc, needed)
    PE, ACT, DVE, POOL, SP = kb.PE, kb.ACT, kb.DVE, kb.POOL, kb.SP

    ident_f = kb.sb([128, 128], F32, "ident_f")
    ident_b = kb.sb([128, 128], BF16, "ident_b")
    IDF, IDB = Buf("idf"), Buf("idb")
    kb.dma(SP, ident_f[:], ident_in[:, :], writes=[IDF])
    kb.op(DVE, lambda e: e.tensor_copy(out=ident_b[:], in_=ident_f[:]), reads=[IDF], writes=[IDB])

    epsc = kb.sb([128, 1], F32, "epsc")
    condTb = kb.sb([128, 8, nseq], BF16, "condTb")
    EPSC = Buf("eps")
    kb.op(DVE, lambda e: e.memset(epsc[:], EPS), writes=[EPSC])

    kb.push()
    cT = kb.sb([128, 8, nseq], F32, "cT")
    CT = Buf("cT")
    for k in range(8):
        kb.dma(SP, cT[:, k, :], c[:, k * 128:(k + 1) * 128].rearrange("b p -> p b"), writes=[CT],
               allow_slow_non_contiguous=True)
    condT = kb.sb([128, 8, nseq], F32, "condT")
    COND = Buf("cond")
    kb.op(ACT, lambda e: e.activation(out=condT[:], in_=cT[:], func=AF.Silu), reads=[CT], writes=[COND])

    NAW = 8
    aw = [kb.sb([128, 8, 512], BF16, f"aw{i}") for i in range(NAW)]
    CONDB = Buf()
    kb.op(DVE, lambda e: e.tensor_copy(out=condTb[:], in_=condT[:]), reads=[COND], writes=[CONDB])
    AW = [Buf(f"aw{i}") for i in range(NAW)]
    brow = [kb.sb([nseq, 512], F32, f"brow{i}") for i in range(NAW)]
    BROW = [Buf() for _ in range(NAW)]
    mrow = [kb.sb([nseq, 512], F32, f"mrow{i}") for i in range(NAW)]
    MROW = [Buf() for _ in range(NAW)]
    ps_mod = [kb.ps([128, 512], F32, f"ps_mod{i}") for i in range(NAW)]
    PSM = [Buf() for _ in range(NAW)]
    pieces = [(ada_w, ada_b, i * 512, i * 512) for i in range(12)] + \
             [(final_ada_w, final_ada_b, i * 512, 6 * D + i * 512) for i in range(4)]
    def load_piece(pi):
        wsrc, bsrc, coff, doff = pieces[pi]
        i = pi % NAW
        for kh in range(2):
            kb.dma(POOL, aw[i][:, 4 * kh:4 * kh + 4, :],
                   wsrc[512 * kh:512 * (kh + 1), coff:coff + 512].rearrange("(k p) n -> p k n", p=128), writes=[AW[i]])

    deferred = list(range(4, 16)) if "p0" in phases else []
    early = [pi for pi in range(16) if pi not in deferred]
    for pi in early[:NAW]:
        load_piece(pi)
    if "p1" in phases:
        kb.push()
        RC = min(2048, seq)
        invf = kb.sb([96, 1], F32, "invf")
        INVF = Buf()
        kb.dma(SP, invf[64:96, :], invf_in[64:96, :], writes=[INVF])
        posi = kb.sb([96, RC], I32, "posi")
        posf = kb.sb([96, RC], F32, "posf")
        yy = kb.sb([96, RC], F32, "rope_y")
        yi = kb.sb([96, RC], I32, "rope_yi")
        yf = kb.sb([96, RC], F32, "rope_yf")
        tab = kb.sb([96, RC], F32, "rope_tab")
        POSI, POSF, YY, YI, YF, TAB = Buf(), Buf(), Buf(), Buf(), Buf(), Buf()
        R = slice(64, 96)
        for b in range(nseq):
            for c0 in range(0, seq, RC):
                kb.dma(SP, posi[R, :], positions[b:b + 1, c0:c0 + RC].partition_broadcast(32), writes=[POSI])
                kb.op(DVE, lambda e: e.tensor_copy(out=posf[R, :], in_=posi[R, :]), reads=[POSI], writes=[POSF])
                for which, off in ((1, 0.0), (0, 0.25)):
                    kb.op(DVE, lambda e, off=off: e.tensor_scalar(out=yy[R, :], in0=posf[R, :], scalar1=invf[R, 0:1],
                                                                  scalar2=off, op0=ALU.mult, op1=ALU.add),
                          reads=[POSF, INVF], writes=[YY])
                    kb.op(DVE, lambda e: e.tensor_copy(out=yi[R, :], in_=yy[R, :]), reads=[YY], writes=[YI])
                    kb.op(DVE, lambda e: e.tensor_copy(out=yf[R, :], in_=yi[R, :]), reads=[YI], writes=[YF])
                    kb.op(DVE, lambda e: e.tensor_tensor(out=yy[R, :], in0=yy[R, :], in1=yf[R, :], op=ALU.subtract),
                          reads=[YY, YF], writes=[YY])
                    kb.op(ACT, lambda e: e.activation(out=tab[R, :], in_=yy[R, :], func=AF.Sin, scale=2.0 * math.pi),
                          reads=[YY], writes=[TAB])
                    kb.dma(SP, ropesc[which, :, b * seq + c0:b * seq + c0 + RC], tab[R, :], reads=[TAB], writes=[ROPESC])
        kb.pop()


    for pi in early:
        wsrc, bsrc, coff, doff = pieces[pi]
        i = pi % NAW
        if early.index(pi) >= NAW:
            load_piece(pi)
        kb.dma(SP, brow[i][:], bcast_rows(bsrc[0:1, coff:coff + 512], nseq), writes=[BROW[i]])
        for k in range(8):
            kb.op(PE, lambda e, k=k, i=i: e.matmul(ps_mod[i][0:nseq, :], lhsT=condTb[:, k, :], rhs=aw[i][:, k, :],
                                                    start=(k == 0), stop=(k == 7)),
                  reads=[CONDB, AW[i]], writes=[PSM[i]])
        kb.op(DVE, lambda e, i=i: e.tensor_tensor(out=mrow[i][:], in0=ps_mod[i][0:nseq, :], in1=brow[i][:], op=ALU.add),
              reads=[PSM[i], BROW[i]], writes=[MROW[i]])
        kb.dma(SP, modsc[:, doff:doff + 512], mrow[i][:], reads=[MROW[i]], writes=[MODSC])

    kb.pop()

    def rsqrt_cols(dst, src, inv_n, SRC, DST):
        kb.op(ACT, lambda e: e.activation(out=dst, in_=src, func=AF.Ln, bias=epsc[:, 0:1], scale=inv_n),
              reads=[SRC, EPSC], writes=[DST])
        kb.op(ACT, lambda e: e.activation(out=dst, in_=dst, func=AF.Exp, scale=-0.5), reads=[DST], writes=[DST])

    if dbg == "setup":
        kb.push()
        t_ = kb.sb([nseq, 8 * D], F32, "dbgt")
        T_ = Buf()
        kb.dma(SP, t_[:], modsc[:, :], reads=[MODSC], writes=[T_])
        for r in range(nseq):
            tok = kb.dma(SP, out[r:r + 1, :].rearrange("o (a n) -> (o a) n", a=1), t_[r:r + 1, 0:D], reads=[T_], writes=[OUTB])
            kb.out_tokens.append(tok)
            tok = kb.dma(SP, out[nseq + r:nseq + r + 1, :], t_[r:r + 1, 7 * D:8 * D], reads=[T_], writes=[OUTB])
            kb.out_tokens.append(tok)
        for tok in kb.out_tokens:
            kb._wait(SP, tok)
        kb.pop()
        return nc, kb

    def load_featmajor(dst_ap, src_row_ap, dstbuf, extra_reads=()):
        kb.dma(SP, dst_ap, src_row_ap.rearrange("o (k p) -> p (o k)", p=128), reads=list(extra_reads), writes=[dstbuf],
               allow_slow_non_contiguous=True)


    if "p0" in phases:
        kb.push()
        T0 = 512
        nt0 = seq // T0
        NC_ = T0 // 8
        TWO_PI = 2.0 * math.pi
        M1a = kb.sb([128, 32, 128], BF16, "M1a")
        M1b = kb.sb([128, 32, 128], BF16, "M1b")
        M2 = kb.sb([128, 32, 128], BF16, "M2")
        M3 = kb.sb([128, 32, 128], BF16, "M3")
        Tc = kb.sb([128, 32, NC_], F32, "Tc")
        Ts = kb.sb([128, 32, NC_], F32, "Ts")
        Rt = kb.sb([128, 32, NC_], F32, "Rt")
        Rho = kb.sb([128, 32], F32, "Rho")
        M1A, M1B, M2B, M3B, TCB, TSB, RTB, RHO = (Buf() for _ in range(8))
        Bk0 = [kb.ps([128, 512], F32, f"p0_bank{i}") for i in range(8)]
        BK0 = [Buf(f"p0bank{i}") for i in range(8)]

        kb.push()
        kr32 = kb.sb([128, 32], F32, "kr32")
        cramp = kb.sb([128, NC_], F32, "cramp")
        mask8 = kb.sb([128, 128], F32, "mask8")
        sgn = kb.sb([128, 1], F32, "sgn")
        KR32, CRAMP, MASK8, SGN = Buf(), Buf(), Buf(), Buf()
        kb.dma(SP, kr32[:], kr32_in[:, :], writes=[KR32])
        kb.dma(SP, cramp[:], cramp_in[:, 0:NC_], writes=[CRAMP])
        kb.dma(SP, mask8[:], mask8_in[:, :], writes=[MASK8])
        kb.dma(SP, sgn[:], sgn_in[:, :], writes=[SGN])

        def DV(fn, reads, writes):
            return kb.op(DVE, fn, reads=reads, writes=writes)

        def frac_(t_ap, shape, TB):
            kb.push()
            ti = kb.sb(shape, I32, "frac_i")
            tf = kb.sb(shape, F32, "frac_f")
            TI, TF = Buf(), Buf()
            DV(lambda e: e.tensor_copy(out=ti[:], in_=t_ap), [TB], [TI])
            DV(lambda e: e.tensor_copy(out=tf[:], in_=ti[:]), [TI], [TF])
            DV(lambda e: e.tensor_tensor(out=t_ap, in0=t_ap, in1=tf[:], op=ALU.subtract), [TB, TF], [TB])
            kb.pop()

        lam2 = kb.sb([32, 2, 128], F32, "lam2")
        LAM2 = Buf()
        for ri, src in enumerate((lam_re, lam_im)):
            for du in range(2):
                kb.dma(SP, lam2[:, ri, du * 64:(du + 1) * 64], src[:, :], writes=[LAM2])
        lamre2 = kb.sb([128, 32], F32, "lamre2")
        lamim2 = kb.sb([128, 32], F32, "lamim2")
        LRE, LIM = Buf(), Buf()
        for ri, (dst, DB) in enumerate(((lamre2, LRE), (lamim2, LIM))):
            kb.op(PE, lambda e, ri=ri: e.transpose(out=Bk0[ri][:, 0:32], in_=lam2[:, ri, :], identity=ident_f[0:32, 0:32]),
                  reads=[LAM2, IDF], writes=[BK0[ri]])
            DV(lambda e, ri=ri, dst=dst: e.tensor_copy(out=dst[:], in_=Bk0[ri][:, 0:32]), [BK0[ri]], [DB])
        dt2 = kb.sb([128, 32], F32, "dt2")
        DT2 = Buf()
        kb.dma(SP, dt2[:], log_dt[0:1, :].partition_broadcast(128), writes=[DT2])
        kb.op(ACT, lambda e: e.activation(out=dt2[:], in_=dt2[:], func=AF.Exp), reads=[DT2], writes=[DT2])
        th = kb.sb([128, 32], F32, "th")
        ld = kb.sb([128, 32], F32, "ld")
        TH, LD = Buf(), Buf()
        DV(lambda e: e.scalar_tensor_tensor(out=th[:], in0=lamim2[:], scalar=1.0 / TWO_PI, in1=dt2[:], op0=ALU.mult,
                                            op1=ALU.mult), [LIM, DT2], [TH])
        DV(lambda e: e.tensor_tensor(out=ld[:], in0=lamre2[:], in1=dt2[:], op=ALU.mult), [LRE, DT2], [LD])

        def powers(ramp_ap, nk, Wre, Wim, WRE, WIM, base_th, BTH, base_ld, BLD, RAMPB):
            shp = [128, 32, nk]
            kb.push()
            y = kb.sb(shp, F32, "pw_y")
            yc = kb.sb(shp, F32, "pw_yc")
            mg = kb.sb(shp, F32, "pw_mg")
            Y, YC, MG = Buf(), Buf(), Buf()
            thb = base_th.unsqueeze(2).to_broadcast(shp)
            ldb = base_ld.unsqueeze(2).to_broadcast(shp)
            rb = ramp_ap.unsqueeze(1).to_broadcast(shp)
            DV(lambda e: e.tensor_tensor(out=y[:], in0=thb, in1=rb, op=ALU.mult), [BTH, RAMPB], [Y])
            frac_(y[:], shp, Y)
            DV(lambda e: e.tensor_scalar(out=yc[:], in0=y[:], scalar1=0.25, scalar2=None, op0=ALU.add), [Y], [YC])
            frac_(yc[:], shp, YC)
            DV(lambda e: e.tensor_tensor(out=mg[:], in0=ldb, in1=rb, op=ALU.mult), [BLD, RAMPB], [MG])
            kb.op(ACT, lambda e: e.activation(out=mg[:], in_=mg[:], func=AF.Exp), reads=[MG], writes=[MG])
            kb.op(ACT, lambda e: e.activation(out=y[:], in_=y[:], func=AF.Sin, scale=TWO_PI), reads=[Y], writes=[Y])
            kb.op(ACT, lambda e: e.activation(out=yc[:], in_=yc[:], func=AF.Sin, scale=TWO_PI), reads=[YC], writes=[YC])
            DV(lambda e: e.tensor_tensor(out=Wre[:], in0=mg[:], in1=yc[:], op=ALU.mult), [MG, YC], [WRE])
            DV(lambda e: e.tensor_tensor(out=Wim[:], in0=mg[:], in1=y[:], op=ALU.mult), [MG, Y], [WIM])
            kb.pop()

        Wre = kb.sb([128, 32, 32], F32, "Wre")
        Wim = kb.sb([128, 32, 32], F32, "Wim")
        WRE, WIM = Buf(), Buf()
        powers(kr32[:], 32, Wre, Wim, WRE, WIM, th[:], TH, ld[:], LD, KR32)
        th8 = kb.sb([128, 32], F32, "th8")
        ld8 = kb.sb([128, 32], F32, "ld8")
        TH8, LD8 = Buf(), Buf()
        DV(lambda e: e.tensor_scalar(out=th8[:], in0=th[:], scalar1=8.0, scalar2=None, op0=ALU.mult), [TH], [TH8])
        frac_(th8[:], [128, 32], TH8)
        DV(lambda e: e.tensor_scalar(out=ld8[:], in0=ld[:], scalar1=8.0, scalar2=None, op0=ALU.mult), [LD], [LD8])
        kb.op(ACT, lambda e: e.activation(out=Rho[:], in_=ld8[:], func=AF.Exp), reads=[LD8], writes=[RHO])
        shpT = [128, 32, NC_]
        kb.push()
        ty = kb.sb(shpT, F32, "ty")
        TY = Buf()
        DV(lambda e: e.tensor_tensor(out=ty[:], in0=th8[:].unsqueeze(2).to_broadcast(shpT),
                                     in1=cramp[:].unsqueeze(1).to_broadcast(shpT), op=ALU.mult), [TH8, CRAMP], [TY])
        frac_(ty[:], shpT, TY)
        kb.op(ACT, lambda e: e.activation(out=Ts[:], in_=ty[:], func=AF.Sin, scale=TWO_PI), reads=[TY], writes=[TSB])
        DV(lambda e: e.tensor_scalar(out=Ts[:], in0=Ts[:], scalar1=sgn[:, 0:1], scalar2=None, op0=ALU.mult), [TSB, SGN], [TSB])
        DV(lambda e: e.tensor_scalar(out=ty[:], in0=ty[:], scalar1=0.25, scalar2=None, op0=ALU.add), [TY], [TY])
        frac_(ty[:], shpT, TY)
        kb.op(ACT, lambda e: e.activation(out=Tc[:], in_=ty[:], func=AF.Sin, scale=TWO_PI), reads=[TY], writes=[TCB])
        kb.pop()
        DV(lambda e: e.memset(Rt[:], 0.0), [], [RTB])
        DV(lambda e: e.tensor_copy(out=Rt[:, :, 1:NC_], in_=Rho[:].unsqueeze(2).to_broadcast([128, 32, NC_ - 1])),
           [RHO], [RTB])
        def small(name):
            return kb.sb([128, 32], F32, name), Buf()
        lr, LR = small("lr"); nre, NRE = small("nre"); nim, NIM = small("nim"); den, DEN = small("den")
        tq, TQ = small("tq"); kre, KRE = small("kre"); kim, KIM = small("kim")
        DV(lambda e: e.tensor_scalar(out=lr[:], in0=Wre[:, :, 17], scalar1=-1.0, scalar2=None, op0=ALU.add), [WRE], [LR])
        DV(lambda e: e.tensor_tensor(out=nre[:], in0=lr[:], in1=lamre2[:], op=ALU.mult), [LR, LRE], [NRE])
        DV(lambda e: e.tensor_tensor(out=tq[:], in0=Wim[:, :, 17], in1=lamim2[:], op=ALU.mult), [WIM, LIM], [TQ])
        DV(lambda e: e.tensor_tensor(out=nre[:], in0=nre[:], in1=tq[:], op=ALU.add), [NRE, TQ], [NRE])
        DV(lambda e: e.tensor_tensor(out=nim[:], in0=Wim[:, :, 17], in1=lamre2[:], op=ALU.mult), [WIM, LRE], [NIM])
        DV(lambda e: e.tensor_tensor(out=tq[:], in0=lr[:], in1=lamim2[:], op=ALU.mult), [LR, LIM], [TQ])
        DV(lambda e: e.tensor_tensor(out=nim[:], in0=nim[:], in1=tq[:], op=ALU.subtract), [NIM, TQ], [NIM])
        DV(lambda e: e.tensor_tensor(out=den[:], in0=lamre2[:], in1=lamre2[:], op=ALU.mult), [LRE], [DEN])
        DV(lambda e: e.tensor_tensor(out=tq[:], in0=lamim2[:], in1=lamim2[:], op=ALU.mult), [LIM], [TQ])
        DV(lambda e: e.tensor_tensor(out=den[:], in0=den[:], in1=tq[:], op=ALU.add), [DEN, TQ], [DEN])
        DV(lambda e: e.reciprocal(out=den[:], in_=den[:]), [DEN], [DEN])
        DV(lambda e: e.tensor_tensor(out=kre[:], in0=nre[:], in1=den[:], op=ALU.mult), [NRE, DEN], [KRE])
        DV(lambda e: e.tensor_tensor(out=kim[:], in0=nim[:], in1=den[:], op=ALU.mult), [NIM, DEN], [KIM])
        shB = [128, 32, 16]
        b2re = kb.sb(shB, F32, "b2re"); b2im = kb.sb(shB, F32, "b2im")
        B2 = Buf()
        for du in range(2):
            kb.dma(SP, b2re[du * 64:(du + 1) * 64, :, :], b_re.rearrange("g p h -> p g h"), writes=[B2])
            kb.dma(SP, b2im[du * 64:(du + 1) * 64, :, :], b_im.rearrange("g p h -> p g h"), writes=[B2])
        bbre = kb.sb(shB, F32, "bbre"); bbim = kb.sb(shB, F32, "bbim")
        tb1 = kb.sb(shB, F32, "tb1"); tb2 = kb.sb(shB, F32, "tb2")
        BBRE, BBIM, TB1, TB2 = Buf(), Buf(), Buf(), Buf()
        kreb = kre[:].unsqueeze(2).to_broadcast(shB)
        kimb = kim[:].unsqueeze(2).to_broadcast(shB)
        DV(lambda e: e.tensor_tensor(out=tb1[:], in0=b2re[:], in1=kreb, op=ALU.mult), [B2, KRE], [TB1])
        DV(lambda e: e.tensor_tensor(out=tb2[:], in0=b2im[:], in1=kimb, op=ALU.mult), [B2, KIM], [TB2])
        DV(lambda e: e.tensor_tensor(out=bbre[:], in0=tb1[:], in1=tb2[:], op=ALU.subtract), [TB1, TB2], [BBRE])
        DV(lambda e: e.tensor_tensor(out=tb1[:], in0=b2im[:], in1=kreb, op=ALU.mult), [B2, KRE], [TB1])
        DV(lambda e: e.tensor_tensor(out=tb2[:], in0=b2re[:], in1=kimb, op=ALU.mult), [B2, KIM], [TB2])
        DV(lambda e: e.tensor_tensor(out=bbim[:], in0=tb1[:], in1=tb2[:], op=ALU.add), [TB1, TB2], [BBIM])
        cdup = kb.sb([128, 4, 2, 128], F32, "cdup")
        CDUP = Buf()
        for j in range(4):
            for ri, src in enumerate((c_re, c_im)):
                for du in range(2):
                    kb.dma(SP, cdup[:, j, ri, du * 64:(du + 1) * 64],
                           src[j * 8:(j + 1) * 8, :, :].rearrange("g h p -> (g h) p"), writes=[CDUP])
        c2re = kb.sb(shB, F32, "c2re"); c2im = kb.sb(shB, F32, "c2im")
        C2RE, C2IM = Buf(), Buf()
        for j in range(4):
            for ri, (dst, DB) in enumerate(((c2re, C2RE), (c2im, C2IM))):
                bk = (j * 2 + ri) % 4
                kb.op(PE, lambda e, j=j, ri=ri, bk=bk: e.transpose(out=Bk0[bk][:, 0:128], in_=cdup[:, j, ri, :],
                                                                   identity=ident_f[:]),
                      reads=[CDUP, IDF], writes=[BK0[bk]])
                DV(lambda e, j=j, dst=dst, bk=bk: e.tensor_copy(
                    out=dst[:, j * 8:(j + 1) * 8, :], in_=Bk0[bk][:, 0:128].rearrange("p (g h) -> p g h", g=8)),
                   [BK0[bk]], [DB])
        drep = kb.sb([32, 8, 16], F32, "drep")
        DREP = Buf()
        for ta in range(8):
            kb.dma(SP, drep[:, ta, :], ssm_d[:, :], writes=[DREP])
        dcol = kb.sb([128, 32], F32, "dcol")
        DCOL = Buf()
        kb.op(PE, lambda e: e.transpose(out=Bk0[4][:, 0:32], in_=drep[:].rearrange("g t h -> g (t h)"),
                                        identity=ident_f[0:32, 0:32]), reads=[DREP, IDF], writes=[BK0[4]])
        DV(lambda e: e.tensor_copy(out=dcol[:], in_=Bk0[4][:, 0:32]), [BK0[4]], [DCOL])

        sh4 = [128, 32, 8, 16]
        pre = kb.sb(sh4, F32, "pre"); pim = kb.sb(sh4, F32, "pim")
        ta_ = kb.sb(sh4, F32, "cp_t1"); tb_ = kb.sb(sh4, F32, "cp_t2")
        arr = kb.sb(sh4, F32, "arr")
        PRE, PIM, TA_, TB_, ARR = Buf(), Buf(), Buf(), Buf(), Buf()

        def cprod(k0, vre, vim, VRE, VIM):
            wre_b = Wre[:, :, k0:k0 + 8].unsqueeze(3).to_broadcast(sh4)
            wim_b = Wim[:, :, k0:k0 + 8].unsqueeze(3).to_broadcast(sh4)
            vre_b = vre[:].unsqueeze(2).to_broadcast(sh4)
            vim_b = vim[:].unsqueeze(2).to_broadcast(sh4)
            DV(lambda e: e.tensor_tensor(out=ta_[:], in0=wre_b, in1=vre_b, op=ALU.mult), [WRE, VRE], [TA_])
            DV(lambda e: e.tensor_tensor(out=tb_[:], in0=wim_b, in1=vim_b, op=ALU.mult), [WIM, VIM], [TB_])
            DV(lambda e: e.tensor_tensor(out=pre[:], in0=ta_[:], in1=tb_[:], op=ALU.subtract), [TA_, TB_], [PRE])
            DV(lambda e: e.tensor_tensor(out=ta_[:], in0=wre_b, in1=vim_b, op=ALU.mult), [WRE, VIM], [TA_])
            DV(lambda e: e.tensor_tensor(out=tb_[:], in0=wim_b, in1=vre_b, op=ALU.mult), [WIM, VRE], [TB_])
            DV(lambda e: e.tensor_tensor(out=pim[:], in0=ta_[:], in1=tb_[:], op=ALU.add), [TA_, TB_], [PIM])

        def arrange(top, TOP, bot, BOT, bot_sign, dst_ap, DST):
            DV(lambda e: e.tensor_copy(out=dst_ap[0:64], in_=top[0:64]), [TOP], [DST])
            DV(lambda e: e.tensor_scalar(out=dst_ap[64:128], in0=bot[64:128], scalar1=bot_sign, scalar2=None, op0=ALU.mult),
               [BOT], [DST])

        def transposed_to(dstM, DSTM):
            for g4 in range(8):
                bk = g4 % 2
                for gl in range(4):
                    g = g4 * 4 + gl
                    kb.op(PE, lambda e, g=g, gl=gl, bk=bk: e.transpose(
                        out=Bk0[bk][:, gl * 128:(gl + 1) * 128], in_=arr[:, g, :, :].rearrange("p a b -> p (a b)"),
                        identity=ident_f[:]), reads=[ARR, IDF], writes=[BK0[bk]])
                DV(lambda e, g4=g4, bk=bk: e.tensor_copy(out=dstM[:, g4 * 4:(g4 + 1) * 4, :].rearrange("p a b -> p (a b)"),
                                                         in_=Bk0[bk][:]), [BK0[bk]], [DSTM])

        cprod(0, bbre, bbim, BBRE, BBIM)
        arrange(pre[:], PRE, pim[:], PIM, 1.0, arr[:], ARR)
        transposed_to(M1a, M1A)
        arrange(pim[:], PIM, pre[:], PRE, 1.0, arr[:], ARR)
        transposed_to(M1b, M1B)
        xarr = kb.sb(sh4, F32, "xarr")
        XARR = Buf()
        cprod(8, bbre, bbim, BBRE, BBIM)
        arrange(pre[:], PRE, pim[:], PIM, 1.0, xarr[:], XARR)
        cprod(16, c2re, c2im, C2RE, C2IM)
        arrange(pre[:], PRE, pim[:], PIM, -1.0, arr[:], ARR)
        m2t = kb.sb([128, 128], F32, "m2t")
        M2T = Buf()
        for g in range(32):
            bk = 2 + g % 2
            kb.op(PE, lambda e, g=g, bk=bk: e.matmul(Bk0[bk][:, 0:128], lhsT=xarr[:, g, :, :].rearrange("p a b -> p (a b)"),
                                                     rhs=arr[:, g, :, :].rearrange("p a b -> p (a b)"), start=True, stop=True),
                  reads=[XARR, ARR], writes=[BK0[bk]])
            DV(lambda e, bk=bk: e.tensor_tensor(out=m2t[:], in0=Bk0[bk][:, 0:128], in1=mask8[:], op=ALU.mult),
               [BK0[bk], MASK8], [M2T])
            DV(lambda e, g=g: e.scalar_tensor_tensor(out=M2[:, g, :], in0=ident_f[:], scalar=dcol[:, g:g + 1], in1=m2t[:],
                                                     op0=ALU.mult, op1=ALU.add), [IDF, DCOL, M2T], [M2B])
        cprod(24, c2re, c2im, C2RE, C2IM)
        arrange(pre[:], PRE, pim[:], PIM, -1.0, arr[:], ARR)
        DV(lambda e: e.tensor_copy(out=M3[:].rearrange("p g m -> p (g m)"), in_=arr[:].rearrange("p g a b -> p (g a b)")),
           [ARR], [M3B])
        kb.pop()

        w_in_u = kb.sb([128, 8, 512], BF16, "w_in_u")
        wglu = kb.sb([128, 4, 1024], BF16, "wglu")
        WINU, WGLU = Buf(), Buf()
        kb.dma(POOL, w_in_u[:], w_in[:, 0:512].rearrange("(k p) n -> p k n", p=128), writes=[WINU])
        kb.dma(POOL, wglu[:], w_glu[:, :].rearrange("(k p) n -> p k n", p=128), writes=[WGLU])
        n1g0 = kb.sb([128, 8], F32, "p0_n1g")
        N1G0 = Buf()
        load_featmajor(n1g0[:], norm1_g[0:1, :], N1G0)
        ones0 = kb.sb([128, 128], BF16, "p0_ones")
        ONES0 = Buf()
        DV(lambda e: e.memset(ones0[:], 1.0), [], [ONES0])
        xs0 = [kb.sb([128, D], F32, f"p0_xs{i}") for i in range(2)]
        XS0 = [Buf(), Buf()]
        xn0 = kb.sb([128, 4, D], BF16, "p0_xn")
        XN0 = Buf()
        hT0 = kb.sb([128, 8, T0], BF16, "p0_hT")
        HT0 = Buf()
        junk0 = kb.sb([128, D], BF16, "p0_junk")
        JUNK0 = Buf()
        st0 = kb.sb([128, 8], F32, "p0_stat")
        SS0, RS0 = Buf(), Buf()
        sh10 = kb.sb([128, 8], F32, "p0_sh1"); sc10 = kb.sb([128, 8], F32, "p0_sc1"); G10 = kb.sb([128, 8], F32, "p0_G1")
        SH10, SC10, G1B0 = Buf(), Buf(), Buf()
        u8g = kb.sb([64, 32, 8, 16], BF16, "u8g")
        U8G = Buf()
        U8 = [kb.sb([128, 32, NC_], BF16, f"U8_{i}") for i in range(2)]
        U8B = [[Buf() for _ in range(2)] for _ in range(2)]
        qa = kb.sb([128, 512], F32, "q_tA"); qb = kb.sb([128, 512], F32, "q_tB")
        wa = [kb.sb([128, 512], F32, f"q_wa{i}") for i in range(2)]
        wb = [kb.sb([128, 512], F32, f"q_wb{i}") for i in range(2)]
        za = [kb.sb([128, 512], F32, f"q_za{i}") for i in range(2)]
        zb = [kb.sb([128, 512], F32, f"q_zb{i}") for i in range(2)]
        sa = kb.sb([128, 512], F32, "q_sa")
        pe_ = kb.sb([128, 512], F32, "q_pe"); pf_ = kb.sb([128, 512], F32, "q_pf")
        QA, QB, SA, PEB, PFB = (Buf() for _ in range(5))
        WA = [Buf(), Buf()]; WB = [Buf(), Buf()]; ZA = [Buf(), Buf()]; ZB = [Buf(), Buf()]
        t8a = kb.sb([128, 8], F32, "t8a"); t8b = kb.sb([128, 8], F32, "t8b")
        t8c = kb.sb([128, 8], F32, "t8c"); t8d = kb.sb([128, 8], F32, "t8d")
        T8A, T8B, T8C, T8D = Buf(), Buf(), Buf(), Buf()
        Sa_prev = kb.sb([128, 32], F32, "Sa_prev"); Sb_prev = kb.sb([128, 32], F32, "Sb_prev")
        SAP, SBP = Buf(), Buf()
        Sbuf = kb.sb([128, 32, NC_], BF16, "Sbuf")
        SBUF = [Buf() for _ in range(4)]
        Y8g = kb.sb([128, 32, NC_], BF16, "Y8g")
        Y8G = [Buf() for _ in range(4)]
        y8tm = kb.sb([64, 8, 512], BF16, "y8tm")
        Y8TM = [Buf() for _ in range(4)]
        yT = kb.sb([128, 4, T0], BF16, "yT")
        YT = Buf()
        sg = [kb.sb([128, T0], F32, f"sg{i}") for i in range(2)]
        SG = [Buf(), Buf()]
        gT = kb.sb([128, 4, T0], BF16, "gT")
        GT = [Buf() for _ in range(4)]
        gsq = [kb.sb([128, T0], BF16, f"gsq{i}") for i in range(2)]
        GSQ = [Buf(), Buf()]
        rbc0 = kb.sb([128, T0], F32, "p0_rbc")
        RBC0 = Buf()
        Gn0 = kb.sb([128, 4, T0], BF16, "p0_Gn")
        aw_d = [kb.sb([128, 8, 512], BF16, f"p0_awd{i}") for i in range(2)]
        AWD = [Buf(), Buf()]
        brow_d = kb.sb([nseq, 512], F32, "p0_browd")
        mrow_d = kb.sb([nseq, 512], F32, "p0_mrowd")
        BROWD, MROWD = Buf(), Buf()
        mod_steps = []
        for n_, pi_ in enumerate(deferred):
            def mA(n_=n_, pi_=pi_):
                wsrc, bsrc, coff, doff = pieces[pi_]
                kb.dma(POOL, aw_d[n_ % 2][:], wsrc[:, coff:coff + 512].rearrange("(k p) n -> p k n", p=128),
                       writes=[AWD[n_ % 2]])
            def mB(n_=n_, pi_=pi_):
                wsrc, bsrc, coff, doff = pieces[pi_]
                kb.dma(SP, brow_d[:], bcast_rows(bsrc[0:1, coff:coff + 512], nseq), writes=[BROWD])
                for k in range(8):
                    kb.op(PE, lambda e, k=k: e.matmul(Bk0[2][0:nseq, :], lhsT=condTb[:, k, :], rhs=aw_d[n_ % 2][:, k, :],
                                                      start=(k == 0), stop=(k == 7)),
                          reads=[CONDB, AWD[n_ % 2]], writes=[BK0[2]])
                kb.op(DVE, lambda e: e.tensor_tensor(out=mrow_d[:], in0=Bk0[2][0:nseq, :], in1=brow_d[:], op=ALU.add),
                      reads=[BK0[2], BROWD], writes=[MROWD])
                kb.dma(SP, modsc[:, doff:doff + 512], mrow_d[:], reads=[MROWD], writes=[MODSC])
            mod_steps.append((mA, mB))
        mod_queue = []
        for n_ in range(len(mod_steps)):
            mod_queue.append(mod_steps[n_][0])
            if n_ >= 1:
                mod_queue.append(mod_steps[n_ - 1][1])
        if mod_steps:
            mod_queue.append(mod_steps[-1][1])
        GN0 = Buf()

        for b in range(nseq):
            load_featmajor(sh10[:], modsc[b:b + 1, 0:D], SH10, extra_reads=[MODSC])
            load_featmajor(sc10[:], modsc[b:b + 1, D:2 * D], SC10, extra_reads=[MODSC])
            DV(lambda e: e.scalar_tensor_tensor(out=G10[:], in0=sc10[:], scalar=1.0, in1=n1g0[:], op0=ALU.add, op1=ALU.mult),
               [SC10, N1G0], [G1B0])
            DV(lambda e: e.memset(Sa_prev[:], 0.0), [], [SAP])
            DV(lambda e: e.memset(Sb_prev[:], 0.0), [], [SBP])
            def front0_steps(i):
                tok0 = b * seq + i * T0
                U8_ = U8[i % 2]
                steps = []
                steps.append(lambda: DV(lambda e: e.memset(st0[:, 0:4], 0.0), [], [SS0]))
                for su in range(4):
                    xb_ = su % 2
                    def s_load(su=su, xb_=xb_):
                        kb.dma(SP, xs0[xb_][:], x[tok0 + su * 128:tok0 + (su + 1) * 128, :], writes=[XS0[xb_]])
                    def s_stat(su=su, xb_=xb_):
                        kb.op(ACT, lambda e: e.activation(out=junk0[:], in_=xs0[xb_][:], func=AF.Square,
                                                          accum_out=st0[:, su:su + 1]), reads=[XS0[xb_]], writes=[JUNK0, SS0])
                        rsqrt_cols(st0[:, 4 + su:5 + su], st0[:, su:su + 1], 1.0 / D, SS0, RS0)
                    def s_xn(su=su, xb_=xb_):
                        DV(lambda e: e.tensor_scalar(out=xn0[:, su, :], in0=xs0[xb_][:], scalar1=st0[:, 4 + su:5 + su],
                                                     scalar2=None, op0=ALU.mult), [XS0[xb_], RS0], [XN0])
                    steps += [s_load, s_stat, s_xn]
                for k in range(8):
                    def s_tr(k=k):
                        pi = k % 2
                        tpv = Bk0[pi][:].bitcast(BF16)
                        for su in range(4):
                            kb.op(PE, lambda e, su=su: e.transpose(out=tpv[:, su * 128:(su + 1) * 128],
                                                                   in_=xn0[:, su, k * 128:(k + 1) * 128], identity=ident_b[:]),
                                  reads=[XN0, IDB], writes=[BK0[pi]])
                        kb.op(ACT, lambda e: e.activation(out=hT0[:, k, :], in_=tpv[:, 0:T0], func=AF.Identity,
                                                          bias=sh10[:, k:k + 1], scale=G10[:, k:k + 1]),
                              reads=[BK0[pi], SH10, G1B0], writes=[HT0])
                    steps.append(s_tr)
                for ta in range(8):
                    def s_u8(ta=ta):
                        bk = 2
                        for k in range(8):
                            kb.op(PE, lambda e, k=k: e.matmul(Bk0[bk][0:NC_, :], lhsT=hT0[:, k, ta:T0:8], rhs=w_in_u[:, k, :],
                                                              start=(k == 0), stop=(k == 7)), reads=[HT0, WINU], writes=[BK0[bk]])
                        kb.op(ACT, lambda e: e.activation(out=u8g[:, :, ta, :],
                                                          in_=Bk0[bk][0:NC_, :].rearrange("p (g h) -> p g h", g=32),
                                                          func=AF.Copy), reads=[BK0[bk]], writes=[U8G])
                    steps.append(s_u8)
                for gh in range(2):
                    def s_U8(gh=gh):
                        tpv = Bk0[gh][:].bitcast(BF16)
                        for gl in range(16):
                            g = gh * 16 + gl
                            kb.op(PE, lambda e, g=g, gl=gl: e.transpose(
                                out=tpv[:, gl * NC_:(gl + 1) * NC_], in_=u8g[:, g, :, :].rearrange("p a b -> p (a b)"),
                                identity=ident_b[0:NC_, 0:NC_]), reads=[U8G, IDB], writes=[BK0[gh]])
                        kb.op(ACT, lambda e: e.activation(
                            out=U8_[:, gh * 16:(gh + 1) * 16, :].rearrange("p a b -> p (a b)"), in_=tpv[:, 0:16 * NC_],
                            func=AF.Copy), reads=[BK0[gh]], writes=[U8B[i % 2][gh]])
                    steps.append(s_U8)
                return steps

            bg0 = {"steps": [], "slots": 1}

            def pull0():
                n = len(bg0["steps"])
                if n:
                    k = -(-n // max(bg0["slots"], 1))
                    for _ in range(k):
                        bg0["steps"].pop(0)()
                bg0["slots"] -= 1

            for st_ in front0_steps(0):
                st_()
            for i in range(nt0):
                tok0 = b * seq + i * T0
                U8c = U8[i % 2]
                U8Bc = U8B[i % 2]
                bg0["steps"] = front0_steps(i + 1) if i + 1 < nt0 else []
                bg0["slots"] = 20
                def stage_A(qd):
                    la, lb = 4 + (qd % 2) * 2, 5 + (qd % 2) * 2
                    for gl in range(8):
                        g = qd * 8 + gl
                        kb.op(PE, lambda e, g=g, gl=gl: e.matmul(Bk0[la][:, gl * NC_:(gl + 1) * NC_], lhsT=M1a[:, g, :],
                                                                 rhs=U8c[:, g, :], start=True, stop=True),
                              reads=[M1A, U8Bc[g // 16]], writes=[BK0[la]])
                    for gl in range(8):
                        g = qd * 8 + gl
                        kb.op(PE, lambda e, g=g, gl=gl: e.matmul(Bk0[lb][:, gl * NC_:(gl + 1) * NC_], lhsT=M1b[:, g, :],
                                                                 rhs=U8c[:, g, :], start=True, stop=True),
                              reads=[M1B, U8Bc[g // 16]], writes=[BK0[lb]])

                def stage_B(qd):
                    la, lb = 4 + (qd % 2) * 2, 5 + (qd % 2) * 2
                    pq = qd % 2
                    gs = slice(qd * 8, (qd + 1) * 8)
                    TcQ = Tc[:, gs, :].rearrange("p a b -> p (a b)")
                    TsQ = Ts[:, gs, :].rearrange("p a b -> p (a b)")
                    RQ = Rt[:, gs, :].rearrange("p a b -> p (a b)")
                    La, Lb = Bk0[la][:], Bk0[lb][:]
                    wa_, wb_, za_, zb_ = wa[pq], wb[pq], za[pq], zb[pq]
                    WA_, WB_, ZA_, ZB_ = WA[pq], WB[pq], ZA[pq], ZB[pq]
                    DV(lambda e: e.tensor_tensor(out=qa[:], in0=La, in1=TcQ, op=ALU.mult), [BK0[la], TCB], [QA])
                    DV(lambda e: e.tensor_tensor(out=qb[:], in0=Lb, in1=TsQ, op=ALU.mult), [BK0[lb], TSB], [QB])
                    DV(lambda e: e.tensor_tensor(out=wa_[:], in0=qa[:], in1=qb[:], op=ALU.add), [QA, QB], [WA_])
                    DV(lambda e: e.tensor_tensor(out=qa[:], in0=Lb, in1=TcQ, op=ALU.mult), [BK0[lb], TCB], [QA])
                    DV(lambda e: e.tensor_tensor(out=qb[:], in0=La, in1=TsQ, op=ALU.mult), [BK0[la], TSB], [QB])
                    DV(lambda e: e.tensor_tensor(out=wb_[:], in0=qa[:], in1=qb[:], op=ALU.subtract), [QA, QB], [WB_])
                    wa3 = wa_[:].rearrange("p (g c) -> p g c", g=8)
                    wb3 = wb_[:].rearrange("p (g c) -> p g c", g=8)
                    za3 = za_[:].rearrange("p (g c) -> p g c", g=8)
                    zb3 = zb_[:].rearrange("p (g c) -> p g c", g=8)
                    sa3 = sa[:].rearrange("p (g c) -> p g c", g=8)
                    DV(lambda e: e.tensor_tensor(out=t8a[:], in0=Rho[:, gs], in1=Sa_prev[:, gs], op=ALU.mult), [RHO, SAP], [T8A])
                    DV(lambda e: e.tensor_tensor(out=wa3[:, :, 0], in0=wa3[:, :, 0], in1=t8a[:], op=ALU.add), [WA_, T8A], [WA_])
                    DV(lambda e: e.tensor_tensor(out=t8b[:], in0=Rho[:, gs], in1=Sb_prev[:, gs], op=ALU.mult), [RHO, SBP], [T8B])
                    DV(lambda e: e.tensor_tensor(out=wb3[:, :, 0], in0=wb3[:, :, 0], in1=t8b[:], op=ALU.add), [WB_, T8B], [WB_])
                    DV(lambda e: e.tensor_tensor_scan(out=za_[:], data0=RQ, data1=wa_[:], initial=0.0, op0=ALU.mult, op1=ALU.add),
                       [RTB, WA_], [ZA_])
                    DV(lambda e: e.tensor_tensor_scan(out=zb_[:], data0=RQ, data1=wb_[:], initial=0.0, op0=ALU.mult, op1=ALU.add),
                       [RTB, WB_], [ZB_])
                    PL = lambda fn, r, w: kb.op(POOL, fn, reads=r, writes=w)
                    PL(lambda e: e.tensor_tensor(out=pe_[:], in0=za_[:], in1=TcQ, op=ALU.mult), [ZA_, TCB], [PEB])
                    PL(lambda e: e.tensor_tensor(out=pf_[:], in0=zb_[:], in1=TsQ, op=ALU.mult), [ZB_, TSB], [PFB])
                    PL(lambda e: e.tensor_tensor(out=sa[:], in0=pe_[:], in1=pf_[:], op=ALU.subtract), [PEB, PFB], [SA])
                    PL(lambda e: e.tensor_copy(out=Sbuf[:, gs, 0], in_=Sa_prev[:, gs]), [SAP], [SBUF[qd]])
                    PL(lambda e: e.tensor_copy(out=Sbuf[:, gs, 1:NC_], in_=sa3[:, :, 0:NC_ - 1]), [SA], [SBUF[qd]])
                    PL(lambda e: e.tensor_tensor(out=t8c[:], in0=zb3[:, :, NC_ - 1], in1=Tc[:, gs, NC_ - 1], op=ALU.mult),
                       [ZB_, TCB], [T8C])
                    PL(lambda e: e.tensor_tensor(out=t8d[:], in0=za3[:, :, NC_ - 1], in1=Ts[:, gs, NC_ - 1], op=ALU.mult),
                       [ZA_, TSB], [T8D])
                    PL(lambda e: e.tensor_tensor(out=Sb_prev[:, gs], in0=t8c[:], in1=t8d[:], op=ALU.add), [T8C, T8D], [SBP])
                    PL(lambda e: e.tensor_copy(out=Sa_prev[:, gs], in_=sa3[:, :, NC_ - 1]), [SA], [SAP])

                def stage_C(qd):
                    gs = slice(qd * 8, (qd + 1) * 8)
                    yb = 2 + qd % 2
                    for gl in range(8):
                        g = qd * 8 + gl
                        kb.op(PE, lambda e, g=g, gl=gl, yb=yb: e.matmul(Bk0[yb][:, gl * NC_:(gl + 1) * NC_], lhsT=M2[:, g, :],
                                                                        rhs=U8c[:, g, :], start=True, stop=False),
                              reads=[M2B, U8Bc[g // 16]], writes=[BK0[yb]])
                        kb.op(PE, lambda e, g=g, gl=gl, yb=yb: e.matmul(Bk0[yb][:, gl * NC_:(gl + 1) * NC_], lhsT=M3[:, g, :],
                                                                        rhs=Sbuf[:, g, :], start=False, stop=True),
                              reads=[M3B, SBUF[qd]], writes=[BK0[yb]])
                    kb.op(ACT, lambda e, yb=yb: e.activation(out=Y8g[:, gs, :].rearrange("p a b -> p (a b)"), in_=Bk0[yb][:],
                                                             func=AF.Gelu_apprx_tanh), reads=[BK0[yb]], writes=[Y8G[qd]])
                    tb = qd % 2
                    tpv = Bk0[tb][:].bitcast(BF16)
                    for gl in range(8):
                        g = qd * 8 + gl
                        kb.op(PE, lambda e, g=g, gl=gl, tpv=tpv: e.transpose(out=tpv[0:NC_, gl * 128:(gl + 1) * 128],
                                                                             in_=Y8g[:, g, :], identity=ident_b[:]),
                              reads=[Y8G[qd], IDB], writes=[BK0[tb]])
                    kb.op(ACT, lambda e, qd=qd, tpv=tpv: e.activation(
                        out=y8tm[:, :, qd * 128:(qd + 1) * 128].rearrange("p t (g h) -> p g t h", g=8),
                        in_=tpv[0:NC_, 0:1024].rearrange("p (g t h) -> p g t h", g=8, t=8), func=AF.Copy),
                          reads=[BK0[tb]], writes=[Y8TM[qd]])

                for _ in range(2):
                    if mod_queue:
                        mod_queue.pop(0)()
                stage_A(0)
                stage_A(1)
                pull0()
                for qd in range(4):
                    stage_B(qd)
                    pull0()
                    stage_C(qd)
                    pull0()
                    if qd + 2 < 4:
                        stage_A(qd + 2)
                        pull0()
                for j in range(4):
                    tb = j % 2
                    tpv = Bk0[tb][:].bitcast(BF16)
                    for t8 in range(8):
                        kb.op(PE, lambda e, j=j, t8=t8, tpv=tpv: e.transpose(out=tpv[:, t8 * NC_:(t8 + 1) * NC_],
                                                                             in_=y8tm[:, t8, j * 128:(j + 1) * 128],
                                                                             identity=ident_b[0:NC_, 0:NC_]),
                              reads=[Y8TM[j], IDB], writes=[BK0[tb]])
                    kb.op(ACT, lambda e, j=j, tpv=tpv: e.activation(out=yT[:, j, :].rearrange("p (c t) -> p t c", t=8),
                                                                    in_=tpv[:, 0:T0].rearrange("p (t c) -> p t c", t=8),
                                                                    func=AF.Copy), reads=[BK0[tb]], writes=[YT])
                    pull0()
                def glu_mm(n):
                    za_, zb_ = 4 + n % 2, 6 + n % 2
                    for cc in range(4):
                        kb.op(PE, lambda e, cc=cc: e.matmul(Bk0[za_][:], lhsT=wglu[:, cc, n * 128:(n + 1) * 128],
                                                            rhs=yT[:, cc, :], start=(cc == 0), stop=(cc == 3)),
                              reads=[WGLU, YT], writes=[BK0[za_]])
                    for cc in range(4):
                        kb.op(PE, lambda e, cc=cc: e.matmul(Bk0[zb_][:], lhsT=wglu[:, cc, 512 + n * 128:512 + (n + 1) * 128],
                                                            rhs=yT[:, cc, :], start=(cc == 0), stop=(cc == 3)),
                              reads=[WGLU, YT], writes=[BK0[zb_]])

                glu_mm(0)
                glu_mm(1)
                for n in range(4):
                    za_, zb_ = 4 + n % 2, 6 + n % 2
                    sg_, SG_ = sg[n % 2], SG[n % 2]
                    gq_, GQ_ = gsq[n % 2], GSQ[n % 2]
                    kb.op(ACT, lambda e: e.activation(out=sg_[:], in_=Bk0[zb_][:], func=AF.Sigmoid), reads=[BK0[zb_]], writes=[SG_])
                    DV(lambda e, n=n: e.tensor_tensor(out=gT[:, n, :], in0=Bk0[za_][:], in1=sg_[:], op=ALU.mult),
                       [BK0[za_], SG_], [GT[n]])
                    kb.op(POOL, lambda e, n=n: e.tensor_tensor(out=gq_[:], in0=gT[:, n, :], in1=gT[:, n, :], op=ALU.mult),
                          reads=[GT[n]], writes=[GQ_])
                    if n + 2 < 4:
                        glu_mm(n + 2)
                    kb.op(PE, lambda e, n=n: e.matmul(Bk0[3][:], lhsT=ones0[:], rhs=gq_[:], start=(n == 0), stop=(n == 3)),
                          reads=[ONES0, GQ_], writes=[BK0[3]])
                    pull0()
                kb.op(ACT, lambda e: e.activation(out=rbc0[:], in_=Bk0[3][:], func=AF.Ln, bias=epsc[:, 0:1], scale=1.0 / 512),
                      reads=[BK0[3], EPSC], writes=[RBC0])
                kb.op(ACT, lambda e: e.activation(out=rbc0[:], in_=rbc0[:], func=AF.Exp, scale=-0.5), reads=[RBC0], writes=[RBC0])
                for n in range(4):
                    DV(lambda e, n=n: e.tensor_tensor(out=Gn0[:, n, :], in0=gT[:, n, :], in1=rbc0[:], op=ALU.mult),
                       [GT[n], RBC0], [GN0])
                kb.dma(SP, gnsc[:, tok0:tok0 + T0].rearrange("(c p) t -> p c t", p=128), Gn0[:], reads=[GN0], writes=[GNSC])
                while bg0["steps"]:
                    bg0["steps"].pop(0)()
                if b == nseq - 1 and i == nt0 - 1:
                    while mod_queue:
                        mod_queue.pop(0)()
        kb.pop()

    if dbg == "p0_dump":
        kb.push()
        gd = kb.sb([128, 4, 1024], BF16, "gd")
        gf = kb.sb([128, 4, 1024], F32, "gf")
        GD, GF = Buf(), Buf()
        for blk in range(ntok // 1024):
            kb.dma(SP, gd[:], gnsc[:, blk * 1024:(blk + 1) * 1024].rearrange("(c p) t -> p c t", p=128), reads=[GNSC], writes=[GD])
            kb.op(DVE, lambda e: e.tensor_copy(out=gf[:], in_=gd[:]), reads=[GD], writes=[GF])
            tok = kb.dma(SP, out[blk * 512:(blk + 1) * 512, :].rearrange("(c p) t -> p c t", p=128), gf[:], reads=[GF], writes=[OUTB])
            kb.out_tokens.append(tok)
        for tok in kb.out_tokens:
            kb._wait(SP, tok)
        kb.pop()
        return nc, kb

    if "p1" in phases:
        kb.push()
        T1 = 512
        nt1 = seq // T1
        NSUB = T1 // 128
        s1, s2, s3 = D_SSM, D_SSM + Q_LORA, D_SSM + Q_LORA + KV_LORA
        w_in_a = kb.sb([128, 8, 672], BF16, "w_in_a")
        w_rot = kb.sb([128, 8, 128], BF16, "w_kr")
        wuq = kb.sb([128, 3, NH * 128], BF16, "wuq_c")
        Kw = kb.sb([128, 2, 512], BF16, "Kw")
        Vw = kb.sb([128, 2, 512], BF16, "Vw")
        wout = kb.sb([128, 8, D], BF16, "wout")
        WINA, WROT, WUQ, KW, VW, WOUT = Buf(), Buf(), Buf(), Buf(), Buf(), Buf()
        kb.dma(POOL, w_in_a[:], w_in[:, s1:IN_COLS].rearrange("(k p) n -> p k n", p=128), writes=[WINA])
        kb.op(DVE, lambda e: e.memset(w_rot[:], 0.0), writes=[WROT])
        kb.dma(POOL, w_rot[:, :, 64:96], w_in[:, s3:s3 + 32].rearrange("(k p) n -> p k n", p=128), writes=[WROT])
        kb.dma(POOL, w_rot[:, :, 96:112], w_in[:, s3 + 16:s3 + 32].rearrange("(k p) n -> p k n", p=128), writes=[WROT])
        kb.dma(POOL, w_rot[:, :, 112:128], w_in[:, s3:s3 + 16].rearrange("(k p) n -> p k n", p=128), writes=[WROT])
        kb.op(DVE, lambda e: e.tensor_scalar(out=w_rot[:, :, 96:112], in0=w_rot[:, :, 96:112], scalar1=-1.0, scalar2=None,
                                             op0=ALU.mult), reads=[WROT], writes=[WROT])
        qg = kb.sb([128, 3], F32, "qg")
        kvg = kb.sb([128, 2], F32, "kvg")
        og = kb.sb([128, 8], F32, "og")
        n1g = kb.sb([128, 8], F32, "n1g")
        QG, KVG, OG, N1G = Buf(), Buf(), Buf(), Buf()
        load_featmajor(qg[:], q_norm_g[0:1, :], QG)
        load_featmajor(kvg[:], kv_norm_g[0:1, :], KVG)
        load_featmajor(og[:, 0:4], ssm_out_g[0:1, :], OG)
        load_featmajor(og[:, 4:8], attn_out_g[0:1, :], OG)
        load_featmajor(n1g[:], norm1_g[0:1, :], N1G)
        kb.push()
        stg = kb.sb([128, 3, 1024], F32, "stg")
        STG = Buf()
        QSCALE = (64 + 32) ** -0.5
        kb.dma(SP, stg[:, :, 0:768], w_uq[:, :].rearrange("(k p) n -> p k n", p=128), writes=[STG])
        wq4 = wuq[:].rearrange("p c (h d) -> p c h d", h=NH)
        st4 = stg[:, :, 0:768].rearrange("p c (h d) -> p c h d", h=NH)
        for cc in range(3):
            kb.op(DVE, lambda e, cc=cc: e.tensor_scalar(out=wq4[:, cc, :, 0:96], in0=st4[:, cc, :, :], scalar1=qg[:, cc:cc + 1],
                                                        scalar2=QSCALE, op0=ALU.mult, op1=ALU.mult),
                  reads=[STG, QG], writes=[WUQ])
            kb.op(DVE, lambda e, cc=cc: e.tensor_scalar(out=wq4[:, cc, :, 96:112], in0=st4[:, cc, :, 80:96],
                                                        scalar1=qg[:, cc:cc + 1], scalar2=-QSCALE, op0=ALU.mult, op1=ALU.mult),
                  reads=[STG, QG], writes=[WUQ])
            kb.op(DVE, lambda e, cc=cc: e.tensor_scalar(out=wq4[:, cc, :, 112:128], in0=st4[:, cc, :, 64:80],
                                                        scalar1=qg[:, cc:cc + 1], scalar2=QSCALE, op0=ALU.mult, op1=ALU.mult),
                  reads=[STG, QG], writes=[WUQ])
        kb.dma(SP, stg[:, 0:2, :], w_ukv[:, :].rearrange("(k p) n -> p k n", p=128), reads=[STG], writes=[STG])
        st5 = stg[:, 0:2, :].rearrange("p c (h t d) -> p c h t d", h=NH, t=2)
        for cc in range(2):
            kb.op(DVE, lambda e, cc=cc: e.tensor_scalar(out=Kw[:, cc, :].rearrange("p (h d) -> p h d", h=NH),
                                                        in0=st5[:, cc, :, 0, :], scalar1=kvg[:, cc:cc + 1], scalar2=None,
                                                        op0=ALU.mult), reads=[STG, KVG], writes=[KW])
            kb.op(DVE, lambda e, cc=cc: e.tensor_scalar(out=Vw[:, cc, :].rearrange("p (h d) -> p h d", h=NH),
                                                        in0=st5[:, cc, :, 1, :], scalar1=kvg[:, cc:cc + 1], scalar2=None,
                                                        op0=ALU.mult), reads=[STG, KVG], writes=[VW])
        kb.pop()
        Kc = kb.sb([96, NH, seq], BF16, "Kc")
        Vc = kb.sb([128, seq // 128, NH, 65], BF16, "Vc")
        KC = [Buf(f"kc{h}") for h in range(NH)]
        VC = [Buf(f"vc{j}") for j in range(seq // 128)]
        kb.op(POOL, lambda e: e.memset(Vc[:], 1.0), writes=VC)
        ones_b = kb.sb([128, 128], BF16, "ones_b")
        ones_f = kb.sb([128, 64], F32, "ones_f")
        tri = kb.sb([128, 128], BF16, "tri")
        ONES, TRI = Buf(), Buf()
        kb.op(DVE, lambda e: e.memset(ones_b[:], 1.0), writes=[ONES])
        kb.op(DVE, lambda e: e.memset(ones_f[:], 1.0), writes=[ONES])
        kb.dma(POOL, tri[:], tri_in[:, :], writes=[TRI])
        KCT = [[Buf(f"kc{h}_{i}") for i in range(nt1)] for h in range(NH)]
        xs_ = [kb.sb([128, D], F32, f"p1_xs{i}") for i in range(2)]
        XS = [Buf(), Buf()]
        scr = [kb.sb([128, 4, D], BF16, f"p1_scr{i}") for i in range(2)]
        SCR = [Buf(), Buf()]

        def qt_view(pp):
            return scr[pp][0:96, :, :].rearrange("p a b -> p (a b)")[:, 0:NH * T1].rearrange("p (h t) -> p h t", h=NH)
        hT = kb.sb([128, 8, T1], BF16, "p1_hT")
        HT = Buf()
        st1 = kb.sb([128, 8], F32, "p1_stat")
        SS1, RS1 = Buf(), Buf()
        sh1 = kb.sb([128, 8], F32, "p1_sh1")
        sc1 = kb.sb([128, 8], F32, "p1_sc1")
        G1 = kb.sb([128, 8], F32, "p1_G1")
        SH1, SC1, G1B = Buf(), Buf(), Buf()
        qnT = kb.sb([128, 3, T1], BF16, "p1_qnT")
        kvnT = kb.sb([128, 2, T1], BF16, "p1_kvnT")
        QNT, KVNT = Buf(), Buf()
        sq = kb.sb([128, T1], BF16, "p1_sq")
        SQ = Buf()
        rbc = kb.sb([128, T1], F32, "p1_rbc")
        RBC = Buf()
        junk1 = rbc[:].bitcast(BF16)
        JUNK1 = RBC
        cosT = kb.sb([96, T1], F32, "p1_cos")
        sinT = kb.sb([128, T1], F32, "p1_sin")
        COS, SIN = Buf(), Buf()
        t1 = kb.sb([96, T1], F32, "p1_t1")
        t2 = kb.sb([96, T1], F32, "p1_t2")
        T1B, T2B = Buf(), Buf()
        kr = kb.sb([96, T1], BF16, "p1_kr")
        KR = Buf()
        pt = [kb.sb([128, T1], BF16, f"p1_pt{i}") for i in range(4)]
        PT = [Buf() for _ in range(4)]
        o_sb = [kb.sb([65, T1], F32, f"p1_osb{i}") for i in range(2)]
        OSB = [Buf(), Buf()]
        Ya = kb.sb([128, 4, T1], BF16, "p1_Ya")
        YA = [Buf() for _ in range(4)]
        Gn = kb.sb([128, 4, T1], BF16, "p1_Gn")
        GN = Buf()
        gate1 = Gn[:].rearrange("p a b -> p (a b)").bitcast(F32)
        wstg = hT[:].rearrange("p a b -> p (a b)").bitcast(F32)[:, 0:D]
        GATE1, WSTG = GN, HT
        B = [kb.ps([128, 512], F32, f"p1_bank{i}") for i in range(8)]
        BK = [Buf(f"bank{i}") for i in range(8)]
        FB = (6, 7)
        R = slice(64, 96)
        use_ssm = "p0" in phases

        def rope_combine(bk, dst_ap, DSTS):
            kb.op(DVE, lambda e: e.tensor_tensor(out=t1[R, :], in0=B[bk][R, :], in1=cosT[R, :], op=ALU.mult),
                  reads=[BK[bk], COS], writes=[T1B])
            kb.op(DVE, lambda e: e.tensor_tensor(out=t2[R, :], in0=B[bk][96:128, :], in1=sinT[96:128, :], op=ALU.mult),
                  reads=[BK[bk], SIN], writes=[T2B])
            kb.op(DVE, lambda e: e.tensor_tensor(out=dst_ap, in0=t1[R, :], in1=t2[R, :], op=ALU.add),
                  reads=[T1B, T2B], writes=DSTS)

        def latent_steps(col0, nchunk, dstT, DST, nlat):
            cb, sb_ = FB
            steps = []
            for cc in range(nchunk):
                def s_mm(cc=cc):
                    for k in range(8):
                        kb.op(PE, lambda e, k=k: e.matmul(
                            B[cb][:], lhsT=w_in_a[:, k, col0 + cc * 128:col0 + (cc + 1) * 128], rhs=hT[:, k, :],
                            start=(k == 0), stop=(k == 7)), reads=[WINA, HT], writes=[BK[cb]])
                def s_sq():
                    kb.op(ACT, lambda e: e.activation(out=sq[:], in_=B[cb][:], func=AF.Square), reads=[BK[cb]], writes=[SQ])
                def s_ones(cc=cc):
                    kb.op(PE, lambda e: e.matmul(B[sb_][:], lhsT=ones_b[:], rhs=sq[:], start=(cc == 0),
                                                 stop=(cc == nchunk - 1)), reads=[ONES, SQ], writes=[BK[sb_]])
                steps += [s_mm, s_sq, s_ones]
            def s_ln():
                kb.op(ACT, lambda e: e.activation(out=rbc[:], in_=B[sb_][:], func=AF.Ln, bias=epsc[:, 0:1], scale=1.0 / nlat),
                      reads=[BK[sb_], EPSC], writes=[RBC])
            def s_exp():
                kb.op(ACT, lambda e: e.activation(out=rbc[:], in_=rbc[:], func=AF.Exp, scale=-0.5), reads=[RBC], writes=[RBC])
            steps += [s_ln, s_exp]
            for cc in range(nchunk):
                bk = FB[cc % 2]
                def s_mm2(cc=cc, bk=bk):
                    for k in range(8):
                        kb.op(PE, lambda e, k=k: e.matmul(
                            B[bk][:], lhsT=w_in_a[:, k, col0 + cc * 128:col0 + (cc + 1) * 128], rhs=hT[:, k, :],
                            start=(k == 0), stop=(k == 7)), reads=[WINA, HT], writes=[BK[bk]])
                def s_scale(cc=cc, bk=bk):
                    kb.op(DVE, lambda e: e.tensor_tensor(out=dstT[:, cc, :], in0=B[bk][:], in1=rbc[:], op=ALU.mult),
                          reads=[BK[bk], RBC], writes=[DST])
                steps += [s_mm2, s_scale]
            return steps

        def front_steps(b, i):
            tok0 = b * seq + i * T1
            pp = i % 2
            xn1 = scr[pp]
            Qt = qt_view(pp)
            cols = slice(i * T1, (i + 1) * T1)
            steps = []

            def s_tables():
                kb.dma(SP, cosT[R, :], ropesc[0, :, tok0:tok0 + T1], reads=[ROPESC], writes=[COS])
                kb.dma(SP, sinT[96:128, :], ropesc[1, :, tok0:tok0 + T1], reads=[ROPESC], writes=[SIN])
                kb.op(DVE, lambda e: e.memset(st1[:, 0:4], 0.0), writes=[SS1])
            steps.append(s_tables)
            for su in range(NSUB):
                xb_ = su % 2
                def s_load(su=su, xb_=xb_):
                    kb.dma(SP, xs_[xb_][:], x[tok0 + su * 128:tok0 + (su + 1) * 128, :], writes=[XS[xb_]])
                def s_stat(su=su, xb_=xb_):
                    kb.op(ACT, lambda e: e.activation(out=junk1, in_=xs_[xb_][:], func=AF.Square,
                                                      accum_out=st1[:, su:su + 1]), reads=[XS[xb_]], writes=[JUNK1, SS1])
                    rsqrt_cols(st1[:, 4 + su:5 + su], st1[:, su:su + 1], 1.0 / D, SS1, RS1)
                def s_xn(su=su, xb_=xb_):
                    kb.op(DVE, lambda e: e.tensor_scalar(out=xn1[:, su, :], in0=xs_[xb_][:], scalar1=st1[:, 4 + su:5 + su],
                                                         scalar2=None, op0=ALU.mult), reads=[XS[xb_], RS1], writes=[SCR[pp]])
                steps += [s_load, s_stat, s_xn]
            for k in range(8):
                bk = FB[k % 2]
                def s_tr(k=k, bk=bk):
                    tpv = B[bk][:].bitcast(BF16)
                    for su in range(NSUB):
                        kb.op(PE, lambda e, su=su: e.transpose(out=tpv[:, su * 128:(su + 1) * 128],
                                                               in_=xn1[:, su, k * 128:(k + 1) * 128], identity=ident_b[:]),
                              reads=[SCR[pp], IDB], writes=[BK[bk]])
                def s_ev(k=k, bk=bk):
                    tpv = B[bk][:].bitcast(BF16)
                    kb.op(DVE, lambda e: e.tensor_scalar(out=hT[:, k, :], in0=tpv[:, 0:T1], scalar1=G1[:, k:k + 1],
                                                         scalar2=sh1[:, k:k + 1], op0=ALU.mult, op1=ALU.add),
                          reads=[BK[bk], SH1, G1B], writes=[HT])
                steps += [s_tr, s_ev]
            steps += latent_steps(0, 3, qnT, QNT, Q_LORA)
            steps += latent_steps(Q_LORA, 2, kvnT, KVNT, KV_LORA)

            def s_kr1():
                for k in range(8):
                    kb.op(PE, lambda e, k=k: e.matmul(B[FB[0]][:], lhsT=w_rot[:, k, :], rhs=hT[:, k, :],
                                                      start=(k == 0), stop=(k == 7)), reads=[WROT, HT], writes=[BK[FB[0]]])
            def s_kr3():
                rope_combine(FB[0], kr[R, :], [KR])
                kb.op(DVE, lambda e: e.tensor_copy(out=Kc[R, :, cols], in_=kr[R, :].unsqueeze(1).to_broadcast([32, NH, T1])),
                      reads=[KR], writes=[KCT[h][i] for h in range(NH)])
            steps += [s_kr1, s_kr3]
            for hp in range(4):
                bk = FB[hp % 2]
                def s_kmm(hp=hp, bk=bk):
                    for cc in range(2):
                        kb.op(PE, lambda e, cc=cc: e.matmul(B[bk][:], lhsT=Kw[:, cc, hp * 128:(hp + 1) * 128], rhs=kvnT[:, cc, :],
                                                            start=(cc == 0), stop=(cc == 1)), reads=[KW, KVNT], writes=[BK[bk]])
                def s_kev(hp=hp, bk=bk):
                    kb.op(DVE, lambda e: e.tensor_copy(out=Kc[0:64, 2 * hp, cols], in_=B[bk][0:64, :]),
                          reads=[BK[bk]], writes=[KCT[2 * hp][i]])
                    kb.op(DVE, lambda e: e.tensor_copy(out=Kc[0:64, 2 * hp + 1, cols], in_=B[bk][64:128, :]),
                          reads=[BK[bk]], writes=[KCT[2 * hp + 1][i]])
                steps += [s_kmm, s_kev]
            for su in range(NSUB):
                bk = FB[su % 2]
                blk = i * NSUB + su
                def s_vmm(su=su, bk=bk):
                    for cc in range(2):
                        kb.op(PE, lambda e, cc=cc: e.matmul(B[bk][:], lhsT=kvnT[:, cc, su * 128:(su + 1) * 128], rhs=Vw[:, cc, :],
                                                            start=(cc == 0), stop=(cc == 1)), reads=[KVNT, VW], writes=[BK[bk]])
                def s_vev(bk=bk, blk=blk):
                    kb.op(DVE, lambda e: e.tensor_copy(out=Vc[:, blk, :, 0:64], in_=B[bk][:].rearrange("p (h d) -> p h d", h=NH)),
                          reads=[BK[bk]], writes=[VC[blk]])
                steps += [s_vmm, s_vev]
            for h in range(NH):
                bk = FB[h % 2]
                def s_qmm(h=h, bk=bk):
                    for cc in range(3):
                        kb.op(PE, lambda e, cc=cc: e.matmul(B[bk][:], lhsT=wuq[:, cc, h * 128:(h + 1) * 128],
                                                            rhs=qnT[:, cc, :], start=(cc == 0), stop=(cc == 2)),
                              reads=[WUQ, QNT], writes=[BK[bk]])
                def s_qev(h=h, bk=bk):
                    kb.op(DVE, lambda e: e.tensor_copy(out=Qt[0:64, h, :], in_=B[bk][0:64, :]),
                          reads=[BK[bk]], writes=[SCR[pp]])
                    rope_combine(bk, Qt[R, h, :], [SCR[pp]])
                steps += [s_qmm, s_qev]
            return steps

        pending = []
        bg = {"urgent": [], "ublocks": 1, "steps": [], "blocks_left": 1}

        def pull_background():
            if bg["urgent"]:
                k = -(-len(bg["urgent"]) // max(bg["ublocks"], 1))
                for _ in range(k):
                    bg["urgent"].pop(0)()
                bg["ublocks"] -= 1
            else:
                n = len(bg["steps"])
                if n:
                    k = -(-n // max(bg["blocks_left"], 1))
                    for _ in range(k):
                        bg["steps"].pop(0)()
            bg["blocks_left"] -= 1

        def attention_head(i, h):
            pp = i % 2
            Qt = qt_view(pp)
            nblk = (i + 1) * NSUB
            ob = 3 + (h % 2)

            def emit_S(j):
                q0 = max(j - i * NSUB, 0) * 128
                sb_ = j % 3
                kb.op(PE, lambda e, j=j, q0=q0, sb_=sb_: e.matmul(
                    B[sb_][:, q0:T1], lhsT=Kc[0:96, h, j * 128:(j + 1) * 128], rhs=Qt[0:96, h, q0:T1],
                    start=True, stop=True), reads=[KCT[h][j // NSUB], SCR[pp]], writes=[BK[sb_]])

            for j in range(min(2, nblk)):
                emit_S(j)
            for j in range(nblk):
                jj = j - i * NSUB
                q0 = max(jj, 0) * 128
                sb_ = j % 3
                pb = j % 4
                kb.op(ACT, lambda e, q0=q0, sb_=sb_, pb=pb: e.activation(out=pt[pb][:, q0:T1], in_=B[sb_][:, q0:T1],
                                                                         func=AF.Exp),
                      reads=[BK[sb_]], writes=[PT[pb]])
                if jj >= 0:
                    kb.op(POOL, lambda e, q0=q0, pb=pb: e.tensor_tensor(out=pt[pb][:, q0:q0 + 128],
                                                                        in0=pt[pb][:, q0:q0 + 128], in1=tri[:],
                                                                        op=ALU.mult),
                          reads=[PT[pb], TRI], writes=[PT[pb]])
                if j + 2 < nblk:
                    emit_S(j + 2)
                kb.op(PE, lambda e, j=j, q0=q0, pb=pb: e.matmul(
                    B[ob][0:65, q0:T1], lhsT=Vc[:, j, h, :], rhs=pt[pb][:, q0:T1],
                    start=(j == 0), stop=(j == nblk - 1)), reads=[VC[j], PT[pb]], writes=[BK[ob]])
                if j == 1 and pending:
                    pending.pop()()
                pull_background()

            def epilogue():
                oi = h % 2
                kb.op(ACT, lambda e: e.activation(out=o_sb[oi][:], in_=B[ob][0:65, :], func=AF.Copy),
                      reads=[BK[ob]], writes=[OSB[oi]])
                kb.op(ACT, lambda e: e.activation(out=o_sb[oi][64:65, :], in_=o_sb[oi][64:65, :], func=AF.Ln),
                      reads=[OSB[oi]], writes=[OSB[oi]])
                kb.op(ACT, lambda e: e.activation(out=o_sb[oi][64:65, :], in_=o_sb[oi][64:65, :], func=AF.Exp, scale=-1.0),
                      reads=[OSB[oi]], writes=[OSB[oi]])
                kb.op(PE, lambda e: e.matmul(B[5][0:64, :], lhsT=ones_f[64:65, 0:64], rhs=o_sb[oi][64:65, :],
                                             start=True, stop=True), reads=[ONES, OSB[oi]], writes=[BK[5]])
                ro = (h % 2) * 64
                kb.op(DVE, lambda e: e.tensor_tensor(out=Ya[ro:ro + 64, h // 2, :], in0=o_sb[oi][0:64, :], in1=B[5][0:64, :],
                                                     op=ALU.mult), reads=[OSB[oi], BK[5]], writes=[YA[h // 2]])

            if pending:
                pending.pop()()
            pending.append(epilogue)

        def tail_steps(b, i):
            tok0 = b * seq + i * T1
            TB_ = FB[1]
            steps = []
            for cc in range(4):
                def s_sq(cc=cc):
                    kb.op(POOL, lambda e: e.tensor_tensor(out=sq[:], in0=Ya[:, cc, :], in1=Ya[:, cc, :], op=ALU.mult),
                          reads=[YA[cc]], writes=[SQ])
                def s_on(cc=cc):
                    kb.op(PE, lambda e: e.matmul(B[TB_][:], lhsT=ones_b[:], rhs=sq[:], start=(cc == 0), stop=(cc == 3)),
                          reads=[ONES, SQ], writes=[BK[TB_]])
                steps += [s_sq, s_on]
            def s_ln():
                kb.op(ACT, lambda e: e.activation(out=rbc[:], in_=B[TB_][:], func=AF.Ln, bias=epsc[:, 0:1], scale=1.0 / 512),
                      reads=[BK[TB_], EPSC], writes=[RBC])
            def s_exp():
                kb.op(ACT, lambda e: e.activation(out=rbc[:], in_=rbc[:], func=AF.Exp, scale=-0.5), reads=[RBC], writes=[RBC])
            def s_norm():
                for cc in range(4):
                    kb.op(DVE, lambda e, cc=cc: e.tensor_tensor(out=Ya[:, cc, :], in0=Ya[:, cc, :], in1=rbc[:], op=ALU.mult),
                          reads=[YA[cc], RBC], writes=[YA[cc]])
                if use_ssm:
                    kb.dma(SP, Gn[:], gnsc[:, tok0:tok0 + T1].rearrange("(c p) t -> p c t", p=128), reads=[GNSC], writes=[GN])
            steps += [s_ln, s_exp, s_norm]
            for su in range(NSUB):
                xb_ = su % 2
                def s_xl(su=su, xb_=xb_):
                    kb.dma(SP, xs_[xb_][:], x[tok0 + su * 128:tok0 + (su + 1) * 128, :], writes=[XS[xb_]])
                steps.append(s_xl)
                for hf in range(2):
                    ob_ = FB[hf]
                    def s_mm(su=su, hf=hf, xb_=xb_, ob_=ob_):
                        nmm = 8 if use_ssm else 4
                        n = 0
                        if use_ssm:
                            for cc in range(4):
                                kb.op(PE, lambda e, cc=cc, n=n: e.matmul(
                                    B[ob_][:], lhsT=Gn[:, cc, su * 128:(su + 1) * 128], rhs=wout[:, cc, hf * 512:(hf + 1) * 512],
                                    start=(n == 0), stop=False), reads=[GN, WOUT], writes=[BK[ob_]])
                                n += 1
                        for cc in range(4):
                            kb.op(PE, lambda e, cc=cc, n=n: e.matmul(
                                B[ob_][:], lhsT=Ya[:, cc, su * 128:(su + 1) * 128], rhs=wout[:, 4 + cc, hf * 512:(hf + 1) * 512],
                                start=(n == 0), stop=(n == nmm - 1)), reads=[YA[cc], WOUT], writes=[BK[ob_]])
                            n += 1
                        kb.op(DVE, lambda e: e.tensor_tensor(out=xs_[xb_][:, hf * 512:(hf + 1) * 512],
                                                             in0=xs_[xb_][:, hf * 512:(hf + 1) * 512], in1=B[ob_][:], op=ALU.add),
                              reads=[XS[xb_], BK[ob_]], writes=[XS[xb_]])
                    steps += [s_mm]
                def s_st(su=su, xb_=xb_):
                    tok = kb.dma(SP, out[tok0 + su * 128:tok0 + (su + 1) * 128, :], xs_[xb_][:], reads=[XS[xb_]], writes=[OUTB])
                    if "p2" not in phases:
                        kb.out_tokens.append(tok)
                steps.append(s_st)
            return steps

        for b in range(nseq):
            load_featmajor(sh1[:], modsc[b:b + 1, 0:D], SH1, extra_reads=[MODSC])
            load_featmajor(sc1[:], modsc[b:b + 1, D:2 * D], SC1, extra_reads=[MODSC])
            kb.op(DVE, lambda e: e.scalar_tensor_tensor(out=G1[:], in0=sc1[:], scalar=1.0, in1=n1g[:],
                                                        op0=ALU.add, op1=ALU.mult), reads=[SC1, N1G], writes=[G1B])
            kb.dma(SP, gate1, bcast_rows(modsc[b:b + 1, 2 * D:3 * D], 128), reads=[MODSC], writes=[GATE1])
            for kc in range(8):
                kb.dma(SP, wstg, w_out[kc * 128:(kc + 1) * 128, :], writes=[WSTG])
                kb.op(DVE, lambda e, kc=kc: e.scalar_tensor_tensor(out=wout[:, kc, :], in0=wstg, scalar=og[:, kc:kc + 1],
                                                                   in1=gate1, op0=ALU.mult, op1=ALU.mult),
                      reads=[WSTG, OG, GATE1], writes=[WOUT])
            for st_ in front_steps(b, 0):
                st_()
            for i in range(nt1):
                bg["urgent"] = tail_steps(b, i - 1) if i > 0 else []
                bg["ublocks"] = (i + 1) * NSUB
                bg["steps"] = front_steps(b, i + 1) if i + 1 < nt1 else []
                bg["blocks_left"] = NH * (i + 1) * NSUB
                for h in range(NH):
                    attention_head(i, h)
                    assert not bg["urgent"]
                while bg["steps"]:
                    bg["steps"].pop(0)()
                if pending:
                    pending.pop()()
            for st_ in tail_steps(b, nt1 - 1):
                st_()
        kb.pop()

    if "p2" in phases:
        kb.push()
        TT = 256
        nt2 = ntok // TT
        src_x1 = out if ("p1" in phases) else x
        wff1 = kb.sb([128, 8, DFF], BF16, "wff1")
        wff2 = kb.sb([128, 32, D], BF16, "wff2")
        WFF1 = [Buf(f"wff1_{k}") for k in range(8)]
        WFF2 = [Buf(f"wff2_{k}") for k in range(8)]
        for cb_ in range(8):
            kb.dma(POOL, wff1[:, :, cb_ * 512:(cb_ + 1) * 512],
                   w_ff1[:, cb_ * 512:(cb_ + 1) * 512].rearrange("(k p) n -> p k n", p=128), writes=[WFF1[cb_]])
        for k in range(8):
            kb.dma(POOL, wff2[:, 4 * k:4 * k + 4, :],
                   w_ff2[k * 512:(k + 1) * 512, :].rearrange("(k p) n -> p k n", p=128), writes=[WFF2[k]])
        xt = [kb.sb([128, 2, D], F32, f"p2_xt{i}") for i in range(2)]
        XT = [Buf("xt0"), Buf("xt1")]
        xn = kb.sb([128, 2, D], BF16, "p2_xn")
        XN = Buf("xn")
        junk = kb.sb([128, D], BF16, "p2_junk")
        JUNK = Buf("junk")
        h2T = kb.sb([128, 8, TT], BF16, "p2_h2T")
        H2T = Buf("h2T")
        hid = kb.sb([128, 32, TT], BF16, "p2_hid")
        HID = [Buf(f"hid{j}") for j in range(32)]
        tmp = kb.sb([128, 512], F32, "p2_tmp")
        TMP = Buf("tmp")
        stat = [kb.sb([128, 8], F32, f"p2_stat{i}") for i in range(2)]
        SS = [Buf(), Buf()]; RS = [Buf(), Buf()]; SS2 = [Buf(), Buf()]; RS2 = [Buf(), Buf()]
        g2n = kb.sb([128, 8], F32, "p2_g2n")
        G2N = Buf()
        load_featmajor(g2n[:], norm2_g[0:1, :], G2N)
        fng = kb.sb([128, D], F32, "p2_fng")
        FNG = Buf()
        kb.dma(SP, fng[:], bcast_rows(final_norm_g[0:1, :], 128), writes=[FNG])
        sc2 = kb.sb([128, 8], F32, "p2_sc2")
        SC2 = Buf()
        sh2 = [kb.sb([128, 8], F32, f"p2_sh2_{i}") for i in range(2)]
        G2 = [kb.sb([128, 8], F32, f"p2_G2_{i}") for i in range(2)]
        gate2 = [kb.sb([128, D], F32, f"p2_gate2_{i}") for i in range(2)]
        FG = [kb.sb([128, D], F32, f"p2_FG_{i}") for i in range(2)]
        fsh = [kb.sb([128, D], F32, f"p2_fsh_{i}") for i in range(2)]
        SH2 = [Buf(), Buf()]; G2B = [Buf(), Buf()]; GATE2 = [Buf(), Buf()]; FGB = [Buf(), Buf()]; FSH = [Buf(), Buf()]
        tp = [kb.ps([128, 1024], BF16, f"p2_tp{i}") for i in range(2)]
        TP = [Buf(), Buf()]
        ps_h = [kb.ps([128, 512], F32, f"p2_psh{i}") for i in range(3)]
        PSH = [Buf() for _ in range(3)]
        ps_o = [kb.ps([128, 512], F32, f"p2_pso{i}") for i in range(2)]
        PSO = [Buf(), Buf()]

        loaded_seq = set()

        def seq_consts(b):
            if b in loaded_seq:
                return
            loaded_seq.add(b)
            bi = b % 2
            load_featmajor(sh2[bi][:], modsc[b:b + 1, 3 * D:4 * D], SH2[bi], extra_reads=[MODSC])
            load_featmajor(sc2[:], modsc[b:b + 1, 4 * D:5 * D], SC2, extra_reads=[MODSC])
            kb.op(DVE, lambda e: e.scalar_tensor_tensor(out=G2[bi][:], in0=sc2[:], scalar=1.0, in1=g2n[:],
                                                        op0=ALU.add, op1=ALU.mult), reads=[SC2, G2N], writes=[G2B[bi]])
            kb.dma(SP, gate2[bi][:], bcast_rows(modsc[b:b + 1, 5 * D:6 * D], 128), reads=[MODSC], writes=[GATE2[bi]])
            kb.dma(SP, FG[bi][:], bcast_rows(modsc[b:b + 1, 7 * D:8 * D], 128), reads=[MODSC], writes=[FGB[bi]])
            kb.dma(SP, fsh[bi][:], bcast_rows(modsc[b:b + 1, 6 * D:7 * D], 128), reads=[MODSC], writes=[FSH[bi]])
            kb.op(DVE, lambda e: e.scalar_tensor_tensor(out=FG[bi][:], in0=FG[bi][:], scalar=1.0, in1=fng[:],
                                                        op0=ALU.add, op1=ALU.mult), reads=[FGB[bi], FNG], writes=[FGB[bi]])

        def prep_a(t):
            b = (t * TT) // seq
            xi = t % 2
            st = stat[xi]
            seq_consts(b)
            kb.dma(SP, xt[xi][:], src_x1[t * TT:(t + 1) * TT, :].rearrange("(s p) n -> p s n", p=128),
                   reads=[OUTB], writes=[XT[xi]])
            kb.op(DVE, lambda e: e.memset(st[:, 0:2], 0.0), writes=[SS[xi]])
            for s_ in range(2):
                kb.op(ACT, lambda e, s_=s_: e.activation(out=junk[:], in_=xt[xi][:, s_, :], func=AF.Square,
                                                         accum_out=st[:, s_:s_ + 1]),
                      reads=[XT[xi]], writes=[JUNK, SS[xi]])
            rsqrt_cols(st[:, 2:4], st[:, 0:2], 1.0 / D, SS[xi], RS[xi])
            for s_ in range(2):
                kb.op(DVE, lambda e, s_=s_: e.tensor_scalar(out=xn[:, s_, :], in0=xt[xi][:, s_, :],
                                                            scalar1=st[:, 2 + s_:3 + s_], scalar2=None, op0=ALU.mult),
                      reads=[XT[xi], RS[xi]], writes=[XN])

        def prep_b(t):
            bi = ((t * TT) // seq) % 2
            for k in range(8):
                pi = k % 2
                for s_ in range(2):
                    kb.op(PE, lambda e, s_=s_, k=k, pi=pi: e.transpose(out=tp[pi][:, s_ * 128:(s_ + 1) * 128],
                                                                       in_=xn[:, s_, k * 128:(k + 1) * 128],
                                                                       identity=ident_b[:]),
                          reads=[XN, IDB], writes=[TP[pi]])
                kb.op(ACT, lambda e, k=k, pi=pi: e.activation(out=h2T[:, k, :], in_=tp[pi][:, 0:TT], func=AF.Identity,
                                                              bias=sh2[bi][:, k:k + 1], scale=G2[bi][:, k:k + 1]),
                      reads=[TP[pi], SH2[bi], G2B[bi]], writes=[H2T])

        prep_a(0)
        prep_b(0)
        for t in range(nt2):
            b = (t * TT) // seq
            bi = b % 2
            xi = t % 2
            st = stat[xi]
            for jj in range(16):
                pj = jj % 3
                for j2 in range(2):
                    j = 2 * jj + j2
                    for k in range(8):
                        kb.op(PE, lambda e, j=j, j2=j2, k=k, pj=pj: e.matmul(
                            ps_h[pj][:, j2 * TT:(j2 + 1) * TT], lhsT=wff1[:, k, j * 128:(j + 1) * 128],
                            rhs=h2T[:, k, :], start=(k == 0), stop=(k == 7)),
                              reads=[H2T, WFF1[j // 4]], writes=[PSH[pj]])
                kb.op(ACT, lambda e, jj=jj, pj=pj: e.activation(out=hid[:, 2 * jj:2 * jj + 2, :], in_=ps_h[pj][:],
                                                                func=AF.Relu),
                      reads=[PSH[pj]], writes=[HID[2 * jj], HID[2 * jj + 1]])
                kb.op(DVE, lambda e, jj=jj: e.tensor_tensor(out=hid[:, 2 * jj:2 * jj + 2, :], in0=hid[:, 2 * jj:2 * jj + 2, :],
                                                            in1=hid[:, 2 * jj:2 * jj + 2, :], op=ALU.mult),
                      reads=[HID[2 * jj], HID[2 * jj + 1]], writes=[HID[2 * jj], HID[2 * jj + 1]])
                if jj == 3 and t + 1 < nt2:
                    prep_a(t + 1)
            if t + 1 < nt2:
                prep_b(t + 1)
            for s_ in range(2):
                for hf in range(2):
                    po = (s_ * 2 + hf) % 2
                    for k in range(32):
                        kb.op(PE, lambda e, s_=s_, hf=hf, k=k, po=po: e.matmul(
                            ps_o[po][:], lhsT=hid[:, k, s_ * 128:(s_ + 1) * 128], rhs=wff2[:, k, hf * 512:(hf + 1) * 512],
                            start=(k == 0), stop=(k == 31)),
                              reads=[HID[k], WFF2[k // 4]], writes=[PSO[po]])
                    kb.op(DVE, lambda e, hf=hf, po=po: e.tensor_tensor(out=tmp[:], in0=ps_o[po][:],
                                                                       in1=gate2[bi][:, hf * 512:(hf + 1) * 512], op=ALU.mult),
                          reads=[PSO[po], GATE2[bi]], writes=[TMP])
                    kb.op(DVE, lambda e, s_=s_, hf=hf: e.tensor_tensor(out=xt[xi][:, s_, hf * 512:(hf + 1) * 512],
                                                                       in0=xt[xi][:, s_, hf * 512:(hf + 1) * 512],
                                                                       in1=tmp[:], op=ALU.add),
                          reads=[TMP, XT[xi]], writes=[XT[xi]])
            kb.op(DVE, lambda e: e.memset(st[:, 4:6], 0.0), writes=[SS2[xi]])
            for s_ in range(2):
                kb.op(ACT, lambda e, s_=s_: e.activation(out=junk[:], in_=xt[xi][:, s_, :], func=AF.Square,
                                                         accum_out=st[:, 4 + s_:5 + s_]),
                      reads=[XT[xi]], writes=[JUNK, SS2[xi]])
            rsqrt_cols(st[:, 6:8], st[:, 4:6], 1.0 / D, SS2[xi], RS2[xi])
            for s_ in range(2):
                kb.op(DVE, lambda e, s_=s_: e.scalar_tensor_tensor(out=xt[xi][:, s_, :], in0=xt[xi][:, s_, :],
                                                                   scalar=st[:, 6 + s_:7 + s_], in1=FG[bi][:],
                                                                   op0=ALU.mult, op1=ALU.mult),
                      reads=[XT[xi], RS2[xi], FGB[bi]], writes=[XT[xi]])
                kb.op(DVE, lambda e, s_=s_: e.tensor_tensor(out=xt[xi][:, s_, :], in0=xt[xi][:, s_, :], in1=fsh[bi][:],
                                                            op=ALU.add),
                      reads=[XT[xi], FSH[bi]], writes=[XT[xi]])
            tok = kb.dma(SP, out[t * TT:(t + 1) * TT, :].rearrange("(s p) n -> p s n", p=128), xt[xi][:],
                         reads=[XT[xi]], writes=[OUTB])
            kb.out_tokens.append(tok)
        kb.pop()

    for tok in kb.out_tokens:
        kb._wait(SP, tok)
    return nc, kb


_NC_CACHE = {}


def _consts():
    inv_freq = 10000.0 ** (-np.arange(0, QK_ROPE, 2, dtype=np.float64) / QK_ROPE)
    invf = np.array([inv_freq[(r % 32) % 16] / (2.0 * np.pi) for r in range(128)], dtype=np.float32).reshape(128, 1)
    tri = np.triu(np.ones((128, 128), dtype=np.float32))
    kr = np.array(list(range(7, -1, -1)) + list(range(0, -8, -1)) + list(range(0, 8)) + list(range(1, 9)), dtype=np.float32)
    kr32 = np.tile(kr[None, :], (128, 1))
    cramp = np.tile(np.arange(1, 65, dtype=np.float32)[None, :], (128, 1))
    tau = np.arange(128) // 16
    mask8 = (tau[None, :] >= tau[:, None]).astype(np.float32)
    sgn = np.concatenate([np.ones(64), -np.ones(64)]).astype(np.float32).reshape(128, 1)
    return {"ident": np.eye(128, dtype=np.float32), "invf": invf, "tri": tri, "kr32": kr32, "cramp": cramp,
            "mask8": mask8, "sgn": sgn}


def kernel(**inputs):
    n = 8
    if "full" not in _NC_CACHE:
        _NC_CACHE["full"] = build_program()
    nc = _NC_CACHE["full"]
    in_maps = []
    for i in range(n):
        m = _core_inputs(inputs, i, NSEQ, SEQ)
        in_maps.append(m)
    res = run_bass_kernel_spmd(nc, in_maps, core_ids=list(range(n)))
    outs = [np.asarray(r["out"]).reshape(NSEQ, SEQ, D) for r in res.results]
    return np.concatenate(outs, axis=0).astype(np.float32)


def _core_inputs(inputs, i, nseq, seq):
    g = lambda k: np.ascontiguousarray(np.asarray(inputs[k]))
    sl = slice(i * nseq, (i + 1) * nseq)
    m = {
        "x": np.ascontiguousarray(g("x")[sl, :seq].reshape(nseq * seq, D)),
        "c": g("c")[sl],
        "positions": np.ascontiguousarray(g("positions")[sl, :seq]).astype(np.int32),
        "ada_w": g("ada_w")[0], "ada_b": g("ada_b").reshape(1, -1), "norm1_g": g("norm1_g").reshape(1, -1),
        "w_in": g("w_in")[0],
        "ssm_lambda_re": g("ssm_lambda_re")[0], "ssm_lambda_im": g("ssm_lambda_im")[0],
        "ssm_b_re": g("ssm_b_re")[0], "ssm_b_im": g("ssm_b_im")[0],
        "ssm_c_re": g("ssm_c_re")[0], "ssm_c_im": g("ssm_c_im")[0],
        "ssm_d": g("ssm_d")[0], "ssm_log_dt": g("ssm_log_dt").reshape(1, -1),
        "w_glu": g("w_glu")[0], "q_norm_g": g("q_norm_g").reshape(1, -1), "w_uq": g("w_uq")[0],
        "kv_norm_g": g("kv_norm_g").reshape(1, -1), "w_ukv": g("w_ukv")[0],
        "ssm_out_g": g("ssm_out_g").reshape(1, -1), "attn_out_g": g("attn_out_g").reshape(1, -1),
        "w_out": g("w_out")[0], "norm2_g": g("norm2_g").reshape(1, -1),
        "w_ff1": g("w_ff1")[0], "w_ff2": g("w_ff2")[0],
        "final_ada_w": g("final_ada_w"), "final_ada_b": g("final_ada_b").reshape(1, -1),
        "final_norm_g": g("final_norm_g").reshape(1, -1),
    }
    m.update(_consts())
    return m
```

```python
import contextlib
import math
import numpy as np
import ml_dtypes
import concourse.bass as bass
import concourse.mybir as mybir
from concourse.bass_utils import run_bass_kernel_spmd

F32 = mybir.dt.float32
BF16 = mybir.dt.bfloat16
I32 = mybir.dt.int32
AF = mybir.ActivationFunctionType
ALU = mybir.AluOpType
AX = mybir.AxisListType

D = 1024
SEQ = 4096
NSEQ = 2
DFF = 4096
EPS = 1e-6
D_SSM = 512
Q_LORA = 384
KV_LORA = 256
QK_ROPE = 32
IN_COLS = 1184
NH = 8


class Buf:
    __slots__ = ("w", "r", "sem", "semcnt", "name")

    def __init__(self, name=""):
        self.w = None
        self.r = []
        self.sem = None
        self.semcnt = 0
        self.name = name


class Eng:
    def __init__(self, nc, name, handle, needed=None):
        self.name = name
        self.h = handle
        self.sem = nc.semaphore("prog_" + name).__enter__()
        self.cnt = 0
        self.incs = 0
        self.waited = {}
        self.needed = needed
        self.used = set()
        self.val = {}


class KB:
    def __init__(self, nc, needed=None):
        self.nc = nc
        nd = needed or {}
        self.PE = Eng(nc, "pe", nc.tensor, nd.get("pe"))
        self.ACT = Eng(nc, "act", nc.scalar, nd.get("act"))
        self.DVE = Eng(nc, "dve", nc.vector, nd.get("dve"))
        self.POOL = Eng(nc, "pool", nc.gpsimd, nd.get("pool"))
        self.SP = Eng(nc, "sp", nc.sync, nd.get("sp"))
        self.nsb = 0
        self.nps = 0
        self.out_tokens = []
        self.stacks = [contextlib.ExitStack()]
        self.dma_bufs = []

    def sb(self, shape, dt, name=None):
        self.nsb += 1
        return self.stacks[-1].enter_context(self.nc.sbuf_tensor(f"{name or 'sb'}_{self.nsb}", list(shape), dt))

    def ps(self, shape, dt, name=None):
        self.nps += 1
        return self.stacks[-1].enter_context(self.nc.psum_tensor(f"{name or 'ps'}_{self.nps}", list(shape), dt))

    def push(self):
        self.stacks.append(contextlib.ExitStack())

    def pop(self):
        engs = [self.PE, self.ACT, self.DVE, self.POOL, self.SP]
        for e in engs:
            for o in engs:
                if o.cnt > 0 and (o is not e or e is not self.PE):
                    self._wait(e, (o.sem, o.cnt, o))
            for b in self.dma_bufs:
                self._wait(e, (b.sem, b.semcnt, None))
        self.stacks.pop().close()

    def _wait(self, eng, tok):
        sem, val, owner = tok
        key = id(sem)
        if eng.waited.get(key, 0) >= val:
            return
        eng.waited[key] = val
        if owner is not None:
            owner.used.add(val)
            eng.h.wait_ge(sem, owner.val[val])
        else:
            eng.h.wait_ge(sem, val)

    def _deps(self, eng, reads, writes):
        for b in reads:
            if b.w is not None:
                if b.w[2] is eng and eng is self.PE:
                    continue
                self._wait(eng, b.w)
        for b in writes:
            if b.w is not None and b.w[2] is not eng:
                self._wait(eng, b.w)
            for t in b.r:
                if t[2] is not eng:
                    self._wait(eng, t)

    def op(self, eng, fn, reads=(), writes=()):
        self._deps(eng, reads, writes)
        ins = fn(eng.h)
        eng.cnt += 1
        if eng.needed is None or eng.cnt in eng.needed:
            eng.incs += 1
            ins.then_inc(eng.sem, 1)
            eng.val[eng.cnt] = eng.incs
        tok = (eng.sem, eng.cnt, eng)
        for b in reads:
            b.r.append(tok)
        for b in writes:
            b.w = tok
            b.r = []
        return tok

    def dma(self, eng, out, in_, reads=(), writes=(), track=None, **kw):
        self._deps(eng, reads, writes)
        tb = track or (writes[0] if writes else reads[0])
        if tb.sem is None:
            tb.sem = self.nc.semaphore("dma_" + str(id(tb))).__enter__()
            self.dma_bufs.append(tb)
        ins = eng.h.dma_start(out=out, in_=in_, **kw)
        tb.semcnt += 16
        ins.then_inc(tb.sem, 16)
        tok = (tb.sem, tb.semcnt, None)
        for b in reads:
            b.r.append(tok)
        for b in writes:
            b.w = tok
            b.r = []
        return tok


def bcast_rows(ap, n):
    return ap.partition_broadcast(n)


def build_program(nseq=NSEQ, seq=SEQ, phases=("p0", "p1", "p2"), dbg=None):
    _, kb1 = _build_once(nseq, seq, phases, dbg, None)
    needed = {e.name: set(e.used) for e in (kb1.PE, kb1.ACT, kb1.DVE, kb1.POOL, kb1.SP)}
    nc, _ = _build_once(nseq, seq, phases, dbg, needed)
    return nc


def _build_once(nseq, seq, phases, dbg, needed):
    nc = bass.Bass("TRN2", target_bir_lowering=False)
    ntok = nseq * seq

    def dram_in(name, shape, dt=F32):
        return nc.dram_tensor(name, list(shape), dt, kind="ExternalInput").ap()

    x = dram_in("x", [ntok, D])
    c = dram_in("c", [nseq, D])
    positions = dram_in("positions", [nseq, seq], I32)
    ada_w = dram_in("ada_w", [D, 6 * D])
    ada_b = dram_in("ada_b", [1, 6 * D])
    norm1_g = dram_in("norm1_g", [1, D])
    w_in = dram_in("w_in", [D, IN_COLS])
    lam_re = dram_in("ssm_lambda_re", [32, 64])
    lam_im = dram_in("ssm_lambda_im", [32, 64])
    b_re = dram_in("ssm_b_re", [32, 64, 16])
    b_im = dram_in("ssm_b_im", [32, 64, 16])
    c_re = dram_in("ssm_c_re", [32, 16, 64])
    c_im = dram_in("ssm_c_im", [32, 16, 64])
    ssm_d = dram_in("ssm_d", [32, 16])
    log_dt = dram_in("ssm_log_dt", [1, 32])
    w_glu = dram_in("w_glu", [D_SSM, 2 * D_SSM])
    q_norm_g = dram_in("q_norm_g", [1, Q_LORA])
    w_uq = dram_in("w_uq", [Q_LORA, 768])
    kv_norm_g = dram_in("kv_norm_g", [1, KV_LORA])
    w_ukv = dram_in("w_ukv", [KV_LORA, 1024])
    ssm_out_g = dram_in("ssm_out_g", [1, 512])
    attn_out_g = dram_in("attn_out_g", [1, 512])
    w_out = dram_in("w_out", [D, D])
    norm2_g = dram_in("norm2_g", [1, D])
    w_ff1 = dram_in("w_ff1", [D, DFF])
    w_ff2 = dram_in("w_ff2", [DFF, D])
    final_ada_w = dram_in("final_ada_w", [D, 2 * D])
    final_ada_b = dram_in("final_ada_b", [1, 2 * D])
    final_norm_g = dram_in("final_norm_g", [1, D])
    ident_in = dram_in("ident", [128, 128])
    invf_in = dram_in("invf", [128, 1])
    tri_in = dram_in("tri", [128, 128])
    kr32_in = dram_in("kr32", [128, 32])
    cramp_in = dram_in("cramp", [128, 64])
    mask8_in = dram_in("mask8", [128, 128])
    sgn_in = dram_in("sgn", [128, 1])

    out = nc.dram_tensor("out", [ntok, D], F32, kind="ExternalOutput").ap()
    modsc = nc.dram_tensor("modsc", [nseq, 8 * D], F32, kind="Internal").ap()
    ropesc = nc.dram_tensor("ropesc", [2, 32, ntok], F32, kind="Internal").ap()
    gnsc = nc.dram_tensor("gnsc", [D_SSM, ntok], BF16, kind="Internal").ap()
    ROPESC, GNSC = Buf("ropesc"), Buf("gnsc")
    MODSC = Buf("modsc")
    OUTB = Buf("out_hbm")

    kb = KB(nc, needed)
    PE, ACT, DVE, POOL, SP = kb.PE, kb.ACT, kb.DVE, kb.POOL, kb.SP

    ident_f = kb.sb([128, 128], F32, "ident_f")
    ident_b = kb.sb([128, 128], BF16, "ident_b")
    IDF, IDB = Buf("idf"), Buf("idb")
    kb.dma(SP, ident_f[:], ident_in[:, :], writes=[IDF])
    kb.op(DVE, lambda e: e.tensor_copy(out=ident_b[:], in_=ident_f[:]), reads=[IDF], writes=[IDB])

    epsc = kb.sb([128, 1], F32, "epsc")
    condTb = kb.sb([128, 8, nseq], BF16, "condTb")
    EPSC = Buf("eps")
    kb.op(DVE, lambda e: e.memset(epsc[:], EPS), writes=[EPSC])

    kb.push()
    cT = kb.sb([128, 8, nseq], F32, "cT")
    CT = Buf("cT")
    for k in range(8):
        kb.dma(SP, cT[:, k, :], c[:, k * 128:(k + 1) * 128].rearrange("b p -> p b"), writes=[CT],
               allow_slow_non_contiguous=True)
    condT = kb.sb([128, 8, nseq], F32, "condT")
    COND = Buf("cond")
    kb.op(ACT, lambda e: e.activation(out=condT[:], in_=cT[:], func=AF.Silu), reads=[CT], writes=[COND])

    NAW = 8
    aw = [kb.sb([128, 8, 512], BF16, f"aw{i}") for i in range(NAW)]
    CONDB = Buf()
    kb.op(DVE, lambda e: e.tensor_copy(out=condTb[:], in_=condT[:]), reads=[COND], writes=[CONDB])
    AW = [Buf(f"aw{i}") for i in range(NAW)]
    brow = [kb.sb([nseq, 512], F32, f"brow{i}") for i in range(NAW)]
    BROW = [Buf() for _ in range(NAW)]
    mrow = [kb.sb([nseq, 512], F32, f"mrow{i}") for i in range(NAW)]
    MROW = [Buf() for _ in range(NAW)]
    ps_mod = [kb.ps([128, 512], F32, f"ps_mod{i}") for i in range(NAW)]
    PSM = [Buf() for _ in range(NAW)]
    pieces = [(ada_w, ada_b, i * 512, i * 512) for i in range(12)] + \
             [(final_ada_w, final_ada_b, i * 512, 6 * D + i * 512) for i in range(4)]
    def load_piece(pi):
        wsrc, bsrc, coff, doff = pieces[pi]
        i = pi % NAW
        for kh in range(2):
            kb.dma(POOL, aw[i][:, 4 * kh:4 * kh + 4, :],
                   wsrc[512 * kh:512 * (kh + 1), coff:coff + 512].rearrange("(k p) n -> p k n", p=128), writes=[AW[i]])

    deferred = list(range(4, 16)) if "p0" in phases else []
    early = [pi for pi in range(16) if pi not in deferred]
    for pi in early[:NAW]:
        load_piece(pi)
    if "p1" in phases:
        kb.push()
        RC = min(2048, seq)
        invf = kb.sb([96, 1], F32, "invf")
        INVF = Buf()
        kb.dma(SP, invf[64:96, :], invf_in[64:96, :], writes=[INVF])
        posi = kb.sb([96, RC], I32, "posi")
        posf = kb.sb([96, RC], F32, "posf")
        yy = kb.sb([96, RC], F32, "rope_y")
        yi = kb.sb([96, RC], I32, "rope_yi")
        yf = kb.sb([96, RC], F32, "rope_yf")
        tab = kb.sb([96, RC], F32, "rope_tab")
        POSI, POSF, YY, YI, YF, TAB = Buf(), Buf(), Buf(), Buf(), Buf(), Buf()
        R = slice(64, 96)
        for b in range(nseq):
            for c0 in range(0, seq, RC):
                kb.dma(SP, posi[R, :], positions[b:b + 1, c0:c0 + RC].partition_broadcast(32), writes=[POSI])
                kb.op(DVE, lambda e: e.tensor_copy(out=posf[R, :], in_=posi[R, :]), reads=[POSI], writes=[POSF])
                for which, off in ((1, 0.0), (0, 0.25)):
                    kb.op(DVE, lambda e, off=off: e.tensor_scalar(out=yy[R, :], in0=posf[R, :], scalar1=invf[R, 0:1],
                                                                  scalar2=off, op0=ALU.mult, op1=ALU.add),
                          reads=[POSF, INVF], writes=[YY])
                    kb.op(DVE, lambda e: e.tensor_copy(out=yi[R, :], in_=yy[R, :]), reads=[YY], writes=[YI])
                    kb.op(DVE, lambda e: e.tensor_copy(out=yf[R, :], in_=yi[R, :]), reads=[YI], writes=[YF])
                    kb.op(DVE, lambda e: e.tensor_tensor(out=yy[R, :], in0=yy[R, :], in1=yf[R, :], op=ALU.subtract),
                          reads=[YY, YF], writes=[YY])
                    kb.op(ACT, lambda e: e.activation(out=tab[R, :], in_=yy[R, :], func=AF.Sin, scale=2.0 * math.pi),
                          reads=[YY], writes=[TAB])
                    kb.dma(SP, ropesc[which, :, b * seq + c0:b * seq + c0 + RC], tab[R, :], reads=[TAB], writes=[ROPESC])
        kb.pop()


    for pi in early:
        wsrc, bsrc, coff, doff = pieces[pi]
        i = pi % NAW
        if early.index(pi) >= NAW:
            load_piece(pi)
        kb.dma(SP, brow[i][:], bcast_rows(bsrc[0:1, coff:coff + 512], nseq), writes=[BROW[i]])
        for k in range(8):
            kb.op(PE, lambda e, k=k, i=i: e.matmul(ps_mod[i][0:nseq, :], lhsT=condTb[:, k, :], rhs=aw[i][:, k, :],
                                                    start=(k == 0), stop=(k == 7)),
                  reads=[CONDB, AW[i]], writes=[PSM[i]])
        kb.op(DVE, lambda e, i=i: e.tensor_tensor(out=mrow[i][:], in0=ps_mod[i][0:nseq, :], in1=brow[i][:], op=ALU.add),
              reads=[PSM[i], BROW[i]], writes=[MROW[i]])
        kb.dma(SP, modsc[:, doff:doff + 512], mrow[i][:], reads=[MROW[i]], writes=[MODSC])

    kb.pop()

    def rsqrt_cols(dst, src, inv_n, SRC, DST):
        kb.op(ACT, lambda e: e.activation(out=dst, in_=src, func=AF.Ln, bias=epsc[:, 0:1], scale=inv_n),
              reads=[SRC, EPSC], writes=[DST])
        kb.op(ACT, lambda e: e.activation(out=dst, in_=dst, func=AF.Exp, scale=-0.5), reads=[DST], writes=[DST])

    if dbg == "setup":
        kb.push()
        t_ = kb.sb([nseq, 8 * D], F32, "dbgt")
        T_ = Buf()
        kb.dma(SP, t_[:], modsc[:, :], reads=[MODSC], writes=[T_])
        for r in range(nseq):
            tok = kb.dma(SP, out[r:r + 1, :].rearrange("o (a n) -> (o a) n", a=1), t_[r:r + 1, 0:D], reads=[T_], writes=[OUTB])
            kb.out_tokens.append(tok)
            tok = kb.dma(SP, out[nseq + r:nseq + r + 1, :], t_[r:r + 1, 7 * D:8 * D], reads=[T_], writes=[OUTB])
            kb.out_tokens.append(tok)
        for tok in kb.out_tokens:
            kb._wait(SP, tok)
        kb.pop()
        return nc, kb

    def load_featmajor(dst_ap, src_row_ap, dstbuf, extra_reads=()):
        kb.dma(SP, dst_ap, src_row_ap.rearrange("o (k p) -> p (o k)", p=128), reads=list(extra_reads), writes=[dstbuf],
               allow_slow_non_contiguous=True)


    if "p0" in phases:
        kb.push()
        T0 = 512
        nt0 = seq // T0
        NC_ = T0 // 8
        TWO_PI = 2.0 * math.pi
        M1a = kb.sb([128, 32, 128], BF16, "M1a")
        M1b = kb.sb([128, 32, 128], BF16, "M1b")
        M2 = kb.sb([128, 32, 128], BF16, "M2")
        M3 = kb.sb([128, 32, 128], BF16, "M3")
        Tc = kb.sb([128, 32, NC_], F32, "Tc")
        Ts = kb.sb([128, 32, NC_], F32, "Ts")
        Rt = kb.sb([128, 32, NC_], F32, "Rt")
        Rho = kb.sb([128, 32], F32, "Rho")
        M1A, M1B, M2B, M3B, TCB, TSB, RTB, RHO = (Buf() for _ in range(8))
        Bk0 = [kb.ps([128, 512], F32, f"p0_bank{i}") for i in range(8)]
        BK0 = [Buf(f"p0bank{i}") for i in range(8)]

        kb.push()
        kr32 = kb.sb([128, 32], F32, "kr32")
        cramp = kb.sb([128, NC_], F32, "cramp")
        mask8 = kb.sb([128, 128], F32, "mask8")
        sgn = kb.sb([128, 1], F32, "sgn")
        KR32, CRAMP, MASK8, SGN = Buf(), Buf(), Buf(), Buf()
        kb.dma(SP, kr32[:], kr32_in[:, :], writes=[KR32])
        kb.dma(SP, cramp[:], cramp_in[:, 0:NC_], writes=[CRAMP])
        kb.dma(SP, mask8[:], mask8_in[:, :], writes=[MASK8])
        kb.dma(SP, sgn[:], sgn_in[:, :], writes=[SGN])

        def DV(fn, reads, writes):
            return kb.op(DVE, fn, reads=reads, writes=writes)

        def frac_(t_ap, shape, TB):
            kb.push()
            ti = kb.sb(shape, I32, "frac_i")
            tf = kb.sb(shape, F32, "frac_f")
            TI, TF = Buf(), Buf()
            DV(lambda e: e.tensor_copy(out=ti[:], in_=t_ap), [TB], [TI])
            DV(lambda e: e.tensor_copy(out=tf[:], in_=ti[:]), [TI], [TF])
            DV(lambda e: e.tensor_tensor(out=t_ap, in0=t_ap, in1=tf[:], op=ALU.subtract), [TB, TF], [TB])
            kb.pop()

        lam2 = kb.sb([32, 2, 128], F32, "lam2")
        LAM2 = Buf()
        for ri, src in enumerate((lam_re, lam_im)):
            for du in range(2):
                kb.dma(SP, lam2[:, ri, du * 64:(du + 1) * 64], src[:, :], writes=[LAM2])
        lamre2 = kb.sb([128, 32], F32, "lamre2")
        lamim2 = kb.sb([128, 32], F32, "lamim2")
        LRE, LIM = Buf(), Buf()
        for ri, (dst, DB) in enumerate(((lamre2, LRE), (lamim2, LIM))):
            kb.op(PE, lambda e, ri=ri: e.transpose(out=Bk0[ri][:, 0:32], in_=lam2[:, ri, :], identity=ident_f[0:32, 0:32]),
                  reads=[LAM2, IDF], writes=[BK0[ri]])
            DV(lambda e, ri=ri, dst=dst: e.tensor_copy(out=dst[:], in_=Bk0[ri][:, 0:32]), [BK0[ri]], [DB])
        dt2 = kb.sb([128, 32], F32, "dt2")
        DT2 = Buf()
        kb.dma(SP, dt2[:], log_dt[0:1, :].partition_broadcast(128), writes=[DT2])
        kb.op(ACT, lambda e: e.activation(out=dt2[:], in_=dt2[:], func=AF.Exp), reads=[DT2], writes=[DT2])
        th = kb.sb([128, 32], F32, "th")
        ld = kb.sb([128, 32], F32, "ld")
        TH, LD = Buf(), Buf()
        DV(lambda e: e.scalar_tensor_tensor(out=th[:], in0=lamim2[:], scalar=1.0 / TWO_PI, in1=dt2[:], op0=ALU.mult,
                                            op1=ALU.mult), [LIM, DT2], [TH])
        DV(lambda e: e.tensor_tensor(out=ld[:], in0=lamre2[:], in1=dt2[:], op=ALU.mult), [LRE, DT2], [LD])

        def powers(ramp_ap, nk, Wre, Wim, WRE, WIM, base_th, BTH, base_ld, BLD, RAMPB):
            shp = [128, 32, nk]
            kb.push()
            y = kb.sb(shp, F32, "pw_y")
            yc = kb.sb(shp, F32, "pw_yc")
            mg = kb.sb(shp, F32, "pw_mg")
            Y, YC, MG = Buf(), Buf(), Buf()
            thb = base_th.unsqueeze(2).to_broadcast(shp)
            ldb = base_ld.unsqueeze(2).to_broadcast(shp)
            rb = ramp_ap.unsqueeze(1).to_broadcast(shp)
            DV(lambda e: e.tensor_tensor(out=y[:], in0=thb, in1=rb, op=ALU.mult), [BTH, RAMPB], [Y])
            frac_(y[:], shp, Y)
            DV(lambda e: e.tensor_scalar(out=yc[:], in0=y[:], scalar1=0.25, scalar2=None, op0=ALU.add), [Y], [YC])
            frac_(yc[:], shp, YC)
            DV(lambda e: e.tensor_tensor(out=mg[:], in0=ldb, in1=rb, op=ALU.mult), [BLD, RAMPB], [MG])
            kb.op(ACT, lambda e: e.activation(out=mg[:], in_=mg[:], func=AF.Exp), reads=[MG], writes=[MG])
            kb.op(ACT, lambda e: e.activation(out=y[:], in_=y[:], func=AF.Sin, scale=TWO_PI), reads=[Y], writes=[Y])
            kb.op(ACT, lambda e: e.activation(out=yc[:], in_=yc[:], func=AF.Sin, scale=TWO_PI), reads=[YC], writes=[YC])
            DV(lambda e: e.tensor_tensor(out=Wre[:], in0=mg[:], in1=yc[:], op=ALU.mult), [MG, YC], [WRE])
            DV(lambda e: e.tensor_tensor(out=Wim[:], in0=mg[:], in1=y[:], op=ALU.mult), [MG, Y], [WIM])
            kb.pop()

        Wre = kb.sb([128, 32, 32], F32, "Wre")
        Wim = kb.sb([128, 32, 32], F32, "Wim")
        WRE, WIM = Buf(), Buf()
        powers(kr32[:], 32, Wre, Wim, WRE, WIM, th[:], TH, ld[:], LD, KR32)
        th8 = kb.sb([128, 32], F32, "th8")
        ld8 = kb.sb([128, 32], F32, "ld8")
        TH8, LD8 = Buf(), Buf()
        DV(lambda e: e.tensor_scalar(out=th8[:], in0=th[:], scalar1=8.0, scalar2=None, op0=ALU.mult), [TH], [TH8])
        frac_(th8[:], [128, 32], TH8)
        DV(lambda e: e.tensor_scalar(out=ld8[:], in0=ld[:], scalar1=8.0, scalar2=None, op0=ALU.mult), [LD], [LD8])
        kb.op(ACT, lambda e: e.activation(out=Rho[:], in_=ld8[:], func=AF.Exp), reads=[LD8], writes=[RHO])
        shpT = [128, 32, NC_]
        kb.push()
        ty = kb.sb(shpT, F32, "ty")
        TY = Buf()
        DV(lambda e: e.tensor_tensor(out=ty[:], in0=th8[:].unsqueeze(2).to_broadcast(shpT),
                                     in1=cramp[:].unsqueeze(1).to_broadcast(shpT), op=ALU.mult), [TH8, CRAMP], [TY])
        frac_(ty[:], shpT, TY)
        kb.op(ACT, lambda e: e.activation(out=Ts[:], in_=ty[:], func=AF.Sin, scale=TWO_PI), reads=[TY], writes=[TSB])
        DV(lambda e: e.tensor_scalar(out=Ts[:], in0=Ts[:], scalar1=sgn[:, 0:1], scalar2=None, op0=ALU.mult), [TSB, SGN], [TSB])
        DV(lambda e: e.tensor_scalar(out=ty[:], in0=ty[:], scalar1=0.25, scalar2=None, op0=ALU.add), [TY], [TY])
        frac_(ty[:], shpT, TY)
        kb.op(ACT, lambda e: e.activation(out=Tc[:], in_=ty[:], func=AF.Sin, scale=TWO_PI), reads=[TY], writes=[TCB])
        kb.pop()
        DV(lambda e: e.memset(Rt[:], 0.0), [], [RTB])
        DV(lambda e: e.tensor_copy(out=Rt[:, :, 1:NC_], in_=Rho[:].unsqueeze(2).to_broadcast([128, 32, NC_ - 1])),
           [RHO], [RTB])
        def small(name):
            return kb.sb([128, 32], F32, name), Buf()
        lr, LR = small("lr"); nre, NRE = small("nre"); nim, NIM = small("nim"); den, DEN = small("den")
        tq, TQ = small("tq"); kre, KRE = small("kre"); kim, KIM = small("kim")
        DV(lambda e: e.tensor_scalar(out=lr[:], in0=Wre[:, :, 17], scalar1=-1.0, scalar2=None, op0=ALU.add), [WRE], [LR])
        DV(lambda e: e.tensor_tensor(out=nre[:], in0=lr[:], in1=lamre2[:], op=ALU.mult), [LR, LRE], [NRE])
        DV(lambda e: e.tensor_tensor(out=tq[:], in0=Wim[:, :, 17], in1=lamim2[:], op=ALU.mult), [WIM, LIM], [TQ])
        DV(lambda e: e.tensor_tensor(out=nre[:], in0=nre[:], in1=tq[:], op=ALU.add), [NRE, TQ], [NRE])
        DV(lambda e: e.tensor_tensor(out=nim[:], in0=Wim[:, :, 17], in1=lamre2[:], op=ALU.mult), [WIM, LRE], [NIM])
        DV(lambda e: e.tensor_tensor(out=tq[:], in0=lr[:], in1=lamim2[:], op=ALU.mult), [LR, LIM], [TQ])
        DV(lambda e: e.tensor_tensor(out=nim[:], in0=nim[:], in1=tq[:], op=ALU.subtract), [NIM, TQ], [NIM])
        DV(lambda e: e.tensor_tensor(out=den[:], in0=lamre2[:], in1=lamre2[:], op=ALU.mult), [LRE], [DEN])
        DV(lambda e: e.tensor_tensor(out=tq[:], in0=lamim2[:], in1=lamim2[:], op=ALU.mult), [LIM], [TQ])
        DV(lambda e: e.tensor_tensor(out=den[:], in0=den[:], in1=tq[:], op=ALU.add), [DEN, TQ], [DEN])
        DV(lambda e: e.reciprocal(out=den[:], in_=den[:]), [DEN], [DEN])
        DV(lambda e: e.tensor_tensor(out=kre[:], in0=nre[:], in1=den[:], op=ALU.mult), [NRE, DEN], [KRE])
        DV(lambda e: e.tensor_tensor(out=kim[:], in0=nim[:], in1=den[:], op=ALU.mult), [NIM, DEN], [KIM])
        shB = [128, 32, 16]
        b2re = kb.sb(shB, F32, "b2re"); b2im = kb.sb(shB, F32, "b2im")
        B2 = Buf()
        for du in range(2):
            kb.dma(SP, b2re[du * 64:(du + 1) * 64, :, :], b_re.rearrange("g p h -> p g h"), writes=[B2])
            kb.dma(SP, b2im[du * 64:(du + 1) * 64, :, :], b_im.rearrange("g p h -> p g h"), writes=[B2])
        bbre = kb.sb(shB, F32, "bbre"); bbim = kb.sb(shB, F32, "bbim")
        tb1 = kb.sb(shB, F32, "tb1"); tb2 = kb.sb(shB, F32, "tb2")
        BBRE, BBIM, TB1, TB2 = Buf(), Buf(), Buf(), Buf()
        kreb = kre[:].unsqueeze(2).to_broadcast(shB)
        kimb = kim[:].unsqueeze(2).to_broadcast(shB)
        DV(lambda e: e.tensor_tensor(out=tb1[:], in0=b2re[:], in1=kreb, op=ALU.mult), [B2, KRE], [TB1])
        DV(lambda e: e.tensor_tensor(out=tb2[:], in0=b2im[:], in1=kimb, op=ALU.mult), [B2, KIM], [TB2])
        DV(lambda e: e.tensor_tensor(out=bbre[:], in0=tb1[:], in1=tb2[:], op=ALU.subtract), [TB1, TB2], [BBRE])
        DV(lambda e: e.tensor_tensor(out=tb1[:], in0=b2im[:], in1=kreb, op=ALU.mult), [B2, KRE], [TB1])
        DV(lambda e: e.tensor_tensor(out=tb2[:], in0=b2re[:], in1=kimb, op=ALU.mult), [B2, KIM], [TB2])
        DV(lambda e: e.tensor_tensor(out=bbim[:], in0=tb1[:], in1=tb2[:], op=ALU.add), [TB1, TB2], [BBIM])
        cdup = kb.sb([128, 4, 2, 128], F32, "cdup")
        CDUP = Buf()
        for j in range(4):
            for ri, src in enumerate((c_re, c_im)):
                for du in range(2):
                    kb.dma(SP, cdup[:, j, ri, du * 64:(du + 1) * 64],
                           src[j * 8:(j + 1) * 8, :, :].rearrange("g h p -> (g h) p"), writes=[CDUP])
        c2re = kb.sb(shB, F32, "c2re"); c2im = kb.sb(shB, F32, "c2im")
        C2RE, C2IM = Buf(), Buf()
        for j in range(4):
            for ri, (dst, DB) in enumerate(((c2re, C2RE), (c2im, C2IM))):
                bk = (j * 2 + ri) % 4
                kb.op(PE, lambda e, j=j, ri=ri, bk=bk: e.transpose(out=Bk0[bk][:, 0:128], in_=cdup[:, j, ri, :],
                                                                   identity=ident_f[:]),
                      reads=[CDUP, IDF], writes=[BK0[bk]])
                DV(lambda e, j=j, dst=dst, bk=bk: e.tensor_copy(
                    out=dst[:, j * 8:(j + 1) * 8, :], in_=Bk0[bk][:, 0:128].rearrange("p (g h) -> p g h", g=8)),
                   [BK0[bk]], [DB])
        drep = kb.sb([32, 8, 16], F32, "drep")
        DREP = Buf()
        for ta in range(8):
            kb.dma(SP, drep[:, ta, :], ssm_d[:, :], writes=[DREP])
        dcol = kb.sb([128, 32], F32, "dcol")
        DCOL = Buf()
        kb.op(PE, lambda e: e.transpose(out=Bk0[4][:, 0:32], in_=drep[:].rearrange("g t h -> g (t h)"),
                                        identity=ident_f[0:32, 0:32]), reads=[DREP, IDF], writes=[BK0[4]])
        DV(lambda e: e.tensor_copy(out=dcol[:], in_=Bk0[4][:, 0:32]), [BK0[4]], [DCOL])

        sh4 = [128, 32, 8, 16]
        pre = kb.sb(sh4, F32, "pre"); pim = kb.sb(sh4, F32, "pim")
        ta_ = kb.sb(sh4, F32, "cp_t1"); tb_ = kb.sb(sh4, F32, "cp_t2")
        arr = kb.sb(sh4, F32, "arr")
        PRE, PIM, TA_, TB_, ARR = Buf(), Buf(), Buf(), Buf(), Buf()

        def cprod(k0, vre, vim, VRE, VIM):
            wre_b = Wre[:, :, k0:k0 + 8].unsqueeze(3).to_broadcast(sh4)
            wim_b = Wim[:, :, k0:k0 + 8].unsqueeze(3).to_broadcast(sh4)
            vre_b = vre[:].unsqueeze(2).to_broadcast(sh4)
            vim_b = vim[:].unsqueeze(2).to_broadcast(sh4)
            DV(lambda e: e.tensor_tensor(out=ta_[:], in0=wre_b, in1=vre_b, op=ALU.mult), [WRE, VRE], [TA_])
            DV(lambda e: e.tensor_tensor(out=tb_[:], in0=wim_b, in1=vim_b, op=ALU.mult), [WIM, VIM], [TB_])
            DV(lambda e: e.tensor_tensor(out=pre[:], in0=ta_[:], in1=tb_[:], op=ALU.subtract), [TA_, TB_], [PRE])
            DV(lambda e: e.tensor_tensor(out=ta_[:], in0=wre_b, in1=vim_b, op=ALU.mult), [WRE, VIM], [TA_])
            DV(lambda e: e.tensor_tensor(out=tb_[:], in0=wim_b, in1=vre_b, op=ALU.mult), [WIM, VRE], [TB_])
            DV(lambda e: e.tensor_tensor(out=pim[:], in0=ta_[:], in1=tb_[:], op=ALU.add), [TA_, TB_], [PIM])

        def arrange(top, TOP, bot, BOT, bot_sign, dst_ap, DST):
            DV(lambda e: e.tensor_copy(out=dst_ap[0:64], in_=top[0:64]), [TOP], [DST])
            DV(lambda e: e.tensor_scalar(out=dst_ap[64:128], in0=bot[64:128], scalar1=bot_sign, scalar2=None, op0=ALU.mult),
               [BOT], [DST])

        def transposed_to(dstM, DSTM):
            for g4 in range(8):
                bk = g4 % 2
                for gl in range(4):
                    g = g4 * 4 + gl
                    kb.op(PE, lambda e, g=g, gl=gl, bk=bk: e.transpose(
                        out=Bk0[bk][:, gl * 128:(gl + 1) * 128], in_=arr[:, g, :, :].rearrange("p a b -> p (a b)"),
                        identity=ident_f[:]), reads=[ARR, IDF], writes=[BK0[bk]])
                DV(lambda e, g4=g4, bk=bk: e.tensor_copy(out=dstM[:, g4 * 4:(g4 + 1) * 4, :].rearrange("p a b -> p (a b)"),
                                                         in_=Bk0[bk][:]), [BK0[bk]], [DSTM])

        cprod(0, bbre, bbim, BBRE, BBIM)
        arrange(pre[:], PRE, pim[:], PIM, 1.0, arr[:], ARR)
        transposed_to(M1a, M1A)
        arrange(pim[:], PIM, pre[:], PRE, 1.0, arr[:], ARR)
        transposed_to(M1b, M1B)
        xarr = kb.sb(sh4, F32, "xarr")
        XARR = Buf()
        cprod(8, bbre, bbim, BBRE, BBIM)
        arrange(pre[:], PRE, pim[:], PIM, 1.0, xarr[:], XARR)
        cprod(16, c2re, c2im, C2RE, C2IM)
        arrange(pre[:], PRE, pim[:], PIM, -1.0, arr[:], ARR)
        m2t = kb.sb([128, 128], F32, "m2t")
        M2T = Buf()
        for g in range(32):
            bk = 2 + g % 2
            kb.op(PE, lambda e, g=g, bk=bk: e.matmul(Bk0[bk][:, 0:128], lhsT=xarr[:, g, :, :].rearrange("p a b -> p (a b)"),
                                                     rhs=arr[:, g, :, :].rearrange("p a b -> p (a b)"), start=True, stop=True),
                  reads=[XARR, ARR], writes=[BK0[bk]])
            DV(lambda e, bk=bk: e.tensor_tensor(out=m2t[:], in0=Bk0[bk][:, 0:128], in1=mask8[:], op=ALU.mult),
               [BK0[bk], MASK8], [M2T])
            DV(lambda e, g=g: e.scalar_tensor_tensor(out=M2[:, g, :], in0=ident_f[:], scalar=dcol[:, g:g + 1], in1=m2t[:],
                                                     op0=ALU.mult, op1=ALU.add), [IDF, DCOL, M2T], [M2B])
        cprod(24, c2re, c2im, C2RE, C2IM)
        arrange(pre[:], PRE, pim[:], PIM, -1.0, arr[:], ARR)
        DV(lambda e: e.tensor_copy(out=M3[:].rearrange("p g m -> p (g m)"), in_=arr[:].rearrange("p g a b -> p (g a b)")),
           [ARR], [M3B])
        kb.pop()

        w_in_u = kb.sb([128, 8, 512], BF16, "w_in_u")
        wglu = kb.sb([128, 4, 1024], BF16, "wglu")
        WINU, WGLU = Buf(), Buf()
        kb.dma(POOL, w_in_u[:], w_in[:, 0:512].rearrange("(k p) n -> p k n", p=128), writes=[WINU])
        kb.dma(POOL, wglu[:], w_glu[:, :].rearrange("(k p) n -> p k n", p=128), writes=[WGLU])
        n1g0 = kb.sb([128, 8], F32, "p0_n1g")
        N1G0 = Buf()
        load_featmajor(n1g0[:], norm1_g[0:1, :], N1G0)
        ones0 = kb.sb([128, 128], BF16, "p0_ones")
        ONES0 = Buf()
        DV(lambda e: e.memset(ones0[:], 1.0), [], [ONES0])
        xs0 = [kb.sb([128, D], F32, f"p0_xs{i}") for i in range(2)]
        XS0 = [Buf(), Buf()]
        xn0 = kb.sb([128, 4, D], BF16, "p0_xn")
        XN0 = Buf()
        hT0 = kb.sb([128, 8, T0], BF16, "p0_hT")
        HT0 = Buf()
        junk0 = kb.sb([128, D], BF16, "p0_junk")
        JUNK0 = Buf()
        st0 = kb.sb([128, 8], F32, "p0_stat")
        SS0, RS0 = Buf(), Buf()
        sh10 = kb.sb([128, 8], F32, "p0_sh1"); sc10 = kb.sb([128, 8], F32, "p0_sc1"); G10 = kb.sb([128, 8], F32, "p0_G1")
        SH10, SC10, G1B0 = Buf(), Buf(), Buf()
        u8g = kb.sb([64, 32, 8, 16], BF16, "u8g")
        U8G = Buf()
        U8 = [kb.sb([128, 32, NC_], BF16, f"U8_{i}") for i in range(2)]
        U8B = [[Buf() for _ in range(2)] for _ in range(2)]
        qa = kb.sb([128, 512], F32, "q_tA"); qb = kb.sb([128, 512], F32, "q_tB")
        wa = [kb.sb([128, 512], F32, f"q_wa{i}") for i in range(2)]
        wb = [kb.sb([128, 512], F32, f"q_wb{i}") for i in range(2)]
        za = [kb.sb([128, 512], F32, f"q_za{i}") for i in range(2)]
        zb = [kb.sb([128, 512], F32, f"q_zb{i}") for i in range(2)]
        sa = kb.sb([128, 512], F32, "q_sa")
        pe_ = kb.sb([128, 512], F32, "q_pe"); pf_ = kb.sb([128, 512], F32, "q_pf")
        QA, QB, SA, PEB, PFB = (Buf() for _ in range(5))
        WA = [Buf(), Buf()]; WB = [Buf(), Buf()]; ZA = [Buf(), Buf()]; ZB = [Buf(), Buf()]
        t8a = kb.sb([128, 8], F32, "t8a"); t8b = kb.sb([128, 8], F32, "t8b")
        t8c = kb.sb([128, 8], F32, "t8c"); t8d = kb.sb([128, 8], F32, "t8d")
        T8A, T8B, T8C, T8D = Buf(), Buf(), Buf(), Buf()
        Sa_prev = kb.sb([128, 32], F32, "Sa_prev"); Sb_prev = kb.sb([128, 32], F32, "Sb_prev")
        SAP, SBP = Buf(), Buf()
        Sbuf = kb.sb([128, 32, NC_], BF16, "Sbuf")
        SBUF = [Buf() for _ in range(4)]
        Y8g = kb.sb([128, 32, NC_], BF16, "Y8g")
        Y8G = [Buf() for _ in range(4)]
        y8tm = kb.sb([64, 8, 512], BF16, "y8tm")
        Y8TM = [Buf() for _ in range(4)]
        yT = kb.sb([128, 4, T0], BF16, "yT")
        YT = Buf()
        sg = [kb.sb([128, T0], F32, f"sg{i}") for i in range(2)]
        SG = [Buf(), Buf()]
        gT = kb.sb([128, 4, T0], BF16, "gT")
        GT = [Buf() for _ in range(4)]
        gsq = [kb.sb([128, T0], BF16, f"gsq{i}") for i in range(2)]
        GSQ = [Buf(), Buf()]
        rbc0 = kb.sb([128, T0], F32, "p0_rbc")
        RBC0 = Buf()
        Gn0 = kb.sb([128, 4, T0], BF16, "p0_Gn")
        aw_d = [kb.sb([128, 8, 512], BF16, f"p0_awd{i}") for i in range(2)]
        AWD = [Buf(), Buf()]
        brow_d = kb.sb([nseq, 512], F32, "p0_browd")
        mrow_d = kb.sb([nseq, 512], F32, "p0_mrowd")
        BROWD, MROWD = Buf(), Buf()
        mod_steps = []
        for n_, pi_ in enumerate(deferred):
            def mA(k_, n_=n_, pi_=pi_):
                wsrc, bsrc, coff, doff = pieces[pi_]
                kb.dma(POOL, aw_d[n_ % 2][:, k_, :], wsrc[k_ * 128:(k_ + 1) * 128, coff:coff + 512], writes=[AWD[n_ % 2]])
            def mB(n_=n_, pi_=pi_):
                wsrc, bsrc, coff, doff = pieces[pi_]
                kb.dma(SP, brow_d[:], bcast_rows(bsrc[0:1, coff:coff + 512], nseq), writes=[BROWD])
                for k in range(8):
                    kb.op(PE, lambda e, k=k: e.matmul(Bk0[2][0:nseq, :], lhsT=condTb[:, k, :], rhs=aw_d[n_ % 2][:, k, :],
                                                      start=(k == 0), stop=(k == 7)),
                          reads=[CONDB, AWD[n_ % 2]], writes=[BK0[2]])
                kb.op(DVE, lambda e: e.tensor_tensor(out=mrow_d[:], in0=Bk0[2][0:nseq, :], in1=brow_d[:], op=ALU.add),
                      reads=[BK0[2], BROWD], writes=[MROWD])
                kb.dma(SP, modsc[:, doff:doff + 512], mrow_d[:], reads=[MROWD], writes=[MODSC])
            mod_steps.append((mA, mB))
        mod_queue = []
        for n_ in range(len(mod_steps)):
            for k_ in range(8):
                mod_queue.append(lambda k_=k_, f=mod_steps[n_][0]: f(k_))
            if n_ >= 1:
                mod_queue.append(mod_steps[n_ - 1][1])
        if mod_steps:
            mod_queue.append(mod_steps[-1][1])
        GN0 = Buf()

        for b in range(nseq):
            load_featmajor(sh10[:], modsc[b:b + 1, 0:D], SH10, extra_reads=[MODSC])
            load_featmajor(sc10[:], modsc[b:b + 1, D:2 * D], SC10, extra_reads=[MODSC])
            DV(lambda e: e.scalar_tensor_tensor(out=G10[:], in0=sc10[:], scalar=1.0, in1=n1g0[:], op0=ALU.add, op1=ALU.mult),
               [SC10, N1G0], [G1B0])
            DV(lambda e: e.memset(Sa_prev[:], 0.0), [], [SAP])
            DV(lambda e: e.memset(Sb_prev[:], 0.0), [], [SBP])
            def front0_steps(i):
                tok0 = b * seq + i * T0
                U8_ = U8[i % 2]
                steps = []
                steps.append(lambda: DV(lambda e: e.memset(st0[:, 0:4], 0.0), [], [SS0]))
                for su in range(4):
                    xb_ = su % 2
                    def s_load(su=su, xb_=xb_):
                        kb.dma(SP, xs0[xb_][:], x[tok0 + su * 128:tok0 + (su + 1) * 128, :], writes=[XS0[xb_]])
                    def s_stat(su=su, xb_=xb_):
                        kb.op(ACT, lambda e: e.activation(out=junk0[:], in_=xs0[xb_][:], func=AF.Square,
                                                          accum_out=st0[:, su:su + 1]), reads=[XS0[xb_]], writes=[JUNK0, SS0])
                        rsqrt_cols(st0[:, 4 + su:5 + su], st0[:, su:su + 1], 1.0 / D, SS0, RS0)
                    def s_xn(su=su, xb_=xb_):
                        DV(lambda e: e.tensor_scalar(out=xn0[:, su, :], in0=xs0[xb_][:], scalar1=st0[:, 4 + su:5 + su],
                                                     scalar2=None, op0=ALU.mult), [XS0[xb_], RS0], [XN0])
                    steps += [s_load, s_stat, s_xn]
                for k in range(8):
                    def s_tr(k=k):
                        pi = k % 2
                        tpv = Bk0[pi][:].bitcast(BF16)
                        for su in range(4):
                            kb.op(PE, lambda e, su=su: e.transpose(out=tpv[:, su * 128:(su + 1) * 128],
                                                                   in_=xn0[:, su, k * 128:(k + 1) * 128], identity=ident_b[:]),
                                  reads=[XN0, IDB], writes=[BK0[pi]])
                        kb.op(ACT, lambda e: e.activation(out=hT0[:, k, :], in_=tpv[:, 0:T0], func=AF.Identity,
                                                          bias=sh10[:, k:k + 1], scale=G10[:, k:k + 1]),
                              reads=[BK0[pi], SH10, G1B0], writes=[HT0])
                    steps.append(s_tr)
                for ta in range(8):
                    def s_u8(ta=ta):
                        bk = 2
                        for k in range(8):
                            kb.op(PE, lambda e, k=k: e.matmul(Bk0[bk][0:NC_, :], lhsT=hT0[:, k, ta:T0:8], rhs=w_in_u[:, k, :],
                                                              start=(k == 0), stop=(k == 7)), reads=[HT0, WINU], writes=[BK0[bk]])
                        kb.op(ACT, lambda e: e.activation(out=u8g[:, :, ta, :],
                                                          in_=Bk0[bk][0:NC_, :].rearrange("p (g h) -> p g h", g=32),
                                                          func=AF.Copy), reads=[BK0[bk]], writes=[U8G])
                    steps.append(s_u8)
                for gh in range(2):
                    def s_U8(gh=gh):
                        tpv = Bk0[gh][:].bitcast(BF16)
                        for gl in range(16):
                            g = gh * 16 + gl
                            kb.op(PE, lambda e, g=g, gl=gl: e.transpose(
                                out=tpv[:, gl * NC_:(gl + 1) * NC_], in_=u8g[:, g, :, :].rearrange("p a b -> p (a b)"),
                                identity=ident_b[0:NC_, 0:NC_]), reads=[U8G, IDB], writes=[BK0[gh]])
                        kb.op(ACT, lambda e: e.activation(
                            out=U8_[:, gh * 16:(gh + 1) * 16, :].rearrange("p a b -> p (a b)"), in_=tpv[:, 0:16 * NC_],
                            func=AF.Copy), reads=[BK0[gh]], writes=[U8B[i % 2][gh]])
                    steps.append(s_U8)
                return steps

            bg0 = {"steps": [], "slots": 1}

            def pull0():
                if mod_queue:
                    mod_queue.pop(0)()
                n = len(bg0["steps"])
                if n:
                    k = -(-n // max(bg0["slots"], 1))
                    for _ in range(k):
                        bg0["steps"].pop(0)()
                bg0["slots"] -= 1

            for st_ in front0_steps(0):
                st_()
            for i in range(nt0):
                tok0 = b * seq + i * T0
                U8c = U8[i % 2]
                U8Bc = U8B[i % 2]
                bg0["steps"] = front0_steps(i + 1) if i + 1 < nt0 else []
                bg0["slots"] = 20
                def stage_A(qd):
                    la, lb = 4 + (qd % 2) * 2, 5 + (qd % 2) * 2
                    for gl in range(8):
                        g = qd * 8 + gl
                        kb.op(PE, lambda e, g=g, gl=gl: e.matmul(Bk0[la][:, gl * NC_:(gl + 1) * NC_], lhsT=M1a[:, g, :],
                                                                 rhs=U8c[:, g, :], start=True, stop=True),
                              reads=[M1A, U8Bc[g // 16]], writes=[BK0[la]])
                    for gl in range(8):
                        g = qd * 8 + gl
                        kb.op(PE, lambda e, g=g, gl=gl: e.matmul(Bk0[lb][:, gl * NC_:(gl + 1) * NC_], lhsT=M1b[:, g, :],
                                                                 rhs=U8c[:, g, :], start=True, stop=True),
                              reads=[M1B, U8Bc[g // 16]], writes=[BK0[lb]])

                def stage_B(qd):
                    la, lb = 4 + (qd % 2) * 2, 5 + (qd % 2) * 2
                    pq = qd % 2
                    gs = slice(qd * 8, (qd + 1) * 8)
                    TcQ = Tc[:, gs, :].rearrange("p a b -> p (a b)")
                    TsQ = Ts[:, gs, :].rearrange("p a b -> p (a b)")
                    RQ = Rt[:, gs, :].rearrange("p a b -> p (a b)")
                    La, Lb = Bk0[la][:], Bk0[lb][:]
                    wa_, wb_, za_, zb_ = wa[pq], wb[pq], za[pq], zb[pq]
                    WA_, WB_, ZA_, ZB_ = WA[pq], WB[pq], ZA[pq], ZB[pq]
                    DV(lambda e: e.tensor_tensor(out=qa[:], in0=La, in1=TcQ, op=ALU.mult), [BK0[la], TCB], [QA])
                    DV(lambda e: e.tensor_tensor(out=qb[:], in0=Lb, in1=TsQ, op=ALU.mult), [BK0[lb], TSB], [QB])
                    DV(lambda e: e.tensor_tensor(out=wa_[:], in0=qa[:], in1=qb[:], op=ALU.add), [QA, QB], [WA_])
                    DV(lambda e: e.tensor_tensor(out=qa[:], in0=Lb, in1=TcQ, op=ALU.mult), [BK0[lb], TCB], [QA])
                    DV(lambda e: e.tensor_tensor(out=qb[:], in0=La, in1=TsQ, op=ALU.mult), [BK0[la], TSB], [QB])
                    DV(lambda e: e.tensor_tensor(out=wb_[:], in0=qa[:], in1=qb[:], op=ALU.subtract), [QA, QB], [WB_])
                    wa3 = wa_[:].rearrange("p (g c) -> p g c", g=8)
                    wb3 = wb_[:].rearrange("p (g c) -> p g c", g=8)
                    za3 = za_[:].rearrange("p (g c) -> p g c", g=8)
                    zb3 = zb_[:].rearrange("p (g c) -> p g c", g=8)
                    sa3 = sa[:].rearrange("p (g c) -> p g c", g=8)
                    DV(lambda e: e.tensor_tensor(out=t8a[:], in0=Rho[:, gs], in1=Sa_prev[:, gs], op=ALU.mult), [RHO, SAP], [T8A])
                    DV(lambda e: e.tensor_tensor(out=wa3[:, :, 0], in0=wa3[:, :, 0], in1=t8a[:], op=ALU.add), [WA_, T8A], [WA_])
                    DV(lambda e: e.tensor_tensor(out=t8b[:], in0=Rho[:, gs], in1=Sb_prev[:, gs], op=ALU.mult), [RHO, SBP], [T8B])
                    DV(lambda e: e.tensor_tensor(out=wb3[:, :, 0], in0=wb3[:, :, 0], in1=t8b[:], op=ALU.add), [WB_, T8B], [WB_])
                    DV(lambda e: e.tensor_tensor_scan(out=za_[:], data0=RQ, data1=wa_[:], initial=0.0, op0=ALU.mult, op1=ALU.add),
                       [RTB, WA_], [ZA_])
                    DV(lambda e: e.tensor_tensor_scan(out=zb_[:], data0=RQ, data1=wb_[:], initial=0.0, op0=ALU.mult, op1=ALU.add),
                       [RTB, WB_], [ZB_])
                    PL = lambda fn, r, w: kb.op(POOL, fn, reads=r, writes=w)
                    PL(lambda e: e.tensor_tensor(out=pe_[:], in0=za_[:], in1=TcQ, op=ALU.mult), [ZA_, TCB], [PEB])
                    PL(lambda e: e.tensor_tensor(out=pf_[:], in0=zb_[:], in1=TsQ, op=ALU.mult), [ZB_, TSB], [PFB])
                    PL(lambda e: e.tensor_tensor(out=sa[:], in0=pe_[:], in1=pf_[:], op=ALU.subtract), [PEB, PFB], [SA])
                    PL(lambda e: e.tensor_copy(out=Sbuf[:, gs, 0], in_=Sa_prev[:, gs]), [SAP], [SBUF[qd]])
                    PL(lambda e: e.tensor_copy(out=Sbuf[:, gs, 1:NC_], in_=sa3[:, :, 0:NC_ - 1]), [SA], [SBUF[qd]])
                    PL(lambda e: e.tensor_tensor(out=t8c[:], in0=zb3[:, :, NC_ - 1], in1=Tc[:, gs, NC_ - 1], op=ALU.mult),
                       [ZB_, TCB], [T8C])
                    PL(lambda e: e.tensor_tensor(out=t8d[:], in0=za3[:, :, NC_ - 1], in1=Ts[:, gs, NC_ - 1], op=ALU.mult),
                       [ZA_, TSB], [T8D])
                    PL(lambda e: e.tensor_tensor(out=Sb_prev[:, gs], in0=t8c[:], in1=t8d[:], op=ALU.add), [T8C, T8D], [SBP])
                    PL(lambda e: e.tensor_copy(out=Sa_prev[:, gs], in_=sa3[:, :, NC_ - 1]), [SA], [SAP])

                def stage_C(qd):
                    gs = slice(qd * 8, (qd + 1) * 8)
                    yb = 2 + qd % 2
                    for gl in range(8):
                        g = qd * 8 + gl
                        kb.op(PE, lambda e, g=g, gl=gl, yb=yb: e.matmul(Bk0[yb][:, gl * NC_:(gl + 1) * NC_], lhsT=M2[:, g, :],
                                                                        rhs=U8c[:, g, :], start=True, stop=False),
                              reads=[M2B, U8Bc[g // 16]], writes=[BK0[yb]])
                        kb.op(PE, lambda e, g=g, gl=gl, yb=yb: e.matmul(Bk0[yb][:, gl * NC_:(gl + 1) * NC_], lhsT=M3[:, g, :],
                                                                        rhs=Sbuf[:, g, :], start=False, stop=True),
                              reads=[M3B, SBUF[qd]], writes=[BK0[yb]])
                    kb.op(ACT, lambda e, yb=yb: e.activation(out=Y8g[:, gs, :].rearrange("p a b -> p (a b)"), in_=Bk0[yb][:],
                                                             func=AF.Gelu_apprx_tanh), reads=[BK0[yb]], writes=[Y8G[qd]])
                    tb = qd % 2
                    tpv = Bk0[tb][:].bitcast(BF16)
                    for gl in range(8):
                        g = qd * 8 + gl
                        kb.op(PE, lambda e, g=g, gl=gl, tpv=tpv: e.transpose(out=tpv[0:NC_, gl * 128:(gl + 1) * 128],
                                                                             in_=Y8g[:, g, :], identity=ident_b[:]),
                              reads=[Y8G[qd], IDB], writes=[BK0[tb]])
                    kb.op(ACT, lambda e, qd=qd, tpv=tpv: e.activation(
                        out=y8tm[:, :, qd * 128:(qd + 1) * 128].rearrange("p t (g h) -> p g t h", g=8),
                        in_=tpv[0:NC_, 0:1024].rearrange("p (g t h) -> p g t h", g=8, t=8), func=AF.Copy),
                          reads=[BK0[tb]], writes=[Y8TM[qd]])

                stage_A(0)
                stage_A(1)
                pull0()
                for qd in range(4):
                    stage_B(qd)
                    pull0()
                    stage_C(qd)
                    pull0()
                    if qd + 2 < 4:
                        stage_A(qd + 2)
                        pull0()
                for j in range(4):
                    tb = j % 2
                    tpv = Bk0[tb][:].bitcast(BF16)
                    for t8 in range(8):
                        kb.op(PE, lambda e, j=j, t8=t8, tpv=tpv: e.transpose(out=tpv[:, t8 * NC_:(t8 + 1) * NC_],
                                                                             in_=y8tm[:, t8, j * 128:(j + 1) * 128],
                                                                             identity=ident_b[0:NC_, 0:NC_]),
                              reads=[Y8TM[j], IDB], writes=[BK0[tb]])
                    kb.op(ACT, lambda e, j=j, tpv=tpv: e.activation(out=yT[:, j, :].rearrange("p (c t) -> p t c", t=8),
                                                                    in_=tpv[:, 0:T0].rearrange("p (t c) -> p t c", t=8),
                                                                    func=AF.Copy), reads=[BK0[tb]], writes=[YT])
                    pull0()
                def glu_mm(n):
                    za_, zb_ = 4 + n % 2, 6 + n % 2
                    for cc in range(4):
                        kb.op(PE, lambda e, cc=cc: e.matmul(Bk0[za_][:], lhsT=wglu[:, cc, n * 128:(n + 1) * 128],
                                                            rhs=yT[:, cc, :], start=(cc == 0), stop=(cc == 3)),
                              reads=[WGLU, YT], writes=[BK0[za_]])
                    for cc in range(4):
                        kb.op(PE, lambda e, cc=cc: e.matmul(Bk0[zb_][:], lhsT=wglu[:, cc, 512 + n * 128:512 + (n + 1) * 128],
                                                            rhs=yT[:, cc, :], start=(cc == 0), stop=(cc == 3)),
                              reads=[WGLU, YT], writes=[BK0[zb_]])

                glu_mm(0)
                glu_mm(1)
                for n in range(4):
                    za_, zb_ = 4 + n % 2, 6 + n % 2
                    sg_, SG_ = sg[n % 2], SG[n % 2]
                    gq_, GQ_ = gsq[n % 2], GSQ[n % 2]
                    kb.op(ACT, lambda e: e.activation(out=sg_[:], in_=Bk0[zb_][:], func=AF.Sigmoid), reads=[BK0[zb_]], writes=[SG_])
                    DV(lambda e, n=n: e.tensor_tensor(out=gT[:, n, :], in0=Bk0[za_][:], in1=sg_[:], op=ALU.mult),
                       [BK0[za_], SG_], [GT[n]])
                    kb.op(POOL, lambda e, n=n: e.tensor_tensor(out=gq_[:], in0=gT[:, n, :], in1=gT[:, n, :], op=ALU.mult),
                          reads=[GT[n]], writes=[GQ_])
                    if n + 2 < 4:
                        glu_mm(n + 2)
                    kb.op(PE, lambda e, n=n: e.matmul(Bk0[3][:], lhsT=ones0[:], rhs=gq_[:], start=(n == 0), stop=(n == 3)),
                          reads=[ONES0, GQ_], writes=[BK0[3]])
                    pull0()
                kb.op(ACT, lambda e: e.activation(out=rbc0[:], in_=Bk0[3][:], func=AF.Ln, bias=epsc[:, 0:1], scale=1.0 / 512),
                      reads=[BK0[3], EPSC], writes=[RBC0])
                kb.op(ACT, lambda e: e.activation(out=rbc0[:], in_=rbc0[:], func=AF.Exp, scale=-0.5), reads=[RBC0], writes=[RBC0])
                for n in range(4):
                    DV(lambda e, n=n: e.tensor_tensor(out=Gn0[:, n, :], in0=gT[:, n, :], in1=rbc0[:], op=ALU.mult),
                       [GT[n], RBC0], [GN0])
                kb.dma(SP, gnsc[:, tok0:tok0 + T0].rearrange("(c p) t -> p c t", p=128), Gn0[:], reads=[GN0], writes=[GNSC])
                while bg0["steps"]:
                    bg0["steps"].pop(0)()
                if b == nseq - 1 and i == nt0 - 1:
                    while mod_queue:
                        mod_queue.pop(0)()
        kb.pop()

    if dbg == "p0_dump":
        kb.push()
        gd = kb.sb([128, 4, 1024], BF16, "gd")
        gf = kb.sb([128, 4, 1024], F32, "gf")
        GD, GF = Buf(), Buf()
        for blk in range(ntok // 1024):
            kb.dma(SP, gd[:], gnsc[:, blk * 1024:(blk + 1) * 1024].rearrange("(c p) t -> p c t", p=128), reads=[GNSC], writes=[GD])
            kb.op(DVE, lambda e: e.tensor_copy(out=gf[:], in_=gd[:]), reads=[GD], writes=[GF])
            tok = kb.dma(SP, out[blk * 512:(blk + 1) * 512, :].rearrange("(c p) t -> p c t", p=128), gf[:], reads=[GF], writes=[OUTB])
            kb.out_tokens.append(tok)
        for tok in kb.out_tokens:
            kb._wait(SP, tok)
        kb.pop()
        return nc, kb

    if "p1" in phases:
        kb.push()
        T1 = 512
        nt1 = seq // T1
        NSUB = T1 // 128
        s1, s2, s3 = D_SSM, D_SSM + Q_LORA, D_SSM + Q_LORA + KV_LORA
        w_in_a = kb.sb([128, 8, 672], BF16, "w_in_a")
        w_rot = kb.sb([128, 8, 128], BF16, "w_kr")
        wuq = kb.sb([128, 3, NH * 128], BF16, "wuq_c")
        Kw = kb.sb([128, 2, 512], BF16, "Kw")
        Vw = kb.sb([128, 2, 512], BF16, "Vw")
        wout = kb.sb([128, 8, D], BF16, "wout")
        WINA, WROT, WUQ, KW, VW, WOUT = Buf(), Buf(), Buf(), Buf(), Buf(), Buf()
        kb.dma(POOL, w_in_a[:], w_in[:, s1:IN_COLS].rearrange("(k p) n -> p k n", p=128), writes=[WINA])
        kb.op(DVE, lambda e: e.memset(w_rot[:], 0.0), writes=[WROT])
        kb.dma(POOL, w_rot[:, :, 64:96], w_in[:, s3:s3 + 32].rearrange("(k p) n -> p k n", p=128), writes=[WROT])
        kb.dma(POOL, w_rot[:, :, 96:112], w_in[:, s3 + 16:s3 + 32].rearrange("(k p) n -> p k n", p=128), writes=[WROT])
        kb.dma(POOL, w_rot[:, :, 112:128], w_in[:, s3:s3 + 16].rearrange("(k p) n -> p k n", p=128), writes=[WROT])
        kb.op(DVE, lambda e: e.tensor_scalar(out=w_rot[:, :, 96:112], in0=w_rot[:, :, 96:112], scalar1=-1.0, scalar2=None,
                                             op0=ALU.mult), reads=[WROT], writes=[WROT])
        qg = kb.sb([128, 3], F32, "qg")
        kvg = kb.sb([128, 2], F32, "kvg")
        og = kb.sb([128, 8], F32, "og")
        n1g = kb.sb([128, 8], F32, "n1g")
        QG, KVG, OG, N1G = Buf(), Buf(), Buf(), Buf()
        load_featmajor(qg[:], q_norm_g[0:1, :], QG)
        load_featmajor(kvg[:], kv_norm_g[0:1, :], KVG)
        load_featmajor(og[:, 0:4], ssm_out_g[0:1, :], OG)
        load_featmajor(og[:, 4:8], attn_out_g[0:1, :], OG)
        load_featmajor(n1g[:], norm1_g[0:1, :], N1G)
        kb.push()
        stg = kb.sb([128, 3, 1024], F32, "stg")
        STG = Buf()
        QSCALE = (64 + 32) ** -0.5
        kb.dma(SP, stg[:, :, 0:768], w_uq[:, :].rearrange("(k p) n -> p k n", p=128), writes=[STG])
        wq4 = wuq[:].rearrange("p c (h d) -> p c h d", h=NH)
        st4 = stg[:, :, 0:768].rearrange("p c (h d) -> p c h d", h=NH)
        for cc in range(3):
            kb.op(DVE, lambda e, cc=cc: e.tensor_scalar(out=wq4[:, cc, :, 0:96], in0=st4[:, cc, :, :], scalar1=qg[:, cc:cc + 1],
                                                        scalar2=QSCALE, op0=ALU.mult, op1=ALU.mult),
                  reads=[STG, QG], writes=[WUQ])
            kb.op(DVE, lambda e, cc=cc: e.tensor_scalar(out=wq4[:, cc, :, 96:112], in0=st4[:, cc, :, 80:96],
                                                        scalar1=qg[:, cc:cc + 1], scalar2=-QSCALE, op0=ALU.mult, op1=ALU.mult),
                  reads=[STG, QG], writes=[WUQ])
            kb.op(DVE, lambda e, cc=cc: e.tensor_scalar(out=wq4[:, cc, :, 112:128], in0=st4[:, cc, :, 64:80],
                                                        scalar1=qg[:, cc:cc + 1], scalar2=QSCALE, op0=ALU.mult, op1=ALU.mult),
                  reads=[STG, QG], writes=[WUQ])
        kb.dma(SP, stg[:, 0:2, :], w_ukv[:, :].rearrange("(k p) n -> p k n", p=128), reads=[STG], writes=[STG])
        st5 = stg[:, 0:2, :].rearrange("p c (h t d) -> p c h t d", h=NH, t=2)
        for cc in range(2):
            kb.op(DVE, lambda e, cc=cc: e.tensor_scalar(out=Kw[:, cc, :].rearrange("p (h d) -> p h d", h=NH),
                                                        in0=st5[:, cc, :, 0, :], scalar1=kvg[:, cc:cc + 1], scalar2=None,
                                                        op0=ALU.mult), reads=[STG, KVG], writes=[KW])
            kb.op(DVE, lambda e, cc=cc: e.tensor_scalar(out=Vw[:, cc, :].rearrange("p (h d) -> p h d", h=NH),
                                                        in0=st5[:, cc, :, 1, :], scalar1=kvg[:, cc:cc + 1], scalar2=None,
                                                        op0=ALU.mult), reads=[STG, KVG], writes=[VW])
        kb.pop()
        Kc = kb.sb([96, NH, seq], BF16, "Kc")
        Vc = kb.sb([128, seq // 128, NH, 65], BF16, "Vc")
        KC = [Buf(f"kc{h}") for h in range(NH)]
        VC = [Buf(f"vc{j}") for j in range(seq // 128)]
        kb.op(POOL, lambda e: e.memset(Vc[:], 1.0), writes=VC)
        ones_b = kb.sb([128, 128], BF16, "ones_b")
        ones_f = kb.sb([128, 64], F32, "ones_f")
        tri = kb.sb([128, 128], BF16, "tri")
        ONES, TRI = Buf(), Buf()
        kb.op(DVE, lambda e: e.memset(ones_b[:], 1.0), writes=[ONES])
        kb.op(DVE, lambda e: e.memset(ones_f[:], 1.0), writes=[ONES])
        kb.dma(POOL, tri[:], tri_in[:, :], writes=[TRI])
        KCT = [[Buf(f"kc{h}_{i}") for i in range(nt1)] for h in range(NH)]
        xs_ = [kb.sb([128, D], F32, f"p1_xs{i}") for i in range(2)]
        XS = [Buf(), Buf()]
        scr = [kb.sb([128, 4, D], BF16, f"p1_scr{i}") for i in range(2)]
        SCR = [Buf(), Buf()]

        def qt_view(pp):
            return scr[pp][0:96, :, :].rearrange("p a b -> p (a b)")[:, 0:NH * T1].rearrange("p (h t) -> p h t", h=NH)
        hT = kb.sb([128, 8, T1], BF16, "p1_hT")
        HT = Buf()
        st1 = kb.sb([128, 8], F32, "p1_stat")
        SS1, RS1 = Buf(), Buf()
        sh1 = kb.sb([128, 8], F32, "p1_sh1")
        sc1 = kb.sb([128, 8], F32, "p1_sc1")
        G1 = kb.sb([128, 8], F32, "p1_G1")
        SH1, SC1, G1B = Buf(), Buf(), Buf()
        qnT = kb.sb([128, 3, T1], BF16, "p1_qnT")
        kvnT = kb.sb([128, 2, T1], BF16, "p1_kvnT")
        QNT, KVNT = Buf(), Buf()
        sq = kb.sb([128, T1], BF16, "p1_sq")
        SQ = Buf()
        rbc = kb.sb([128, T1], F32, "p1_rbc")
        RBC = Buf()
        junk1 = rbc[:].bitcast(BF16)
        JUNK1 = RBC
        cosT = kb.sb([96, T1], F32, "p1_cos")
        sinT = kb.sb([128, T1], F32, "p1_sin")
        COS, SIN = Buf(), Buf()
        t1 = kb.sb([96, T1], F32, "p1_t1")
        t2 = kb.sb([96, T1], F32, "p1_t2")
        T1B, T2B = Buf(), Buf()
        kr = kb.sb([96, T1], BF16, "p1_kr")
        KR = Buf()
        pt = [kb.sb([128, T1], BF16, f"p1_pt{i}") for i in range(4)]
        PT = [Buf() for _ in range(4)]
        o_sb = [kb.sb([65, T1], F32, f"p1_osb{i}") for i in range(2)]
        OSB = [Buf(), Buf()]
        Ya = kb.sb([128, 4, T1], BF16, "p1_Ya")
        YA = [Buf() for _ in range(4)]
        Gn = kb.sb([128, 4, T1], BF16, "p1_Gn")
        GN = Buf()
        gate1 = Gn[:].rearrange("p a b -> p (a b)").bitcast(F32)
        wstg = hT[:].rearrange("p a b -> p (a b)").bitcast(F32)[:, 0:D]
        GATE1, WSTG = GN, HT
        B = [kb.ps([128, 512], F32, f"p1_bank{i}") for i in range(8)]
        BK = [Buf(f"bank{i}") for i in range(8)]
        FB = (6, 7)
        R = slice(64, 96)
        use_ssm = "p0" in phases

        def rope_combine(bk, dst_ap, DSTS):
            kb.op(DVE, lambda e: e.tensor_tensor(out=t1[R, :], in0=B[bk][R, :], in1=cosT[R, :], op=ALU.mult),
                  reads=[BK[bk], COS], writes=[T1B])
            kb.op(DVE, lambda e: e.tensor_tensor(out=t2[R, :], in0=B[bk][96:128, :], in1=sinT[96:128, :], op=ALU.mult),
                  reads=[BK[bk], SIN], writes=[T2B])
            kb.op(DVE, lambda e: e.tensor_tensor(out=dst_ap, in0=t1[R, :], in1=t2[R, :], op=ALU.add),
                  reads=[T1B, T2B], writes=DSTS)

        def latent_steps(col0, nchunk, dstT, DST, nlat):
            cb, sb_ = FB
            steps = []
            for cc in range(nchunk):
                def s_mm(cc=cc):
                    for k in range(8):
                        kb.op(PE, lambda e, k=k: e.matmul(
                            B[cb][:], lhsT=w_in_a[:, k, col0 + cc * 128:col0 + (cc + 1) * 128], rhs=hT[:, k, :],
                            start=(k == 0), stop=(k == 7)), reads=[WINA, HT], writes=[BK[cb]])
                def s_sq():
                    kb.op(ACT, lambda e: e.activation(out=sq[:], in_=B[cb][:], func=AF.Square), reads=[BK[cb]], writes=[SQ])
                def s_ones(cc=cc):
                    kb.op(PE, lambda e: e.matmul(B[sb_][:], lhsT=ones_b[:], rhs=sq[:], start=(cc == 0),
                                                 stop=(cc == nchunk - 1)), reads=[ONES, SQ], writes=[BK[sb_]])
                steps += [s_mm, s_sq, s_ones]
            def s_ln():
                kb.op(ACT, lambda e: e.activation(out=rbc[:], in_=B[sb_][:], func=AF.Ln, bias=epsc[:, 0:1], scale=1.0 / nlat),
                      reads=[BK[sb_], EPSC], writes=[RBC])
            def s_exp():
                kb.op(ACT, lambda e: e.activation(out=rbc[:], in_=rbc[:], func=AF.Exp, scale=-0.5), reads=[RBC], writes=[RBC])
            steps += [s_ln, s_exp]
            for cc in range(nchunk):
                bk = FB[cc % 2]
                def s_mm2(cc=cc, bk=bk):
                    for k in range(8):
                        kb.op(PE, lambda e, k=k: e.matmul(
                            B[bk][:], lhsT=w_in_a[:, k, col0 + cc * 128:col0 + (cc + 1) * 128], rhs=hT[:, k, :],
                            start=(k == 0), stop=(k == 7)), reads=[WINA, HT], writes=[BK[bk]])
                def s_scale(cc=cc, bk=bk):
                    kb.op(DVE, lambda e: e.tensor_tensor(out=dstT[:, cc, :], in0=B[bk][:], in1=rbc[:], op=ALU.mult),
                          reads=[BK[bk], RBC], writes=[DST])
                steps += [s_mm2, s_scale]
            return steps

        def front_steps(b, i):
            tok0 = b * seq + i * T1
            pp = i % 2
            xn1 = scr[pp]
            Qt = qt_view(pp)
            cols = slice(i * T1, (i + 1) * T1)
            steps = []

            def s_tables():
                kb.dma(SP, cosT[R, :], ropesc[0, :, tok0:tok0 + T1], reads=[ROPESC], writes=[COS])
                kb.dma(SP, sinT[96:128, :], ropesc[1, :, tok0:tok0 + T1], reads=[ROPESC], writes=[SIN])
                kb.op(DVE, lambda e: e.memset(st1[:, 0:4], 0.0), writes=[SS1])
            steps.append(s_tables)
            for su in range(NSUB):
                xb_ = su % 2
                def s_load(su=su, xb_=xb_):
                    kb.dma(SP, xs_[xb_][:], x[tok0 + su * 128:tok0 + (su + 1) * 128, :], writes=[XS[xb_]])
                def s_stat(su=su, xb_=xb_):
                    kb.op(ACT, lambda e: e.activation(out=junk1, in_=xs_[xb_][:], func=AF.Square,
                                                      accum_out=st1[:, su:su + 1]), reads=[XS[xb_]], writes=[JUNK1, SS1])
                    rsqrt_cols(st1[:, 4 + su:5 + su], st1[:, su:su + 1], 1.0 / D, SS1, RS1)
                def s_xn(su=su, xb_=xb_):
                    kb.op(DVE, lambda e: e.tensor_scalar(out=xn1[:, su, :], in0=xs_[xb_][:], scalar1=st1[:, 4 + su:5 + su],
                                                         scalar2=None, op0=ALU.mult), reads=[XS[xb_], RS1], writes=[SCR[pp]])
                steps += [s_load, s_stat, s_xn]
            for k in range(8):
                bk = FB[k % 2]
                def s_tr(k=k, bk=bk):
                    tpv = B[bk][:].bitcast(BF16)
                    for su in range(NSUB):
                        kb.op(PE, lambda e, su=su: e.transpose(out=tpv[:, su * 128:(su + 1) * 128],
                                                               in_=xn1[:, su, k * 128:(k + 1) * 128], identity=ident_b[:]),
                              reads=[SCR[pp], IDB], writes=[BK[bk]])
                def s_ev(k=k, bk=bk):
                    tpv = B[bk][:].bitcast(BF16)
                    kb.op(DVE, lambda e: e.tensor_scalar(out=hT[:, k, :], in0=tpv[:, 0:T1], scalar1=G1[:, k:k + 1],
                                                         scalar2=sh1[:, k:k + 1], op0=ALU.mult, op1=ALU.add),
                          reads=[BK[bk], SH1, G1B], writes=[HT])
                steps += [s_tr, s_ev]
            steps += latent_steps(0, 3, qnT, QNT, Q_LORA)
            steps += latent_steps(Q_LORA, 2, kvnT, KVNT, KV_LORA)

            def s_kr1():
                for k in range(8):
                    kb.op(PE, lambda e, k=k: e.matmul(B[FB[0]][:], lhsT=w_rot[:, k, :], rhs=hT[:, k, :],
                                                      start=(k == 0), stop=(k == 7)), reads=[WROT, HT], writes=[BK[FB[0]]])
            def s_kr3():
                rope_combine(FB[0], kr[R, :], [KR])
                kb.op(DVE, lambda e: e.tensor_copy(out=Kc[R, :, cols], in_=kr[R, :].unsqueeze(1).to_broadcast([32, NH, T1])),
                      reads=[KR], writes=[KCT[h][i] for h in range(NH)])
            steps += [s_kr1, s_kr3]
            for hp in range(4):
                bk = FB[hp % 2]
                def s_kmm(hp=hp, bk=bk):
                    for cc in range(2):
                        kb.op(PE, lambda e, cc=cc: e.matmul(B[bk][:], lhsT=Kw[:, cc, hp * 128:(hp + 1) * 128], rhs=kvnT[:, cc, :],
                                                            start=(cc == 0), stop=(cc == 1)), reads=[KW, KVNT], writes=[BK[bk]])
                def s_kev(hp=hp, bk=bk):
                    kb.op(DVE, lambda e: e.tensor_copy(out=Kc[0:64, 2 * hp, cols], in_=B[bk][0:64, :]),
                          reads=[BK[bk]], writes=[KCT[2 * hp][i]])
                    kb.op(DVE, lambda e: e.tensor_copy(out=Kc[0:64, 2 * hp + 1, cols], in_=B[bk][64:128, :]),
                          reads=[BK[bk]], writes=[KCT[2 * hp + 1][i]])
                steps += [s_kmm, s_kev]
            for su in range(NSUB):
                bk = FB[su % 2]
                blk = i * NSUB + su
                def s_vmm(su=su, bk=bk):
                    for cc in range(2):
                        kb.op(PE, lambda e, cc=cc: e.matmul(B[bk][:], lhsT=kvnT[:, cc, su * 128:(su + 1) * 128], rhs=Vw[:, cc, :],
                                                            start=(cc == 0), stop=(cc == 1)), reads=[KVNT, VW], writes=[BK[bk]])
                def s_vev(bk=bk, blk=blk):
                    kb.op(DVE, lambda e: e.tensor_copy(out=Vc[:, blk, :, 0:64], in_=B[bk][:].rearrange("p (h d) -> p h d", h=NH)),
                          reads=[BK[bk]], writes=[VC[blk]])
                steps += [s_vmm, s_vev]
            for h in range(NH):
                bk = FB[h % 2]
                def s_qmm(h=h, bk=bk):
                    for cc in range(3):
                        kb.op(PE, lambda e, cc=cc: e.matmul(B[bk][:], lhsT=wuq[:, cc, h * 128:(h + 1) * 128],
                                                            rhs=qnT[:, cc, :], start=(cc == 0), stop=(cc == 2)),
                              reads=[WUQ, QNT], writes=[BK[bk]])
                def s_qev(h=h, bk=bk):
                    kb.op(DVE, lambda e: e.tensor_copy(out=Qt[0:64, h, :], in_=B[bk][0:64, :]),
                          reads=[BK[bk]], writes=[SCR[pp]])
                    rope_combine(bk, Qt[R, h, :], [SCR[pp]])
                steps += [s_qmm, s_qev]
            return steps

        pending = []
        bg = {"urgent": [], "ublocks": 1, "steps": [], "blocks_left": 1}

        def pull_background():
            if bg["urgent"]:
                k = -(-len(bg["urgent"]) // max(bg["ublocks"], 1))
                for _ in range(k):
                    bg["urgent"].pop(0)()
                bg["ublocks"] -= 1
            else:
                n = len(bg["steps"])
                if n:
                    k = -(-n // max(bg["blocks_left"], 1))
                    for _ in range(k):
                        bg["steps"].pop(0)()
            bg["blocks_left"] -= 1

        def attention_head(i, h):
            pp = i % 2
            Qt = qt_view(pp)
            nblk = (i + 1) * NSUB
            ob = 3 + (h % 2)

            def emit_S(j):
                q0 = max(j - i * NSUB, 0) * 128
                sb_ = j % 3
                kb.op(PE, lambda e, j=j, q0=q0, sb_=sb_: e.matmul(
                    B[sb_][:, q0:T1], lhsT=Kc[0:96, h, j * 128:(j + 1) * 128], rhs=Qt[0:96, h, q0:T1],
                    start=True, stop=True), reads=[KCT[h][j // NSUB], SCR[pp]], writes=[BK[sb_]])

            for j in range(min(2, nblk)):
                emit_S(j)
            for j in range(nblk):
                jj = j - i * NSUB
                q0 = max(jj, 0) * 128
                sb_ = j % 3
                pb = j % 4
                kb.op(ACT, lambda e, q0=q0, sb_=sb_, pb=pb: e.activation(out=pt[pb][:, q0:T1], in_=B[sb_][:, q0:T1],
                                                                         func=AF.Exp),
                      reads=[BK[sb_]], writes=[PT[pb]])
                if jj >= 0:
                    kb.op(POOL, lambda e, q0=q0, pb=pb: e.tensor_tensor(out=pt[pb][:, q0:q0 + 128],
                                                                        in0=pt[pb][:, q0:q0 + 128], in1=tri[:],
                                                                        op=ALU.mult),
                          reads=[PT[pb], TRI], writes=[PT[pb]])
                if j + 2 < nblk:
                    emit_S(j + 2)
                kb.op(PE, lambda e, j=j, q0=q0, pb=pb: e.matmul(
                    B[ob][0:65, q0:T1], lhsT=Vc[:, j, h, :], rhs=pt[pb][:, q0:T1],
                    start=(j == 0), stop=(j == nblk - 1)), reads=[VC[j], PT[pb]], writes=[BK[ob]])
                if j == 1 and pending:
                    pending.pop()()
                pull_background()

            def epilogue():
                oi = h % 2
                kb.op(ACT, lambda e: e.activation(out=o_sb[oi][:], in_=B[ob][0:65, :], func=AF.Copy),
                      reads=[BK[ob]], writes=[OSB[oi]])
                kb.op(ACT, lambda e: e.activation(out=o_sb[oi][64:65, :], in_=o_sb[oi][64:65, :], func=AF.Ln),
                      reads=[OSB[oi]], writes=[OSB[oi]])
                kb.op(ACT, lambda e: e.activation(out=o_sb[oi][64:65, :], in_=o_sb[oi][64:65, :], func=AF.Exp, scale=-1.0),
                      reads=[OSB[oi]], writes=[OSB[oi]])
                kb.op(PE, lambda e: e.matmul(B[5][0:64, :], lhsT=ones_f[64:65, 0:64], rhs=o_sb[oi][64:65, :],
                                             start=True, stop=True), reads=[ONES, OSB[oi]], writes=[BK[5]])
                ro = (h % 2) * 64
                kb.op(DVE, lambda e: e.tensor_tensor(out=Ya[ro:ro + 64, h // 2, :], in0=o_sb[oi][0:64, :], in1=B[5][0:64, :],
                                                     op=ALU.mult), reads=[OSB[oi], BK[5]], writes=[YA[h // 2]])

            if pending:
                pending.pop()()
            pending.append(epilogue)

        def tail_steps(b, i):
            tok0 = b * seq + i * T1
            TB_ = FB[1]
            steps = []
            for cc in range(4):
                def s_sq(cc=cc):
                    kb.op(POOL, lambda e: e.tensor_tensor(out=sq[:], in0=Ya[:, cc, :], in1=Ya[:, cc, :], op=ALU.mult),
                          reads=[YA[cc]], writes=[SQ])
                def s_on(cc=cc):
                    kb.op(PE, lambda e: e.matmul(B[TB_][:], lhsT=ones_b[:], rhs=sq[:], start=(cc == 0), stop=(cc == 3)),
                          reads=[ONES, SQ], writes=[BK[TB_]])
                steps += [s_sq, s_on]
            def s_ln():
                kb.op(ACT, lambda e: e.activation(out=rbc[:], in_=B[TB_][:], func=AF.Ln, bias=epsc[:, 0:1], scale=1.0 / 512),
                      reads=[BK[TB_], EPSC], writes=[RBC])
            def s_exp():
                kb.op(ACT, lambda e: e.activation(out=rbc[:], in_=rbc[:], func=AF.Exp, scale=-0.5), reads=[RBC], writes=[RBC])
            def s_norm():
                for cc in range(4):
                    kb.op(DVE, lambda e, cc=cc: e.tensor_tensor(out=Ya[:, cc, :], in0=Ya[:, cc, :], in1=rbc[:], op=ALU.mult),
                          reads=[YA[cc], RBC], writes=[YA[cc]])
                if use_ssm:
                    kb.dma(SP, Gn[:], gnsc[:, tok0:tok0 + T1].rearrange("(c p) t -> p c t", p=128), reads=[GNSC], writes=[GN])
            steps += [s_ln, s_exp, s_norm]
            for su in range(NSUB):
                xb_ = su % 2
                def s_xl(su=su, xb_=xb_):
                    kb.dma(SP, xs_[xb_][:], x[tok0 + su * 128:tok0 + (su + 1) * 128, :], writes=[XS[xb_]])
                steps.append(s_xl)
                for hf in range(2):
                    ob_ = FB[hf]
                    def s_mm(su=su, hf=hf, xb_=xb_, ob_=ob_):
                        nmm = 8 if use_ssm else 4
                        n = 0
                        if use_ssm:
                            for cc in range(4):
                                kb.op(PE, lambda e, cc=cc, n=n: e.matmul(
                                    B[ob_][:], lhsT=Gn[:, cc, su * 128:(su + 1) * 128], rhs=wout[:, cc, hf * 512:(hf + 1) * 512],
                                    start=(n == 0), stop=False), reads=[GN, WOUT], writes=[BK[ob_]])
                                n += 1
                        for cc in range(4):
                            kb.op(PE, lambda e, cc=cc, n=n: e.matmul(
                                B[ob_][:], lhsT=Ya[:, cc, su * 128:(su + 1) * 128], rhs=wout[:, 4 + cc, hf * 512:(hf + 1) * 512],
                                start=(n == 0), stop=(n == nmm - 1)), reads=[YA[cc], WOUT], writes=[BK[ob_]])
                            n += 1
                        kb.op(DVE, lambda e: e.tensor_tensor(out=xs_[xb_][:, hf * 512:(hf + 1) * 512],
                                                             in0=xs_[xb_][:, hf * 512:(hf + 1) * 512], in1=B[ob_][:], op=ALU.add),
                              reads=[XS[xb_], BK[ob_]], writes=[XS[xb_]])
                    steps += [s_mm]
                def s_st(su=su, xb_=xb_):
                    tok = kb.dma(SP, out[tok0 + su * 128:tok0 + (su + 1) * 128, :], xs_[xb_][:], reads=[XS[xb_]], writes=[OUTB])
                    if "p2" not in phases:
                        kb.out_tokens.append(tok)
                steps.append(s_st)
            return steps

        for b in range(nseq):
            load_featmajor(sh1[:], modsc[b:b + 1, 0:D], SH1, extra_reads=[MODSC])
            load_featmajor(sc1[:], modsc[b:b + 1, D:2 * D], SC1, extra_reads=[MODSC])
            kb.op(DVE, lambda e: e.scalar_tensor_tensor(out=G1[:], in0=sc1[:], scalar=1.0, in1=n1g[:],
                                                        op0=ALU.add, op1=ALU.mult), reads=[SC1, N1G], writes=[G1B])
            kb.dma(SP, gate1, bcast_rows(modsc[b:b + 1, 2 * D:3 * D], 128), reads=[MODSC], writes=[GATE1])
            for kc in range(8):
                kb.dma(SP, wstg, w_out[kc * 128:(kc + 1) * 128, :], writes=[WSTG])
                kb.op(DVE, lambda e, kc=kc: e.scalar_tensor_tensor(out=wout[:, kc, :], in0=wstg, scalar=og[:, kc:kc + 1],
                                                                   in1=gate1, op0=ALU.mult, op1=ALU.mult),
                      reads=[WSTG, OG, GATE1], writes=[WOUT])
            for st_ in front_steps(b, 0):
                st_()
            for i in range(nt1):
                bg["urgent"] = tail_steps(b, i - 1) if i > 0 else []
                bg["ublocks"] = (i + 1) * NSUB
                bg["steps"] = front_steps(b, i + 1) if i + 1 < nt1 else []
                bg["blocks_left"] = NH * (i + 1) * NSUB
                for h in range(NH):
                    attention_head(i, h)
                    assert not bg["urgent"]
                while bg["steps"]:
                    bg["steps"].pop(0)()
                if pending:
                    pending.pop()()
            for st_ in tail_steps(b, nt1 - 1):
                st_()
        kb.pop()

    if "p2" in phases:
        kb.push()
        TT = 256
        nt2 = ntok // TT
        src_x1 = out if ("p1" in phases) else x
        wff1 = kb.sb([128, 8, DFF], BF16, "wff1")
        wff2 = kb.sb([128, 32, D], BF16, "wff2")
        WFF1 = [Buf(f"wff1_{k}") for k in range(8)]
        WFF2 = [Buf(f"wff2_{k}") for k in range(8)]
        for cb_ in range(8):
            kb.dma(POOL, wff1[:, :, cb_ * 512:(cb_ + 1) * 512],
                   w_ff1[:, cb_ * 512:(cb_ + 1) * 512].rearrange("(k p) n -> p k n", p=128), writes=[WFF1[cb_]])
        for k in range(8):
            kb.dma(POOL, wff2[:, 4 * k:4 * k + 4, :],
                   w_ff2[k * 512:(k + 1) * 512, :].rearrange("(k p) n -> p k n", p=128), writes=[WFF2[k]])
        xt = [kb.sb([128, 2, D], F32, f"p2_xt{i}") for i in range(2)]
        XT = [Buf("xt0"), Buf("xt1")]
        xn = kb.sb([128, 2, D], BF16, "p2_xn")
        XN = Buf("xn")
        junk = kb.sb([128, D], BF16, "p2_junk")
        JUNK = Buf("junk")
        h2T = kb.sb([128, 8, TT], BF16, "p2_h2T")
        H2T = Buf("h2T")
        hid = kb.sb([128, 32, TT], BF16, "p2_hid")
        HID = [Buf(f"hid{j}") for j in range(32)]
        tmp = kb.sb([128, 512], F32, "p2_tmp")
        TMP = Buf("tmp")
        stat = [kb.sb([128, 8], F32, f"p2_stat{i}") for i in range(2)]
        SS = [Buf(), Buf()]; RS = [Buf(), Buf()]; SS2 = [Buf(), Buf()]; RS2 = [Buf(), Buf()]
        g2n = kb.sb([128, 8], F32, "p2_g2n")
        G2N = Buf()
        load_featmajor(g2n[:], norm2_g[0:1, :], G2N)
        fng = kb.sb([128, D], F32, "p2_fng")
        FNG = Buf()
        kb.dma(SP, fng[:], bcast_rows(final_norm_g[0:1, :], 128), writes=[FNG])
        sc2 = kb.sb([128, 8], F32, "p2_sc2")
        SC2 = Buf()
        sh2 = [kb.sb([128, 8], F32, f"p2_sh2_{i}") for i in range(2)]
        G2 = [kb.sb([128, 8], F32, f"p2_G2_{i}") for i in range(2)]
        gate2 = [kb.sb([128, D], F32, f"p2_gate2_{i}") for i in range(2)]
        FG = [kb.sb([128, D], F32, f"p2_FG_{i}") for i in range(2)]
        fsh = [kb.sb([128, D], F32, f"p2_fsh_{i}") for i in range(2)]
        SH2 = [Buf(), Buf()]; G2B = [Buf(), Buf()]; GATE2 = [Buf(), Buf()]; FGB = [Buf(), Buf()]; FSH = [Buf(), Buf()]
        tp = [kb.ps([128, 1024], BF16, f"p2_tp{i}") for i in range(2)]
        TP = [Buf(), Buf()]
        ps_h = [kb.ps([128, 512], F32, f"p2_psh{i}") for i in range(3)]
        PSH = [Buf() for _ in range(3)]
        ps_o = [kb.ps([128, 512], F32, f"p2_pso{i}") for i in range(2)]
        PSO = [Buf(), Buf()]

        loaded_seq = set()

        def seq_consts(b):
            if b in loaded_seq:
                return
            loaded_seq.add(b)
            bi = b % 2
            load_featmajor(sh2[bi][:], modsc[b:b + 1, 3 * D:4 * D], SH2[bi], extra_reads=[MODSC])
            load_featmajor(sc2[:], modsc[b:b + 1, 4 * D:5 * D], SC2, extra_reads=[MODSC])
            kb.op(DVE, lambda e: e.scalar_tensor_tensor(out=G2[bi][:], in0=sc2[:], scalar=1.0, in1=g2n[:],
                                                        op0=ALU.add, op1=ALU.mult), reads=[SC2, G2N], writes=[G2B[bi]])
            kb.dma(SP, gate2[bi][:], bcast_rows(modsc[b:b + 1, 5 * D:6 * D], 128), reads=[MODSC], writes=[GATE2[bi]])
            kb.dma(SP, FG[bi][:], bcast_rows(modsc[b:b + 1, 7 * D:8 * D], 128), reads=[MODSC], writes=[FGB[bi]])
            kb.dma(SP, fsh[bi][:], bcast_rows(modsc[b:b + 1, 6 * D:7 * D], 128), reads=[MODSC], writes=[FSH[bi]])
            kb.op(DVE, lambda e: e.scalar_tensor_tensor(out=FG[bi][:], in0=FG[bi][:], scalar=1.0, in1=fng[:],
                                                        op0=ALU.add, op1=ALU.mult), reads=[FGB[bi], FNG], writes=[FGB[bi]])

        def prep_a(t):
            b = (t * TT) // seq
            xi = t % 2
            st = stat[xi]
            seq_consts(b)
            kb.dma(SP, xt[xi][:], src_x1[t * TT:(t + 1) * TT, :].rearrange("(s p) n -> p s n", p=128),
                   reads=[OUTB], writes=[XT[xi]])
            kb.op(DVE, lambda e: e.memset(st[:, 0:2], 0.0), writes=[SS[xi]])
            for s_ in range(2):
                kb.op(ACT, lambda e, s_=s_: e.activation(out=junk[:], in_=xt[xi][:, s_, :], func=AF.Square,
                                                         accum_out=st[:, s_:s_ + 1]),
                      reads=[XT[xi]], writes=[JUNK, SS[xi]])
            rsqrt_cols(st[:, 2:4], st[:, 0:2], 1.0 / D, SS[xi], RS[xi])
            for s_ in range(2):
                kb.op(DVE, lambda e, s_=s_: e.tensor_scalar(out=xn[:, s_, :], in0=xt[xi][:, s_, :],
                                                            scalar1=st[:, 2 + s_:3 + s_], scalar2=None, op0=ALU.mult),
                      reads=[XT[xi], RS[xi]], writes=[XN])

        def prep_b(t):
            bi = ((t * TT) // seq) % 2
            for k in range(8):
                pi = k % 2
                for s_ in range(2):
                    kb.op(PE, lambda e, s_=s_, k=k, pi=pi: e.transpose(out=tp[pi][:, s_ * 128:(s_ + 1) * 128],
                                                                       in_=xn[:, s_, k * 128:(k + 1) * 128],
                                                                       identity=ident_b[:]),
                          reads=[XN, IDB], writes=[TP[pi]])
                kb.op(ACT, lambda e, k=k, pi=pi: e.activation(out=h2T[:, k, :], in_=tp[pi][:, 0:TT], func=AF.Identity,
                                                              bias=sh2[bi][:, k:k + 1], scale=G2[bi][:, k:k + 1]),
                      reads=[TP[pi], SH2[bi], G2B[bi]], writes=[H2T])

        prep_a(0)
        prep_b(0)
        for t in range(nt2):
            b = (t * TT) // seq
            bi = b % 2
            xi = t % 2
            st = stat[xi]
            for jj in range(16):
                pj = jj % 3
                for j2 in range(2):
                    j = 2 * jj + j2
                    for k in range(8):
                        kb.op(PE, lambda e, j=j, j2=j2, k=k, pj=pj: e.matmul(
                            ps_h[pj][:, j2 * TT:(j2 + 1) * TT], lhsT=wff1[:, k, j * 128:(j + 1) * 128],
                            rhs=h2T[:, k, :], start=(k == 0), stop=(k == 7)),
                              reads=[H2T, WFF1[j // 4]], writes=[PSH[pj]])
                kb.op(ACT, lambda e, jj=jj, pj=pj: e.activation(out=hid[:, 2 * jj:2 * jj + 2, :], in_=ps_h[pj][:],
                                                                func=AF.Relu),
                      reads=[PSH[pj]], writes=[HID[2 * jj], HID[2 * jj + 1]])
                kb.op(DVE, lambda e, jj=jj: e.tensor_tensor(out=hid[:, 2 * jj:2 * jj + 2, :], in0=hid[:, 2 * jj:2 * jj + 2, :],
                                                            in1=hid[:, 2 * jj:2 * jj + 2, :], op=ALU.mult),
                      reads=[HID[2 * jj], HID[2 * jj + 1]], writes=[HID[2 * jj], HID[2 * jj + 1]])
                if jj == 3 and t + 1 < nt2:
                    prep_a(t + 1)
            if t + 1 < nt2:
                prep_b(t + 1)
            for s_ in range(2):
                for hf in range(2):
                    po = (s_ * 2 + hf) % 2
                    for k in range(32):
                        kb.op(PE, lambda e, s_=s_, hf=hf, k=k, po=po: e.matmul(
                            ps_o[po][:], lhsT=hid[:, k, s_ * 128:(s_ + 1) * 128], rhs=wff2[:, k, hf * 512:(hf + 1) * 512],
                            start=(k == 0), stop=(k == 31)),
                              reads=[HID[k], WFF2[k // 4]], writes=[PSO[po]])
                    kb.op(DVE, lambda e, hf=hf, po=po: e.tensor_tensor(out=tmp[:], in0=ps_o[po][:],
                                                                       in1=gate2[bi][:, hf * 512:(hf + 1) * 512], op=ALU.mult),
                          reads=[PSO[po], GATE2[bi]], writes=[TMP])
                    kb.op(DVE, lambda e, s_=s_, hf=hf: e.tensor_tensor(out=xt[xi][:, s_, hf * 512:(hf + 1) * 512],
                                                                       in0=xt[xi][:, s_, hf * 512:(hf + 1) * 512],
                                                                       in1=tmp[:], op=ALU.add),
                          reads=[TMP, XT[xi]], writes=[XT[xi]])
            kb.op(DVE, lambda e: e.memset(st[:, 4:6], 0.0), writes=[SS2[xi]])
            for s_ in range(2):
                kb.op(ACT, lambda e, s_=s_: e.activation(out=junk[:], in_=xt[xi][:, s_, :], func=AF.Square,
                                                         accum_out=st[:, 4 + s_:5 + s_]),
                      reads=[XT[xi]], writes=[JUNK, SS2[xi]])
            rsqrt_cols(st[:, 6:8], st[:, 4:6], 1.0 / D, SS2[xi], RS2[xi])
            for s_ in range(2):
                kb.op(DVE, lambda e, s_=s_: e.scalar_tensor_tensor(out=xt[xi][:, s_, :], in0=xt[xi][:, s_, :],
                                                                   scalar=st[:, 6 + s_:7 + s_], in1=FG[bi][:],
                                                                   op0=ALU.mult, op1=ALU.mult),
                      reads=[XT[xi], RS2[xi], FGB[bi]], writes=[XT[xi]])
                kb.op(DVE, lambda e, s_=s_: e.tensor_tensor(out=xt[xi][:, s_, :], in0=xt[xi][:, s_, :], in1=fsh[bi][:],
                                                            op=ALU.add),
                      reads=[XT[xi], FSH[bi]], writes=[XT[xi]])
            tok = kb.dma(SP, out[t * TT:(t + 1) * TT, :].rearrange("(s p) n -> p s n", p=128), xt[xi][:],
                         reads=[XT[xi]], writes=[OUTB])
            kb.out_tokens.append(tok)
        kb.pop()

    for tok in kb.out_tokens:
        kb._wait(SP, tok)
    return nc, kb


_NC_CACHE = {}


def _consts():
    inv_freq = 10000.0 ** (-np.arange(0, QK_ROPE, 2, dtype=np.float64) / QK_ROPE)
    invf = np.array([inv_freq[(r % 32) % 16] / (2.0 * np.pi) for r in range(128)], dtype=np.float32).reshape(128, 1)
    tri = np.triu(np.ones((128, 128), dtype=np.float32))
    kr = np.array(list(range(7, -1, -1)) + list(range(0, -8, -1)) + list(range(0, 8)) + list(range(1, 9)), dtype=np.float32)
    kr32 = np.tile(kr[None, :], (128, 1))
    cramp = np.tile(np.arange(1, 65, dtype=np.float32)[None, :], (128, 1))
    tau = np.arange(128) // 16
    mask8 = (tau[None, :] >= tau[:, None]).astype(np.float32)
    sgn = np.concatenate([np.ones(64), -np.ones(64)]).astype(np.float32).reshape(128, 1)
    return {"ident": np.eye(128, dtype=np.float32), "invf": invf, "tri": tri, "kr32": kr32, "cramp": cramp,
            "mask8": mask8, "sgn": sgn}


def kernel(**inputs):
    n = 8
    if "full" not in _NC_CACHE:
        _NC_CACHE["full"] = build_program()
    nc = _NC_CACHE["full"]
    in_maps = []
    for i in range(n):
        m = _core_inputs(inputs, i, NSEQ, SEQ)
        in_maps.append(m)
    res = run_bass_kernel_spmd(nc, in_maps, core_ids=list(range(n)))
    outs = [np.asarray(r["out"]).reshape(NSEQ, SEQ, D) for r in res.results]
    return np.concatenate(outs, axis=0).astype(np.float32)


def _core_inputs(inputs, i, nseq, seq):
    g = lambda k: np.ascontiguousarray(np.asarray(inputs[k]))
    sl = slice(i * nseq, (i + 1) * nseq)
    m = {
        "x": np.ascontiguousarray(g("x")[sl, :seq].reshape(nseq * seq, D)),
        "c": g("c")[sl],
        "positions": np.ascontiguousarray(g("positions")[sl, :seq]).astype(np.int32),
        "ada_w": g("ada_w")[0], "ada_b": g("ada_b").reshape(1, -1), "norm1_g": g("norm1_g").reshape(1, -1),
        "w_in": g("w_in")[0],
        "ssm_lambda_re": g("ssm_lambda_re")[0], "ssm_lambda_im": g("ssm_lambda_im")[0],
        "ssm_b_re": g("ssm_b_re")[0], "ssm_b_im": g("ssm_b_im")[0],
        "ssm_c_re": g("ssm_c_re")[0], "ssm_c_im": g("ssm_c_im")[0],
        "ssm_d": g("ssm_d")[0], "ssm_log_dt": g("ssm_log_dt").reshape(1, -1),
        "w_glu": g("w_glu")[0], "q_norm_g": g("q_norm_g").reshape(1, -1), "w_uq": g("w_uq")[0],
        "kv_norm_g": g("kv_norm_g").reshape(1, -1), "w_ukv": g("w_ukv")[0],
        "ssm_out_g": g("ssm_out_g").reshape(1, -1), "attn_out_g": g("attn_out_g").reshape(1, -1),
        "w_out": g("w_out")[0], "norm2_g": g("norm2_g").reshape(1, -1),
        "w_ff1": g("w_ff1")[0], "w_ff2": g("w_ff2")[0],
        "final_ada_w": g("final_ada_w"), "final_ada_b": g("final_ada_b").reshape(1, -1),
        "final_norm_g": g("final_norm_g").reshape(1, -1),
    }
    m.update(_consts())
    return m
```
